# Optimizing a Trainium2 kernel written in Bass

```python
import math
import jax, jax.numpy as jnp
from jax import lax
import numpy as np

D_MODEL = 1024
BATCH = 8
SEQ = 2048
DEPTH = 4
DEC_BATCH = 128
DEC_SEQ = 8
PAST_LEN = 16384
PAGE_SIZE = 128

N_MIXERS = 2
N_HGRN = (DEPTH + 1) // 2
N_S5 = DEPTH // 2
HG_HEAD_DIM = 128
HG_HEADS = D_MODEL // HG_HEAD_DIM
HG_QK_SCALE = HG_HEAD_DIM ** -0.5
CHUNK = 16
S5_GROUP = 16
S5_GROUPS = D_MODEL // S5_GROUP
S5_STATE = 64
D_FF = -(-8 * D_MODEL // (3 * 256)) * 256
PLE_DIM = 256
ALPHA = (2 * DEPTH) ** 0.25
BETA = (8 * DEPTH) ** -0.25
LN_EPS = 1e-5
RMS_EPS = 1e-6

kernel_name = 'hgrn2_s5_hybrid_decoder_step'


def layer_norm(x, w, b):
    xf = x.astype(jnp.float32)
    mu = jnp.mean(xf, axis=-1, keepdims=True)
    var = jnp.mean(jnp.square(xf - mu), axis=-1, keepdims=True)
    y = (xf - mu) * lax.rsqrt(var + LN_EPS) * w.astype(jnp.float32) + b.astype(jnp.float32)
    return y.astype(x.dtype)


def rms_norm(x, w):
    xf = x.astype(jnp.float32)
    y = xf * lax.rsqrt(jnp.mean(jnp.square(xf), axis=-1, keepdims=True) + RMS_EPS) * w.astype(jnp.float32)
    return y.astype(x.dtype)


def hgrn2_chunk_scan(q, k, v, logf, s0):
    bsz, t = q.shape[0], q.shape[1]
    pad = (-t) % CHUNK
    pw = ((0, 0), (0, pad), (0, 0), (0, 0))
    q, k, v, logf = (jnp.pad(a, pw) for a in (q, k, v, logf))
    nc = (t + pad) // CHUNK

    def to_chunks(a):
        return a.reshape(bsz, nc, CHUNK, a.shape[2], a.shape[3]).swapaxes(0, 1)

    causal = jnp.tril(jnp.ones((CHUNK, CHUNK), dtype=bool))

    def step(s, xs):
        qc, kc, vc, lc = xs
        b = jnp.cumsum(lc, axis=1)
        bm = b[:, CHUNK // 2 - 1][:, None]
        bl = b[:, -1]
        qh = qc * jnp.exp(b - bm)
        kh = kc * jnp.exp(bm - b)
        att = jnp.where(causal, jnp.einsum('bthk,bshk->bhts', qh, kh), 0.0)
        o = (jnp.einsum('bhts,bshv->bthv', att, vc)
             + jnp.einsum('bthk,bhkv->bthv', qc * jnp.exp(b), s))
        s_new = (jnp.exp(bl)[..., None] * s
                 + jnp.einsum('bshk,bshv->bhkv', kc * jnp.exp(bl[:, None] - b), vc))
        return s_new, o

    s_fin, o = lax.scan(step, s0, (to_chunks(q), to_chunks(k), to_chunks(v), to_chunks(logf)))
    o = o.swapaxes(0, 1).reshape(bsz, nc * CHUNK, o.shape[3], o.shape[4])[:, :t]
    return o, s_fin


def hgrn2_mixer(x, s0, w_in, lb, gnorm_w, w_out):
    bsz, t, _ = x.shape
    q, fz, v, g = jnp.split(x @ w_in, 4, axis=-1)

    def heads(a):
        return a.reshape(bsz, t, HG_HEADS, HG_HEAD_DIM)

    lbf = lb.astype(jnp.float32)
    f = lbf + (1.0 - lbf) * jax.nn.sigmoid(fz.astype(jnp.float32))
    qh = heads(jax.nn.silu(q.astype(jnp.float32)) * HG_QK_SCALE)
    kh = heads(1.0 - f)
    o, s = hgrn2_chunk_scan(qh, kh, heads(v.astype(jnp.float32)), heads(jnp.log(f)),
                            s0.astype(jnp.float32))
    o = rms_norm(o, gnorm_w) * jax.nn.silu(heads(g.astype(jnp.float32)))
    o = o.reshape(bsz, t, D_MODEL).astype(x.dtype)
    return o @ w_out, s


def s5_combine(e1, e2):
    a1, b1 = e1
    a2, b2 = e2
    return a1 * a2, a2 * b1 + b2


def s5_mixer(x, h0_re, h0_im, a_re, a_im, b_re, b_im, c_re, c_im, d_skip, log_step, w_glu):
    bsz, t, _ = x.shape
    f32 = jnp.float32
    u = x.astype(f32).reshape(bsz, t, S5_GROUPS, S5_GROUP)
    lam = lax.complex(a_re.astype(f32), a_im.astype(f32))
    delta = jnp.exp(log_step.astype(f32))[:, None]
    lam_bar = jnp.exp(lam * delta)
    b_bar = ((lam_bar - 1.0) / lam)[..., None] * lax.complex(b_re.astype(f32), b_im.astype(f32))
    c_mat = lax.complex(c_re.astype(f32), c_im.astype(f32))
    bu = jnp.einsum('btgc,gpc->btgp', u.astype(jnp.complex64), b_bar)
    h0 = lax.complex(h0_re.astype(f32), h0_im.astype(f32))
    bu = bu.at[:, 0].add(lam_bar * h0)
    a_el = jnp.broadcast_to(lam_bar, (1, t, S5_GROUPS, S5_STATE))
    _, h = lax.associative_scan(s5_combine, (a_el, bu), axis=1)
    y = jnp.einsum('btgp,gcp->btgc', h, c_mat).real + d_skip.astype(f32).reshape(S5_GROUPS, S5_GROUP) * u
    y = jax.nn.gelu(y.reshape(bsz, t, D_MODEL)).astype(x.dtype)
    val, gate = jnp.split(y @ w_glu, 2, axis=-1)
    h_last = h[:, -1]
    return val * jax.nn.sigmoid(gate), h_last.real, h_last.imag


def swiglu(x, w_gate_up, w_down):
    g, u = jnp.split(x @ w_gate_up, 2, axis=-1)
    return (jax.nn.silu(g) * u) @ w_down


def per_layer_embed(x, p_i, w_proj, w_gate, norm_w):
    e = (p_i @ w_proj).astype(jnp.float32) * jax.nn.sigmoid((x @ w_gate).astype(jnp.float32))
    return x + rms_norm(e, norm_w).astype(x.dtype)


def trunk(x, p, st_hg, st_re, st_im, lb,
          hg_w_in, hg_gnorm_w, hg_w_out,
          s5_a_re, s5_a_im, s5_b_re, s5_b_im, s5_c_re, s5_c_im, s5_d, s5_log_step, s5_w_glu,
          ln_mix_w, ln_mix_b, ffn_w_gate_up, ffn_w_down, ln_ffn_w, ln_ffn_b,
          ple_w_proj, ple_w_gate, ple_norm_w):
    new_hg, new_re, new_im = [], [], []
    for i in range(DEPTH):
        j = i // N_MIXERS
        if i % N_MIXERS == 0:
            mix, s = hgrn2_mixer(x, st_hg[j], hg_w_in[j], lb[j], hg_gnorm_w[j], hg_w_out[j])
            new_hg.append(s)
        else:
            mix, hr, hi = s5_mixer(x, st_re[j], st_im[j], s5_a_re[j], s5_a_im[j], s5_b_re[j], s5_b_im[j],
                                   s5_c_re[j], s5_c_im[j], s5_d[j], s5_log_step[j], s5_w_glu[j])
            new_re.append(hr)
            new_im.append(hi)
        x = layer_norm(ALPHA * x + mix, ln_mix_w[i], ln_mix_b[i])
        x = layer_norm(ALPHA * x + swiglu(x, ffn_w_gate_up[i], ffn_w_down[i]), ln_ffn_w[i], ln_ffn_b[i])
        x = per_layer_embed(x, p[i], ple_w_proj[i], ple_w_gate[i], ple_norm_w[i])
    return x, jnp.stack(new_hg), jnp.stack(new_re), jnp.stack(new_im)


def setup_inputs(seed: int = 0) -> dict:
    key = jax.random.key(seed)
    ks = jax.random.split(key, 32)
    f32 = jnp.float32

    def nrm(k, shape, s):
        return jax.random.normal(k, shape, f32) * s

    s5_a_im = (jnp.pi * jnp.arange(S5_STATE, dtype=f32))[None, None, :] + nrm(ks[12], (N_S5, S5_GROUPS, S5_STATE), 0.01)
    return {
        'x_prompt': nrm(ks[0], (BATCH, SEQ, D_MODEL), 1.0),
        'x_sample': nrm(ks[1], (DEC_BATCH, DEC_SEQ, D_MODEL), 1.0),
        'state_hgrn': nrm(ks[2], (N_HGRN, DEC_BATCH, HG_HEADS, HG_HEAD_DIM, HG_HEAD_DIM), 0.5),
        'state_s5_re': nrm(ks[3], (N_S5, DEC_BATCH, S5_GROUPS, S5_STATE), 0.1),
        'state_s5_im': nrm(ks[4], (N_S5, DEC_BATCH, S5_GROUPS, S5_STATE), 0.1),
        'p_prompt': nrm(ks[5], (DEPTH, BATCH, SEQ, PLE_DIM), 1.0),
        'p_sample': nrm(ks[6], (DEPTH, DEC_BATCH, DEC_SEQ, PLE_DIM), 1.0),
        'hg_w_in': nrm(ks[7], (N_HGRN, D_MODEL, 4 * D_MODEL), D_MODEL ** -0.5),
        'hg_lower_bounds': nrm(ks[8], (N_HGRN, D_MODEL), 0.1),
        'hg_gnorm_w': 1.0 + nrm(ks[9], (N_HGRN, HG_HEAD_DIM), 0.01),
        'hg_w_out': nrm(ks[10], (N_HGRN, D_MODEL, D_MODEL), BETA * D_MODEL ** -0.5),
        's5_a_re': -0.5 + nrm(ks[11], (N_S5, S5_GROUPS, S5_STATE), 0.01),
        's5_a_im': s5_a_im,
        's5_b_re': nrm(ks[13], (N_S5, S5_GROUPS, S5_STATE, S5_GROUP), (2 * S5_GROUP) ** -0.5),
        's5_b_im': nrm(ks[14], (N_S5, S5_GROUPS, S5_STATE, S5_GROUP), (2 * S5_GROUP) ** -0.5),
        's5_c_re': nrm(ks[15], (N_S5, S5_GROUPS, S5_GROUP, S5_STATE), (2 * S5_STATE) ** -0.5),
        's5_c_im': nrm(ks[16], (N_S5, S5_GROUPS, S5_GROUP, S5_STATE), (2 * S5_STATE) ** -0.5),
        's5_d': nrm(ks[17], (N_S5, D_MODEL), 1.0),
        's5_log_step': jax.random.uniform(ks[18], (N_S5, S5_GROUPS), f32, math.log(1e-3), math.log(1e-1)),
        's5_w_glu': nrm(ks[19], (N_S5, D_MODEL, 2 * D_MODEL), BETA * D_MODEL ** -0.5),
        'ln_mix_w': 1.0 + nrm(ks[20], (DEPTH, D_MODEL), 0.01),
        'ln_mix_b': nrm(ks[21], (DEPTH, D_MODEL), 0.01),
        'ffn_w_gate_up': nrm(ks[22], (DEPTH, D_MODEL, 2 * D_FF), D_MODEL ** -0.5),
        'ffn_w_down': nrm(ks[23], (DEPTH, D_FF, D_MODEL), BETA * D_FF ** -0.5),
        'ln_ffn_w': 1.0 + nrm(ks[24], (DEPTH, D_MODEL), 0.01),
        'ln_ffn_b': nrm(ks[25], (DEPTH, D_MODEL), 0.01),
        'ple_w_proj': nrm(ks[26], (DEPTH, PLE_DIM, D_MODEL), PLE_DIM ** -0.5),
        'ple_w_gate': nrm(ks[27], (DEPTH, D_MODEL, D_MODEL), D_MODEL ** -0.5),
        'ple_norm_w': 1.0 + nrm(ks[28], (DEPTH, D_MODEL), 0.01),
    }


def reference(x_prompt, x_sample, state_hgrn, state_s5_re, state_s5_im, p_prompt, p_sample,
              hg_w_in, hg_lower_bounds, hg_gnorm_w, hg_w_out,
              s5_a_re, s5_a_im, s5_b_re, s5_b_im, s5_c_re, s5_c_im, s5_d, s5_log_step, s5_w_glu,
              ln_mix_w, ln_mix_b, ffn_w_gate_up, ffn_w_down, ln_ffn_w, ln_ffn_b,
              ple_w_proj, ple_w_gate, ple_norm_w):
    lb_soft = jax.nn.softmax(hg_lower_bounds.astype(jnp.float32), axis=0)
    lb = jnp.cumsum(lb_soft, axis=0) - lb_soft[0]
    weights = (hg_w_in, hg_gnorm_w, hg_w_out,
               s5_a_re, s5_a_im, s5_b_re, s5_b_im, s5_c_re, s5_c_im, s5_d, s5_log_step, s5_w_glu,
               ln_mix_w, ln_mix_b, ffn_w_gate_up, ffn_w_down, ln_ffn_w, ln_ffn_b,
               ple_w_proj, ple_w_gate, ple_norm_w)
    zero_hg = jnp.zeros((N_HGRN, BATCH, HG_HEADS, HG_HEAD_DIM, HG_HEAD_DIM), jnp.float32)
    zero_s5 = jnp.zeros((N_S5, BATCH, S5_GROUPS, S5_STATE), jnp.float32)
    y_prompt, hg_p, re_p, im_p = trunk(x_prompt, p_prompt, zero_hg, zero_s5, zero_s5, lb, *weights)
    y_sample, hg_s, re_s, im_s = trunk(x_sample, p_sample, state_hgrn, state_s5_re, state_s5_im, lb, *weights)
    return (y_prompt, y_sample, hg_p, re_p, im_p, hg_s, re_s, im_s)
```

```python
import math
from contextlib import ExitStack

import numpy as np
import concourse.bass as bass
import concourse.mybir as mybir
from concourse.ap import AP
from concourse.bass_utils import run_bass_kernel_spmd

F32 = mybir.dt.float32
BF16 = mybir.dt.bfloat16
ALU = mybir.AluOpType
AF = mybir.ActivationFunctionType

D = 1024
KC = 8
NT = 2176
NPR = 2048
NSM = 128
NB = 272
DEPTH = 4
STS = [(0, 512), (512, 512), (1024, 512), (1536, 512), (2048, 128)]
NST = len(STS)
FPASS = [(0, 8), (8, 15), (15, 22)]
ALPHA = float((2 * DEPTH) ** 0.25)
LN_EPS = 1e-5
RMS_EPS = 1e-6
QK = float(128 ** -0.5)
TWO_PI = 2.0 * math.pi
MAGIC = 12582912.0

C_ID, C_CAUS, C_BD, C_SEG, C_TM, C_RM, C_R8 = 0, 128, 256, 384, 400, 528, 530
NCP = 658
C_R512, C_TAU = 658, 1170
NCONST = 1682
V_LAYER = 40
V_LB = 160
V_GN = 176
NVEC = 178


class Buf:
    __slots__ = ("name", "w", "r")

    def __init__(self, name):
        self.name = name
        self.w = None
        self.r = []


class Chan:
    _uid = [0]

    def __init__(self, S, name):
        Chan._uid[0] += 1
        self.sem = S.es.enter_context(S.nc.semaphore("c_%s_%d" % (name, Chan._uid[0])))
        self.cnt = 0


class Tile:
    _uid = [0]

    def __init__(self, S, es, name, shape, dtype, nb=1, psum=False):
        Tile._uid[0] += 1
        name = "%s_%d" % (name, Tile._uid[0])
        if psum:
            self.h = es.enter_context(S.nc.psum_tensor(name, shape, dtype))
        else:
            self.h = es.enter_context(S.nc.sbuf_tensor(name, shape, dtype))
        self.b = [Buf(name + str(i)) for i in range(nb)]
        self.shape = shape
        a = self.h[:]
        self.tensor = a.tensor
        self.pstep = a.ap[0][0]

    def __getitem__(self, k):
        return self.h[k]

    def v(self, dims, off=0, p0=0, np_=128):
        return AP(self.tensor, p0 * self.pstep + off, [[self.pstep, np_]] + [list(d) for d in dims])


class HView:
    def __init__(self, tile, t0, bufs):
        self.t, self.t0, self.b = tile, t0, bufs

    def __getitem__(self, k):
        p, r, c = k
        return self.t[p, r, self.t0 + c.start:self.t0 + c.stop]


class Sched:
    def __init__(self, nc, es):
        self.nc = nc
        self.es = es
        self.eng = {"pe": nc.tensor, "act": nc.scalar, "dve": nc.vector, "pool": nc.gpsimd, "sp": nc.sync}
        self.esem = {}
        self.cnt = {}
        for k in self.eng:
            self.esem[k] = es.enter_context(nc.semaphore("s_" + k))
            self.cnt[k] = 0
        self.seen = {k: {} for k in self.eng}
        self.pending = {}
        self.banks = []
        self.bank_i = 0
        self.nop = 0

    def _wait(self, e, ev):
        if ev is None:
            return
        sem, val = ev
        d = self.seen[e]
        key = id(sem)
        if d.get(key, 0) >= val:
            return
        d[key] = val
        self.eng[e].wait_ge(sem, val)

    def _deps(self, e, reads, writes, same_ok=False):
        own = self.esem[e] if same_ok else None
        for b in reads:
            ev = b.w
            if ev is not None and ev[0] is not own:
                self._wait(e, ev)
        for b in writes:
            ev = b.w
            if ev is not None and ev[0] is not own:
                self._wait(e, ev)
            for ev in b.r:
                if ev[0] is not own:
                    self._wait(e, ev)

    def _record(self, ev, reads, writes):
        for b in reads:
            b.r.append(ev)
            if len(b.r) > 24:
                d = {}
                for s, v in b.r:
                    if d.get(id(s), (None, 0))[1] < v:
                        d[id(s)] = (s, v)
                b.r = list(d.values())
        for b in writes:
            b.w = ev
            b.r = []

    rec = None

    def op(self, e, fn, reads=(), writes=()):
        if self.rec is not None:
            self.rec.append(("op", e, fn, list(reads), list(writes)))
            return None
        self._deps(e, reads, writes, same_ok=(e == "pe"))
        ins = fn(self.eng[e])
        self.cnt[e] += 1
        ins.then_inc(self.esem[e], 1)
        ev = (self.esem[e], self.cnt[e])
        self._record(ev, reads, writes)
        self.nop += 1
        return ev

    def dma(self, q, out, in_, chan, reads=(), writes=()):
        if self.rec is not None:
            self.rec.append(("dma", q, out, in_, chan, list(reads), list(writes)))
            return None
        self._deps(q, reads, writes)
        if chan.cnt > 0:
            self._wait(q, (chan.sem, chan.cnt))
        ins = self.eng[q].dma_start(out=out, in_=in_)
        chan.cnt += 16
        ins.then_inc(chan.sem, 16)
        ev = (chan.sem, chan.cnt)
        self.pending[id(chan.sem)] = ev
        self._record(ev, reads, writes)
        return ev

    def record(self, fn):
        assert self.rec is None
        self.rec = []
        fn()
        r, self.rec = self.rec, None
        return r

    def replay(self, *lists, weights=None):
        if weights is None:
            weights = [1.0] * len(lists)
        weights = [w for l, w in zip(lists, weights) if l]
        lists = [l for l in lists if l]
        pos = [0] * len(lists)
        tot = sum(len(l) for l in lists)
        for _ in range(tot):
            k = min((i for i in range(len(lists)) if pos[i] < len(lists[i])),
                    key=lambda i: weights[i] * (pos[i] + 0.5) / len(lists[i]))
            it = lists[k][pos[k]]
            pos[k] += 1
            if it[0] == "op":
                self.op(it[1], it[2], it[3], it[4])
            else:
                self.dma(it[1], it[2], it[3], it[4], it[5], it[6])

    def barrier(self):
        for e in self.eng:
            for e2 in self.eng:
                if e2 != e and self.cnt[e2] > 0:
                    self._wait(e, (self.esem[e2], self.cnt[e2]))
            for ev in self.pending.values():
                self._wait(e, ev)

    def bank(self):
        t = self.banks[self.bank_i % len(self.banks)]
        self.bank_i += 1
        return t


class WStream:
    NRING = 4
    LA = 2

    def __init__(self, S, es, plan, width=2048, nring=None, la=None):
        self.S = S
        self.plan = plan
        if nring is not None:
            self.NRING = nring
        if la is not None:
            self.LA = la
        self.ring = [Tile(S, es, "wring%d" % i, [128, width], BF16) for i in range(self.NRING)]
        self.chan = [Chan(S, "wr%d" % i) for i in range(self.NRING)]
        self.issued = 0
        self.slot = {}
        self.next = 0

    def _issue(self, c):
        S = self.S
        tag, src, E, dest = self.plan[c]
        r = self.ring[c % self.NRING]
        S.dma("pool", r[:, 0:E], src, self.chan[c % self.NRING], writes=[r.b[0]])
        self.slot[c] = r

    def get(self, tag):
        c = self.next
        assert self.plan[c][0] == tag, (self.plan[c][0], tag)
        self.next += 1
        while self.issued < len(self.plan) and self.issued <= c + self.LA:
            if self.issued > c and (self.issued - max(c - 1, 0)) + 1 > self.NRING:
                break
            self._issue(self.issued)
            self.issued += 1
        return self.slot.pop(c)


def build_program(n_layers=4):
    nc = bass.Bass("TRN2", target_bir_lowering=False)

    def din(name, shape):
        return nc.dram_tensor(name, shape, F32, kind="ExternalInput").ap()

    def dout(name, shape):
        return nc.dram_tensor(name, shape, F32, kind="ExternalOutput").ap()

    xT = din("xT", [D, NT])
    pT = din("pT", [4, 256, NT])
    hg0 = din("hg0", [2, 16, 8, 128, 128])
    s5h0 = din("s5h0", [2, 2, 128, 32 * 16])
    consts = din("consts", [128, NCONST])
    vecs = din("vecs", [128, NVEC])
    emat = din("emat", [4, 128, 2048])
    whg = din("whg", [2, 8, 2, 128, 2048])
    who = din("who", [2, 8, 128, 1024])
    wglu = din("wglu", [2, 8, 128, 2048])
    wgu = din("wgu", [4, 22, 128, 2048])
    wdn = din("wdn", [4, 3, 8, 128, 1024])
    wple = din("wple", [4, 8, 128, 1280])
    s5a = din("s5a", [2, 128, 96])
    s5b = din("s5b", [2, 2, 128, 512])
    s5c = din("s5c", [2, 2, 128, 512])
    s5d = din("s5d", [2, 128, 64])

    yT = dout("yT", [D, NT])
    hgp = dout("hgp", [2, 128, 8, 128])
    hgs = dout("hgs", [2, 16, 8, 128, 128])
    s5p = dout("s5p", [2, 2, 128, 32])
    s5s = dout("s5s", [2, 2, 128, 32 * 16])

    with ExitStack() as es:
        S = Sched(nc, es)
        S.banks = [Tile(S, es, "bank%d" % i, [128, 512], F32, psum=True) for i in range(8)]

        XF = Tile(S, es, "XF", [128, KC, NT], F32, nb=KC * NST)
        XB = Tile(S, es, "XB", [128, KC, NT], BF16, nb=KC * NST)
        CN = Tile(S, es, "CN", [128, NCP], F32)
        VEC = Tile(S, es, "VEC", [128, NVEC], F32)
        ONESB = Tile(S, es, "ONESB", [128, 128], BF16)
        ONESD = Tile(S, es, "ONESD", [128, 128], BF16)
        LBT = Tile(S, es, "LBT", [128, 2, 4, 8], F32)
        EPS = Tile(S, es, "EPS", [128, 2], F32)

        def xfb(st, kcs=range(KC)):
            return [XF.b[kc * NST + st] for kc in kcs]

        def xbb(st, kcs=range(KC)):
            return [XB.b[kc * NST + st] for kc in kcs]

        plan = []
        for l in range(n_layers):
            j = l // 2
            if l % 2 == 0:
                for st in range(NST):
                    for h in range(8):
                        plan.append((("hgA", l, st, h), whg[j, h, 0], 2048, None))
                        plan.append((("hgB", l, st, h), whg[j, h, 1], 2048, None))
                    for n in range(8):
                        plan.append((("who", l, st, n), who[j, n], 1024, None))
            else:
                for st in range(NST):
                    for n in range(8):
                        plan.append((("wglu", l, st, n), wglu[j, n], 2048, None))
            for ps, (c0, c1) in enumerate(FPASS):
                for c in range(c0, c1):
                    plan.append((("wgu", l, c), wgu[l, c], 2048, None))
                E = (c1 - c0) * 128
                if ps < len(FPASS) - 1:
                    for n in range(8):
                        plan.append((("wdn", l, ps, n), wdn[l, ps, n, :, 0:E], E, None))
                else:
                    for st in range(NST):
                        for n in range(8):
                            plan.append((("wdn", l, ps, st, n), wdn[l, ps, n, :, 0:E], E, None))
        W = WStream(S, es, plan)

        ldc = [Chan(S, "ld%d" % i) for i in range(4)]
        stc = [Chan(S, "st%d" % i) for i in range(4)]
        ctr = {"ld": 0, "st": 0}

        def ldchan():
            ctr["ld"] += 1
            return ldc[ctr["ld"] % 4]

        def stchan():
            ctr["st"] += 1
            return stc[ctr["st"] % 4]

        S.dma("sp", CN[:], consts[:, 0:NCP], ldchan(), writes=[CN.b[0]])
        S.dma("sp", VEC[:], vecs, ldchan(), writes=[VEC.b[0]])
        for st, (t0, n) in enumerate(STS):
            S.dma("sp", XF[:, :, t0:t0 + n], xT[:, t0:t0 + n].rearrange("(k p) t -> p k t", p=128),
                  ldchan(), writes=xfb(st))
        S.op("pool", lambda e: e.memset(ONESB[:], 1.0), writes=[ONESB.b[0]])
        S.op("pool", lambda e: e.memset(ONESD[:], 1.0 / D), writes=[ONESD.b[0]])
        S.op("pool", lambda e: e.memset(EPS[:, 0:1], LN_EPS), writes=[EPS.b[0]])
        S.op("pool", lambda e: e.memset(EPS[:, 1:2], RMS_EPS), writes=[EPS.b[0]])
        for bk in S.banks:
            S.op("dve", lambda e, bk=bk: e.memset(bk[:], 0.0), writes=[bk.b[0]])
        for st, (t0, n) in enumerate(STS):
            S.op("act", lambda e, t0=t0, n=n: e.activation(out=XB[:, :, t0:t0 + n], in_=XF[:, :, t0:t0 + n], func=AF.Identity),
                 reads=xfb(st), writes=xbb(st))
        with ExitStack() as ph:
            tmp = Tile(S, ph, "lbtmp", [128, 4, 8], F32)
            b0 = VEC[:, V_LB:V_LB + 8]
            b1 = VEC[:, V_LB + 8:V_LB + 16]
            vb = [VEC.b[0]]
            tb = [tmp.b[0]]
            S.op("dve", lambda e: e.tensor_tensor(out=tmp[:, 0, :], in0=b0, in1=b1, op=ALU.subtract), reads=vb, writes=tb)
            S.op("act", lambda e: e.activation(out=tmp[:, 1, :], in_=tmp[:, 0, :], func=AF.Sigmoid), reads=tb, writes=tb)
            S.op("act", lambda e: e.activation(out=tmp[:, 2, :], in_=tmp[:, 0, :], func=AF.Sigmoid, scale=-1.0), reads=tb, writes=tb)
            S.op("dve", lambda e: e.tensor_tensor(out=LBT[:, 0, 0, :], in0=tmp[:, 1, :], in1=tmp[:, 1, :], op=ALU.subtract),
                 reads=tb, writes=[LBT.b[0]])
            S.op("dve", lambda e: e.tensor_tensor(out=tmp[:, 3, :], in0=tmp[:, 1, :], in1=tmp[:, 2, :], op=ALU.add), reads=tb, writes=tb)
            S.op("dve", lambda e: e.tensor_tensor(out=LBT[:, 1, 0, :], in0=tmp[:, 3, :], in1=tmp[:, 1, :], op=ALU.subtract),
                 reads=tb, writes=[LBT.b[0]])
            for jj in range(2):
                S.op("dve", lambda e, jj=jj: e.tensor_scalar(out=LBT[:, jj, 1, :], in0=LBT[:, jj, 0, :], scalar1=-1.0, scalar2=1.0,
                                                             op0=ALU.mult, op1=ALU.add), reads=[LBT.b[0]], writes=[LBT.b[0]])
                S.op("dve", lambda e, jj=jj: e.tensor_scalar(out=LBT[:, jj, 3, :], in0=LBT[:, jj, 1, :], scalar1=0.5, scalar2=None,
                                                             op0=ALU.mult, op1=ALU.bypass), reads=[LBT.b[0]], writes=[LBT.b[0]])
                S.op("dve", lambda e, jj=jj: e.tensor_tensor(out=LBT[:, jj, 2, :], in0=LBT[:, jj, 0, :], in1=LBT[:, jj, 3, :], op=ALU.add),
                     reads=[LBT.b[0]], writes=[LBT.b[0]])
            S.barrier()

        def mm_group(bank, n, lhs_list, rhs_list, reads):
            k = len(lhs_list)
            for i in range(k):
                S.op("pe", lambda e, i=i: e.matmul(bank[:, 0:n], lhsT=lhs_list[i], rhs=rhs_list[i],
                                                   start=(i == 0), stop=(i == k - 1)),
                     reads=reads[i], writes=[bank.b[0]])

        def colsum_bcast(bank, n, src_tile_ap_fn, reads_fn, lhs=None):
            lhs = lhs or ONESB
            for kc in range(KC):
                S.op("pe", lambda e, kc=kc: e.matmul(bank[:, 0:n], lhsT=lhs[:], rhs=src_tile_ap_fn(kc),
                                                     start=(kc == 0), stop=(kc == KC - 1)),
                     reads=[lhs.b[0]] + reads_fn(kc), writes=[bank.b[0]])

        def layer_norm(ph_t, st, wcol, bcol, banks=None):
            t0, n = STS[st]
            SQ, MEAN, M2, RSTD = ph_t
            xf3 = XF[:, :, t0:t0 + n]
            xb3 = XB[:, :, t0:t0 + n]
            S.op("act", lambda e: e.activation(out=xb3, in_=xf3, func=AF.Identity), reads=xfb(st), writes=xbb(st))
            S.op("act", lambda e: e.activation(out=SQ[:, :, 0:n], in_=xf3, func=AF.Square), reads=xfb(st), writes=SQ.b)
            p1 = banks[0] if banks else S.bank()
            colsum_bcast(p1, n, lambda kc: XB[:, kc, t0:t0 + n], lambda kc: xbb(st, [kc]), lhs=ONESD)
            p2 = banks[1] if banks else S.bank()
            colsum_bcast(p2, n, lambda kc: SQ[:, kc, 0:n], lambda kc: SQ.b, lhs=ONESD)
            S.op("act", lambda e: e.activation(out=M2[:, 0:n], in_=p1[:, 0:n], func=AF.Square), reads=[p1.b[0]], writes=[M2.b[0]])
            S.op("dve", lambda e: e.tensor_tensor(out=RSTD[:, 0:n], in0=p2[:, 0:n], in1=M2[:, 0:n], op=ALU.subtract),
                 reads=[p2.b[0], M2.b[0]], writes=[RSTD.b[0]])
            S.op("act", lambda e: e.activation(out=RSTD[:, 0:n], in_=RSTD[:, 0:n], func=AF.Ln, bias=EPS[:, 0:1]),
                 reads=[RSTD.b[0], EPS.b[0]], writes=[RSTD.b[0]])
            S.op("act", lambda e: e.activation(out=p2[:, 0:n], in_=RSTD[:, 0:n], func=AF.Exp, scale=-0.5),
                 reads=[RSTD.b[0]], writes=[p2.b[0]])
            mb = p1.v([[0, KC], [1, n]])
            rb = p2.v([[0, KC], [1, n]])
            S.op("dve", lambda e: e.tensor_tensor(out=xf3, in0=xf3, in1=mb, op=ALU.subtract),
                 reads=xfb(st) + [p1.b[0]], writes=xfb(st))
            S.op("dve", lambda e: e.tensor_tensor(out=xf3, in0=xf3, in1=rb, op=ALU.mult),
                 reads=xfb(st) + [p2.b[0]], writes=xfb(st))
            for kc in range(KC):
                S.op("act", lambda e, kc=kc: e.activation(out=XF[:, kc, t0:t0 + n], in_=XF[:, kc, t0:t0 + n], func=AF.Identity,
                                                          scale=VEC[:, wcol + kc:wcol + kc + 1], bias=VEC[:, bcol + kc:bcol + kc + 1]),
                     reads=xfb(st, [kc]) + [VEC.b[0]], writes=xfb(st, [kc]))
            S.op("dve", lambda e: e.tensor_copy(out=xb3, in_=xf3), reads=xfb(st), writes=xbb(st))

        def ln_phase(wcol, bcol):
            with ExitStack() as ph:
                sets = []
                for i in range(2):
                    sets.append((Tile(S, ph, "lnSQ", [128, KC, 512], BF16), Tile(S, ph, "lnMEAN", [128, 512], F32),
                                 Tile(S, ph, "lnM2", [128, 512], F32), Tile(S, ph, "lnRSTD", [128, 512], F32)))
                for st in range(NST):
                    layer_norm(sets[st % 2], st, wcol, bcol)
                S.barrier()

        def ffn_phase(l, last):
            vbl = l * V_LAYER
            vb = vbl + 32
            with ExitStack() as ph:
                H = Tile(S, ph, "ffH", [128, 8, NT], BF16, nb=8 * NST)
                SGt = [Tile(S, ph, "ffSG%d" % i, [128, 512], F32) for i in range(2)]
                plan2 = [(("wple", l, st, n), wple[l, n], 1280, None) for st in range(NST) for n in range(8)]
                W2 = WStream(S, ph, plan2, width=1280, nring=3, la=1)
                M2 = Tile(S, ph, "ffM2", [128, 512], F32)
                RSTD = Tile(S, ph, "ffRSTD", [128, 512], F32)
                PS = Tile(S, ph, "plPS", [128, 2, 512], F32)
                PB = Tile(S, ph, "plPB", [128, 2, 512], BF16)
                E = Tile(S, ph, "plE", [128, KC, 512], F32, nb=KC)
                RS = Tile(S, ph, "plRS", [128, 512], F32)
                k = 0

                def stage2_evac(ps, nn, st, po):
                    t0, n = STS[st]
                    xs = XF[:, nn, t0:t0 + n]
                    if ps == 0:
                        S.op("dve", lambda e: e.scalar_tensor_tensor(out=xs, in0=xs, scalar=ALPHA, in1=po[:, 0:n],
                                                                    op0=ALU.mult, op1=ALU.add),
                             reads=xfb(st, [nn]) + [po.b[0]], writes=xfb(st, [nn]))
                    else:
                        S.op("dve", lambda e: e.tensor_tensor(out=xs, in0=xs, in1=po[:, 0:n], op=ALU.add),
                             reads=xfb(st, [nn]) + [po.b[0]], writes=xfb(st, [nn]))

                def hsq(st):
                    return HView(H, STS[st][0], [H.b[ci * NST + st] for ci in range(8)])

                def ple_st(st):
                    t0, n = STS[st]
                    SQ = hsq(st)
                    S.dma("sp", PS[:, :, 0:n], pT[l, :, t0:t0 + n].rearrange("(k p) t -> p k t", p=128), ldchan(),
                          writes=[PS.b[0]])
                    S.op("act", lambda e: e.activation(out=PB[:, :, 0:n], in_=PS[:, :, 0:n], func=AF.Identity), reads=[PS.b[0]], writes=[PB.b[0]])
                    for nn in range(8):
                        w = W2.get(("wple", l, st, nn))
                        pp = S.banks[4 + 2 * (nn % 2)]
                        for k2 in range(2):
                            S.op("pe", lambda e, k2=k2, w=w, pp=pp: e.matmul(pp[:, 0:n], lhsT=w[:, 1024 + k2 * 128: 1024 + (k2 + 1) * 128],
                                                                            rhs=PB[:, k2, 0:n], start=(k2 == 0), stop=(k2 == 1)),
                                 reads=[w.b[0], PB.b[0]], writes=[pp.b[0]])
                        pg = S.banks[5 + 2 * (nn % 2)]
                        for kc in range(KC):
                            S.op("pe", lambda e, kc=kc, w=w, pg=pg: e.matmul(pg[:, 0:n], lhsT=w[:, kc * 128:(kc + 1) * 128],
                                                                            rhs=XB[:, kc, t0:t0 + n], start=(kc == 0), stop=(kc == KC - 1)),
                                 reads=[w.b[0]] + xbb(st, [kc]), writes=[pg.b[0]])
                        sg = SGt[nn % 2]
                        S.op("act", lambda e, sg=sg, pg=pg: e.activation(out=sg[:, 0:n], in_=pg[:, 0:n], func=AF.Sigmoid),
                             reads=[pg.b[0]], writes=[sg.b[0]])
                        S.op("dve", lambda e, sg=sg, pp=pp, nn=nn: e.tensor_tensor(out=E[:, nn, 0:n], in0=sg[:, 0:n], in1=pp[:, 0:n],
                                                                                op=ALU.mult),
                             reads=[sg.b[0], pp.b[0]], writes=[E.b[nn]])
                    S.op("act", lambda e: e.activation(out=SQ[:, :, 0:n], in_=E[:, :, 0:n], func=AF.Square), reads=E.b, writes=SQ.b)
                    p2 = S.banks[4]
                    colsum_bcast(p2, n, lambda kc: SQ[:, kc, 0:n], lambda kc: SQ.b)
                    S.op("act", lambda e: e.activation(out=RS[:, 0:n], in_=p2[:, 0:n], func=AF.Ln, scale=1.0 / D, bias=EPS[:, 1:2]),
                         reads=[p2.b[0], EPS.b[0]], writes=[RS.b[0]])
                    S.op("act", lambda e: e.activation(out=p2[:, 0:n], in_=RS[:, 0:n], func=AF.Exp, scale=-0.5),
                         reads=[RS.b[0]], writes=[p2.b[0]])
                    S.op("dve", lambda e: e.tensor_tensor(out=E[:, :, 0:n], in0=E[:, :, 0:n], in1=p2.v([[0, KC], [1, n]]), op=ALU.mult),
                         reads=E.b + [p2.b[0]], writes=E.b)
                    for nn in range(8):
                        xs = XF[:, nn, t0:t0 + n]
                        S.op("dve", lambda e, nn=nn, xs=xs: e.scalar_tensor_tensor(out=xs, in0=E[:, nn, 0:n],
                                                                                  scalar=VEC[:, vb + nn:vb + nn + 1], in1=xs,
                                                                                  op0=ALU.mult, op1=ALU.add),
                             reads=[E.b[nn], VEC.b[0]] + xfb(st, [nn]), writes=xfb(st, [nn]))
                    if last:
                        S.dma("sp", yT[:, t0:t0 + n].rearrange("(k p) t -> p k t", p=128), XF[:, :, t0:t0 + n], stchan(),
                              reads=xfb(st))
                    else:
                        S.op("act", lambda e: e.activation(out=XB[:, :, t0:t0 + n], in_=XF[:, :, t0:t0 + n], func=AF.Identity),
                             reads=xfb(st), writes=xbb(st))

                for ps, (c0, c1) in enumerate(FPASS):
                    ncp = c1 - c0
                    for ci in range(ncp):
                        w = W.get(("wgu", l, c0 + ci))
                        for st, (t0, n) in enumerate(STS):
                            pg = S.bank()
                            pu = S.bank()
                            for gu, bk in ((0, pg), (1, pu)):
                                for kc in range(KC):
                                    S.op("pe", lambda e, gu=gu, kc=kc, bk=bk: e.matmul(
                                        bk[:, 0:n], lhsT=w[:, (gu * 8 + kc) * 128:(gu * 8 + kc + 1) * 128],
                                        rhs=XB[:, kc, t0:t0 + n], start=(kc == 0), stop=(kc == KC - 1)),
                                        reads=[w.b[0]] + xbb(st, [kc]), writes=[bk.b[0]])
                            sg = SGt[k % 2]
                            k += 1
                            S.op("act", lambda e, sg=sg, pg=pg: e.activation(out=sg[:, 0:n], in_=pg[:, 0:n], func=AF.Silu),
                                 reads=[pg.b[0]], writes=[sg.b[0]])
                            S.op("dve", lambda e, sg=sg, pu=pu, ci=ci: e.tensor_tensor(out=H[:, ci, t0:t0 + n], in0=sg[:, 0:n],
                                                                                    in1=pu[:, 0:n], op=ALU.mult),
                                 reads=[sg.b[0], pu.b[0]], writes=[H.b[ci * NST + st]])
                    if ps < len(FPASS) - 1:
                        for nn in range(8):
                            w = W.get(("wdn", l, ps, nn))
                            for st, (t0, n) in enumerate(STS):
                                po = S.bank()
                                for ci in range(ncp):
                                    S.op("pe", lambda e, ci=ci: e.matmul(po[:, 0:n], lhsT=w[:, ci * 128:(ci + 1) * 128],
                                                                        rhs=H[:, ci, t0:t0 + n], start=(ci == 0), stop=(ci == ncp - 1)),
                                         reads=[w.b[0], H.b[ci * NST + st]], writes=[po.b[0]])
                                stage2_evac(ps, nn, st, po)
                    else:
                        def s2_st(st):
                            t0, n = STS[st]
                            for nn in range(8):
                                w = W.get(("wdn", l, ps, st, nn))
                                po = S.banks[nn % 2]
                                for ci in range(ncp):
                                    S.op("pe", lambda e, ci=ci, w=w, po=po: e.matmul(po[:, 0:n], lhsT=w[:, ci * 128:(ci + 1) * 128],
                                                                                    rhs=H[:, ci, t0:t0 + n], start=(ci == 0), stop=(ci == ncp - 1)),
                                         reads=[w.b[0], H.b[ci * NST + st]], writes=[po.b[0]])
                                stage2_evac(ps, nn, st, po)

                        for r in range(NST + 2):
                            l1 = S.record(lambda: s2_st(r)) if r < NST else []
                            l2 = S.record(lambda: layer_norm((hsq(r - 1), None, M2, RSTD), r - 1, vbl + 16, vbl + 24,
                                                             banks=(S.banks[2], S.banks[3]))) if 0 <= r - 1 < NST else []
                            l3 = S.record(lambda: ple_st(r - 2)) if 0 <= r - 2 < NST else []
                            S.replay(l1, l2, l3)
                S.barrier()

        def hgrn_phase(l):
            j = l // 2
            lbA = lambda h: LBT[:, j, 2, h:h + 1]
            omA = lambda h: LBT[:, j, 3, h:h + 1]
            gw = VEC[:, V_GN + j:V_GN + j + 1]
            with ExitStack() as ph:
                def t32(name, w=512, nb=1):
                    return Tile(S, ph, name, [128, w], F32, nb=nb)

                def t16(name, w=512, nb=1):
                    return Tile(S, ph, name, [128, w], BF16, nb=nb)
                SGf, Fm, LF = [t32("hg" + n) for n in "SGf Fm LF".split()]
                KK2 = [t32("hgKK") for i in range(2)]
                Bc2 = [t32("hgBc") for i in range(2)]
                Q12 = [t32("hgQ1") for i in range(2)]
                RV = t32("hgRV", 16)
                BR, EB, KD = [t32("hg" + n) for n in "BR EB KD".split()]
                EQ = BR
                QT = t16("hgQT")
                KT = [t16("hgKT%d" % i) for i in range(4)]
                KTf = t32("hgKTf")
                KDT2 = [t16("hgKDT") for i in range(2)]
                QD2 = [t16("hgQD") for i in range(2)]
                AT2 = [t16("hgATm") for i in range(2)]
                VT3 = [t16("hgVT") for i in range(3)]
                SGG3 = [t16("hgSGG") for i in range(3)]
                EBE2 = [t32("hgEBE", 16) for i in range(2)]
                OSQ = t16("hgOSQ")
                RS, T1 = t32("hgRS"), t32("hgT1")
                ST4 = Tile(S, ph, "hgST4", [128, 3, 128], F32)
                STB4 = Tile(S, ph, "hgSTB4", [128, 4, 128], BF16)
                OG = Tile(S, ph, "hgOG", [128, 8, 512], BF16, nb=8)
                ST = Tile(S, ph, "hgS", [128, 8, 128], F32, nb=8)
                S0 = Tile(S, ph, "hgS0", [128, 8, 128], F32)
                S0B = Tile(S, ph, "hgS0B", [128, 8, 128], BF16)
                SN = Tile(S, ph, "hgSN", [128, 8, 128], F32)
                VM2 = [t16("hgVM", 2048) for i in range(2)]
                R512 = t32("hgR512")
                S.dma("sp", R512[:], consts[:, C_R512:C_R512 + 512], ldchan(), writes=[R512.b[0]])
                S.op("pool", lambda e: e.memset(ST[:], 0.0), writes=ST.b)
                S.op("pool", lambda e: e.memset(RV[:], 0.0), writes=[RV.b[0]])
                for kt in KT:
                    S.op("pool", lambda e, kt=kt: e.memset(kt[:], 0.0), writes=[kt.b[0]])
                cb = [CN.b[0]]
                rr = {"i": 0}

                def bank6():
                    b = S.banks[rr["i"] % 6]
                    rr["i"] += 1
                    return b

                def stage_a(st, h, par):
                    stage_a1(st, h, par)
                    stage_a2(st, h, par)

                def stage_a1(st, h, par):
                    t0, n = STS[st]
                    sample = (st == NST - 1)
                    ntile = n // 128
                    QD, ATm, VT, SGG, EBE = QD2[par], AT2[par], VT3[h % 3], SGG3[h % 3], EBE2[par]
                    KK, Bc, Q1 = KK2[par], Bc2[par], Q12[par]
                    wA = W.get(("hgA", l, st, h))
                    wB = W.get(("hgB", l, st, h))
                    pq, pf, pgt, pv = S.banks[0], S.banks[1], S.banks[2], S.banks[3]
                    for wsel, sub, bk in ((wA, 1, pf), (wA, 0, pq), (wB, 1, pgt)):
                        for kc in range(KC):
                            S.op("pe", lambda e, wsel=wsel, sub=sub, bk=bk, kc=kc: e.matmul(
                                bk[:, 0:n], lhsT=wsel[:, (sub * 8 + kc) * 128:(sub * 8 + kc + 1) * 128],
                                rhs=XB[:, kc, t0:t0 + n], start=(kc == 0), stop=(kc == KC - 1)),
                                reads=[wsel.b[0]] + xbb(st, [kc]), writes=[bk.b[0]])
                    for tt_ in range(ntile):
                        for kc in range(KC):
                            S.op("pe", lambda e, tt_=tt_, kc=kc: e.matmul(
                                pv[:, tt_ * 128:(tt_ + 1) * 128], lhsT=XB[:, kc, t0 + tt_ * 128:t0 + (tt_ + 1) * 128],
                                rhs=wB[:, kc * 128:(kc + 1) * 128], start=(kc == 0), stop=(kc == KC - 1)),
                                reads=[wB.b[0]] + xbb(st, [kc]), writes=[pv.b[0]])
                    S.op("act", lambda e: e.activation(out=SGf[:, 0:n], in_=pf[:, 0:n], func=AF.Tanh, scale=0.5), reads=[pf.b[0]], writes=[SGf.b[0]])
                    S.op("act", lambda e: e.activation(out=Q1[:, 0:n], in_=pq[:, 0:n], func=AF.Silu), reads=[pq.b[0]], writes=[Q1.b[0]])
                    S.op("act", lambda e: e.activation(out=SGG[:, 0:n], in_=pgt[:, 0:n], func=AF.Silu), reads=[pgt.b[0]], writes=[SGG.b[0]])
                    S.op("act", lambda e: e.activation(out=VT[:, 0:n], in_=pv[:, 0:n], func=AF.Identity), reads=[pv.b[0]], writes=[VT.b[0]])
                    S.op("dve", lambda e: e.tensor_scalar(out=Fm[:, 0:n], in0=SGf[:, 0:n], scalar1=omA(h), scalar2=lbA(h),
                                                          op0=ALU.mult, op1=ALU.add),
                         reads=[SGf.b[0], LBT.b[0]], writes=[Fm.b[0]])
                    S.op("act", lambda e: e.activation(out=LF[:, 0:n], in_=Fm[:, 0:n], func=AF.Ln), reads=[Fm.b[0]], writes=[LF.b[0]])
                    S.op("pool", lambda e: e.tensor_scalar(out=KK[:, 0:n], in0=Fm[:, 0:n], scalar1=-1.0, scalar2=1.0,
                                                           op0=ALU.mult, op1=ALU.add), reads=[Fm.b[0]], writes=[KK.b[0]])
                    rst = CN[:, C_R8:C_R8 + 128] if sample else R512[:, 0:n]
                    S.op("dve", lambda e: e.tensor_tensor_scan(out=Bc[:, 0:n], data0=rst, data1=LF[:, 0:n], initial=0.0,
                                                               op0=ALU.mult, op1=ALU.add),
                         reads=[LF.b[0], R512.b[0]] + cb, writes=[Bc.b[0]])

                def stage_a2(st, h, par):
                    t0, n = STS[st]
                    sample = (st == NST - 1)
                    ntile = n // 128
                    QD, ATm, VT, SGG, EBE = QD2[par], AT2[par], VT3[h % 3], SGG3[h % 3], EBE2[par]
                    KK, Bc, Q1 = KK2[par], Bc2[par], Q12[par]
                    KDT, VM = KDT2[par], VM2[par]
                    psu = S.banks[6 + par]
                    S.op("act", lambda e: e.activation(out=EB[:, 0:n], in_=Bc[:, 0:n], func=AF.Exp), reads=[Bc.b[0]], writes=[EB.b[0]])
                    S.op("dve", lambda e: e.scalar_tensor_tensor(out=QD[:, 0:n], in0=Q1[:, 0:n], scalar=QK, in1=EB[:, 0:n],
                                                                op0=ALU.mult, op1=ALU.mult),
                         reads=[Q1.b[0], EB.b[0]], writes=[QD.b[0]])
                    L = 8 if sample else 128
                    nseg = n // L
                    bend = Bc.v([[L, nseg], [0, L]], off=L - 1)
                    b3 = Bc.v([[L, nseg], [1, L]])
                    S.op("pool", lambda e: e.tensor_tensor(out=KD.v([[L, nseg], [1, L]]), in0=bend, in1=b3, op=ALU.subtract),
                         reads=[Bc.b[0]], writes=[KD.b[0]])
                    S.op("act", lambda e: e.activation(out=KD[:, 0:n], in_=KD[:, 0:n], func=AF.Exp), reads=[KD.b[0]], writes=[KD.b[0]])
                    S.op("pool", lambda e: e.tensor_tensor(out=KD[:, 0:n], in0=KD[:, 0:n], in1=KK[:, 0:n], op=ALU.mult),
                         reads=[KD.b[0], KK.b[0]], writes=[KD.b[0]])
                    S.op("act", lambda e: e.activation(out=EBE[:, 0:nseg], in_=Bc.v([[L, nseg]], off=L - 1), func=AF.Exp),
                         reads=[Bc.b[0]], writes=[EBE.b[0]])
                    pk = S.banks[5]
                    for tt_ in range(ntile):
                        S.op("pe", lambda e, tt_=tt_: e.transpose(out=pk[:, tt_ * 128:(tt_ + 1) * 128], in_=KD[:, tt_ * 128:(tt_ + 1) * 128],
                                                                  identity=CN[:, C_ID:C_ID + 128]),
                             reads=[KD.b[0]] + cb, writes=[pk.b[0]])
                    S.op("act", lambda e: e.activation(out=KDT[:, 0:n], in_=pk[:, 0:n], func=AF.Identity), reads=[pk.b[0]], writes=[KDT.b[0]])
                    pa = S.banks[5]
                    if not sample:
                        for tt_ in range(ntile):
                            cs = slice(tt_ * 128, (tt_ + 1) * 128)
                            S.op("pe", lambda e, cs=cs: e.matmul(psu[:, cs], lhsT=KDT[:, cs], rhs=VT[:, cs], start=True, stop=True),
                                 reads=[KDT.b[0], VT.b[0]], writes=[psu.b[0]])
                        S.op("pool", lambda e: e.tensor_copy(out=RV.v([[4, ntile], [1, 3]], off=1),
                                                             in_=Bc.v([[128, ntile], [32, 3]], off=31)),
                             reads=[Bc.b[0]], writes=[RV.b[0]])
                        S.op("pool", lambda e: e.tensor_tensor(out=BR.v([[32, 16], [1, 32]]), in0=Bc.v([[32, 16], [1, 32]]),
                                                               in1=RV.v([[1, 16], [0, 32]]), op=ALU.subtract),
                             reads=[Bc.b[0], RV.b[0]], writes=[BR.b[0]])
                        S.op("act", lambda e: e.activation(out=EQ[:, 0:n], in_=BR[:, 0:n], func=AF.Exp), reads=[BR.b[0]], writes=[EQ.b[0]])
                        S.op("dve", lambda e: e.scalar_tensor_tensor(out=QT[:, 0:n], in0=Q1[:, 0:n], scalar=QK, in1=EQ[:, 0:n],
                                                                    op0=ALU.mult, op1=ALU.mult),
                             reads=[Q1.b[0], EQ.b[0]], writes=[QT.b[0]])
                        for i in range(4):
                            wd_ = 32 * (i + 1)
                            kf = KTf
                            S.op("pool", lambda e, i=i, wd_=wd_, kf=kf: e.tensor_tensor(
                                out=kf.v([[128, ntile], [1, wd_]]), in0=RV.v([[4, ntile], [0, wd_]], off=i),
                                in1=Bc.v([[128, ntile], [1, wd_]]), op=ALU.subtract),
                                reads=[RV.b[0], Bc.b[0]], writes=[kf.b[0]])
                            S.op("act", lambda e, wd_=wd_, kf=kf: e.activation(out=kf.v([[128, ntile], [1, wd_]]),
                                                                              in_=kf.v([[128, ntile], [1, wd_]]), func=AF.Exp),
                                 reads=[kf.b[0]], writes=[kf.b[0]])
                            S.op("dve", lambda e, i=i, wd_=wd_, kf=kf: e.tensor_tensor(
                                out=KT[i].v([[128, ntile], [1, wd_]]), in0=kf.v([[128, ntile], [1, wd_]]),
                                in1=KK.v([[128, ntile], [1, wd_]]), op=ALU.mult),
                                reads=[kf.b[0], KK.b[0]], writes=[KT[i].b[0]])
                        for tt_ in range(ntile):
                            for i in range(4):
                                c0 = tt_ * 128 + 32 * i
                                S.op("pe", lambda e, tt_=tt_, i=i, c0=c0: e.matmul(pa[:, c0:c0 + 32], lhsT=KT[i][:, tt_ * 128:(tt_ + 1) * 128],
                                                                                 rhs=QT[:, c0:c0 + 32], start=True, stop=True),
                                     reads=[KT[i].b[0], QT.b[0]], writes=[pa.b[0]])
                        S.op("dve", lambda e: e.tensor_tensor(out=ATm.v([[128, ntile], [1, 128]]), in0=pa.v([[128, ntile], [1, 128]]),
                                                              in1=CN.v([[0, ntile], [1, 128]], off=C_CAUS), op=ALU.mult),
                             reads=[pa.b[0]] + cb, writes=[ATm.b[0]])
                    else:
                        kf = KTf
                        S.op("act", lambda e: e.activation(out=kf[:, 0:n], in_=Bc[:, 0:n], func=AF.Exp, scale=-1.0),
                             reads=[Bc.b[0]], writes=[kf.b[0]])
                        S.op("dve", lambda e: e.tensor_tensor(out=KT[0][:, 0:n], in0=kf[:, 0:n], in1=KK[:, 0:n], op=ALU.mult),
                             reads=[kf.b[0], KK.b[0]], writes=[KT[0].b[0]])
                        S.op("pe", lambda e: e.matmul(pa[:, 0:128], lhsT=KT[0][:, 0:128], rhs=QD[:, 0:128], start=True, stop=True),
                             reads=[KT[0].b[0], QD.b[0]], writes=[pa.b[0]])
                        S.op("dve", lambda e: e.tensor_tensor(out=ATm[:, 0:128], in0=pa[:, 0:128], in1=CN[:, C_BD:C_BD + 128], op=ALU.mult),
                             reads=[pa.b[0]] + cb, writes=[ATm.b[0]])
                        S.op("dve", lambda e: e.tensor_tensor(out=VM.v([[128, 16], [1, 128]]), in0=VT.v([[0, 16], [1, 128]]),
                                                              in1=CN.v([[1, 16], [0, 128]], off=C_SEG), op=ALU.mult),
                             reads=[VT.b[0]] + cb, writes=[VM.b[0]])

                def stage_b(st, h, par):
                    t0, n = STS[st]
                    sample = (st == NST - 1)
                    ntile = n // 128
                    QD, ATm, VT, SGG, EBE = QD2[par], AT2[par], VT3[h % 3], SGG3[h % 3], EBE2[par]
                    psu = S.banks[6 + par]
                    KDT, VM = KDT2[par], VM2[par]
                    po = S.banks[4]
                    if not sample:
                        S.op("act", lambda e: e.activation(out=STB4[:, 0, :], in_=ST[:, h, :], func=AF.Identity),
                             reads=[ST.b[h]], writes=[STB4.b[0]])
                        for tt_ in range(ntile):
                            src = ST[:, h, :] if tt_ == 0 else ST4[:, tt_ - 1, :]
                            dst = ST[:, h, :] if tt_ == ntile - 1 else ST4[:, tt_, :]
                            S.op("dve", lambda e, tt_=tt_, src=src, dst=dst: e.scalar_tensor_tensor(
                                out=dst, in0=src, scalar=EBE[:, tt_:tt_ + 1], in1=psu[:, tt_ * 128:(tt_ + 1) * 128],
                                op0=ALU.mult, op1=ALU.add),
                                reads=[ST.b[h], ST4.b[0], EBE.b[0], psu.b[0], STB4.b[0]], writes=[ST.b[h], ST4.b[0]])
                        S.op("act", lambda e: e.activation(out=STB4[:, 1:4, :], in_=ST4[:, 0:3, :], func=AF.Identity),
                             reads=[ST4.b[0]], writes=[STB4.b[0]])
                        for tt_ in range(ntile):
                            cs = slice(tt_ * 128, (tt_ + 1) * 128)
                            S.op("pe", lambda e, cs=cs: e.matmul(po[:, cs], lhsT=VT[:, cs], rhs=ATm[:, cs], start=True, stop=False),
                                 reads=[VT.b[0], ATm.b[0]], writes=[po.b[0]])
                            S.op("pe", lambda e, cs=cs, tt_=tt_: e.matmul(po[:, cs], lhsT=STB4[:, tt_, :], rhs=QD[:, cs], start=False, stop=True),
                                 reads=[STB4.b[0], QD.b[0]], writes=[po.b[0]])
                        if st == NST - 2:
                            S.dma("sp", hgp[j, :, h, :], ST[:, h, :], stchan(), reads=[ST.b[h]])
                    else:
                        S.op("pe", lambda e: e.matmul(po[:, 0:128], lhsT=VT[:, 0:128], rhs=ATm[:, 0:128], start=True, stop=False),
                             reads=[VT.b[0], ATm.b[0]], writes=[po.b[0]])
                        for hf in range(2):
                            S.dma("sp", S0[:], hg0[j, hf * 8:(hf + 1) * 8, h].rearrange("b k v -> k b v"), ldchan(), writes=[S0.b[0]])
                            S.op("act", lambda e: e.activation(out=S0B[:], in_=S0[:], func=AF.Identity), reads=[S0.b[0]], writes=[S0B.b[0]])
                            for bb in range(8):
                                b = hf * 8 + bb
                                S.op("pe", lambda e, bb=bb, b=b: e.matmul(po[:, b * 8:(b + 1) * 8], lhsT=S0B[:, bb, :],
                                                                       rhs=QD[:, b * 8:(b + 1) * 8], start=False, stop=(b == 15)),
                                     reads=[S0B.b[0], QD.b[0]], writes=[po.b[0]])
                            pss2 = [S.banks[6], S.banks[7]]
                            for q in range(2):
                                S.op("pe", lambda e, q=q, hf=hf: e.matmul(pss2[q][:, 0:512], lhsT=KDT[:, 0:128],
                                                                       rhs=VM[:, hf * 1024 + q * 512: hf * 1024 + (q + 1) * 512],
                                                                       start=True, stop=True),
                                     reads=[KDT.b[0], VM.b[0]], writes=[pss2[q].b[0]])
                            S.op("pool", lambda e, hf=hf: e.tensor_tensor(out=SN[:], in0=S0[:],
                                                                       in1=EBE.v([[1, 8], [0, 128]], off=hf * 8), op=ALU.mult),
                                 reads=[S0.b[0], EBE.b[0]], writes=[SN.b[0]])
                            for q in range(2):
                                S.op("dve", lambda e, q=q: e.tensor_tensor(out=SN[:, q * 4:(q + 1) * 4, :], in0=SN[:, q * 4:(q + 1) * 4, :],
                                                                        in1=pss2[q].v([[128, 4], [1, 128]]), op=ALU.add),
                                     reads=[SN.b[0], pss2[q].b[0]], writes=[SN.b[0]])
                            S.dma("sp", hgs[j, hf * 8:(hf + 1) * 8, h].rearrange("b k v -> k b v"), SN[:], stchan(), reads=[SN.b[0]])
                    S.op("act", lambda e: e.activation(out=OSQ[:, 0:n], in_=po[:, 0:n], func=AF.Square), reads=[po.b[0]], writes=[OSQ.b[0]])
                    pss = psu if not sample else S.banks[6]
                    S.op("pe", lambda e: e.matmul(pss[:, 0:n], lhsT=ONESB[:], rhs=OSQ[:, 0:n], start=True, stop=True),
                         reads=[ONESB.b[0], OSQ.b[0]], writes=[pss.b[0]])
                    S.op("act", lambda e: e.activation(out=RS[:, 0:n], in_=pss[:, 0:n], func=AF.Ln, scale=1.0 / 128, bias=EPS[:, 1:2]),
                         reads=[pss.b[0], EPS.b[0]], writes=[RS.b[0]])
                    S.op("act", lambda e: e.activation(out=RS[:, 0:n], in_=RS[:, 0:n], func=AF.Exp, scale=-0.5),
                         reads=[RS.b[0]], writes=[RS.b[0]])
                    S.op("dve", lambda e: e.tensor_tensor(out=T1[:, 0:n], in0=po[:, 0:n], in1=RS[:, 0:n], op=ALU.mult),
                         reads=[po.b[0], RS.b[0]], writes=[T1.b[0]])
                    S.op("dve", lambda e: e.scalar_tensor_tensor(out=OG[:, h, 0:n], in0=T1[:, 0:n], scalar=gw, in1=SGG[:, 0:n],
                                                                op0=ALU.mult, op1=ALU.mult),
                         reads=[T1.b[0], SGG.b[0], VEC.b[0]], writes=[OG.b[h]])

                def out_proj(st):
                    t0, n = STS[st]
                    for nn in range(8):
                        w = W.get(("who", l, st, nn))
                        po = bank6()
                        for h in range(8):
                            S.op("pe", lambda e, h=h: e.matmul(po[:, 0:n], lhsT=w[:, h * 128:(h + 1) * 128], rhs=OG[:, h, 0:n],
                                                              start=(h == 0), stop=(h == 7)),
                                 reads=[w.b[0], OG.b[h]], writes=[po.b[0]])
                        xs = XF[:, nn, t0:t0 + n]
                        S.op("dve", lambda e, xs=xs: e.scalar_tensor_tensor(out=xs, in0=xs, scalar=ALPHA, in1=po[:, 0:n],
                                                                           op0=ALU.mult, op1=ALU.add),
                             reads=xfb(st, [nn]) + [po.b[0]], writes=xfb(st, [nn]))

                for st in range(NST):
                    if True:
                        stage_a1(st, 0, 0)
                        S.replay(S.record(lambda: stage_a1(st, 1, 1)), S.record(lambda: stage_a2(st, 0, 0)))
                        for h in range(8):
                            l1 = S.record(lambda: stage_a1(st, h + 2, (h + 2) % 2)) if h + 2 < 8 else []
                            l2 = S.record(lambda: stage_a2(st, h + 1, (h + 1) % 2)) if h + 1 < 8 else []
                            l3 = S.record(lambda: stage_b(st, h, h % 2))
                            S.replay(l1, l2, l3, weights=[0.4, 1.0, 1.0])
                    out_proj(st)
                    layer_norm((OG, None, SGf, Fm), st, l * V_LAYER + 0, l * V_LAYER + 8)
                S.barrier()

        def s5_phase(l):
            j = l // 2
            PI_S = 3.1415925

            def tt(eng, out, a, b, op, r, w):
                S.op(eng, lambda e: e.tensor_tensor(out=out, in0=a, in1=b, op=op), reads=r, writes=w)

            def ts(eng, out, a, s1, s2, op0, op1, r, w):
                S.op(eng, lambda e: e.tensor_scalar(out=out, in0=a, scalar1=s1, scalar2=s2, op0=op0, op1=op1), reads=r, writes=w)

            def stt(out, a, sc, b, op0, op1, r, w):
                S.op("dve", lambda e: e.scalar_tensor_tensor(out=out, in0=a, scalar=sc, in1=b, op0=op0, op1=op1), reads=r, writes=w)

            def act(out, in_, func, r, w, **kw):
                S.op("act", lambda e: e.activation(out=out, in_=in_, func=func, **kw), reads=r, writes=w)

            with ExitStack() as ph:
                EQ = Tile(S, ph, "s5EQ", [128, 2048], BF16)
                EPQ = Tile(S, ph, "s5EPQ", [128, 2048], BF16)
                EQz = Tile(S, ph, "s5EQz", [128, 2048], BF16)
                EPQz = Tile(S, ph, "s5EPQz", [128, 2048], BF16)
                PAr, PAi, PBr, PBi = [Tile(S, ph, "s5P" + n, [128, 16, 32], F32) for n in "Ar Ai Br Bi".split()]
                BBR, BBI, CR, CI = [Tile(S, ph, "s5" + n, [128, 32, 16], F32) for n in "BBR BBI CR CI".split()]
                MUr, MUi, MUn = [Tile(S, ph, "s5MU" + n, [128, 8, 32], F32) for n in "r i n".split()]
                DB = Tile(S, ph, "s5DB", [128, 64], F32)
                OP = Tile(S, ph, "s5OP", [128, 2, 32], F32)
                cb = [CN.b[0]]
                S.dma("sp", DB[:], s5d[j], ldchan(), writes=[DB.b[0]])
                S.dma("sp", CR[:], s5c[j, 0].rearrange("p (a c) -> p a c", c=16), ldchan(), writes=[CR.b[0]])
                S.dma("sp", CI[:], s5c[j, 1].rearrange("p (a c) -> p a c", c=16), ldchan(), writes=[CI.b[0]])
                with ExitStack() as su:
                    A3 = Tile(S, su, "s5A3", [128, 96], F32)
                    BRE = Tile(S, su, "s5BRE", [128, 32, 16], F32)
                    BIM = Tile(S, su, "s5BIM", [128, 32, 16], F32)
                    EST = Tile(S, su, "s5EST", [128, 2048], F32)
                    ANG, T1, T2, SNt, CSt, MG = [Tile(S, su, "s5" + n, [128, 512], F32) for n in "ANG T1 T2 SN CS MG".split()]
                    SM = Tile(S, su, "s5SM", [128, 16, 32], F32)
                    TAU = Tile(S, su, "s5TAU", [128, 512], F32)
                    S.dma("sp", TAU[:], consts[:, C_TAU:C_TAU + 512], ldchan(), writes=[TAU.b[0]])
                    S.dma("sp", A3[:], s5a[j], ldchan(), writes=[A3.b[0]])
                    S.dma("sp", BRE[:], s5b[j, 0].rearrange("p (a c) -> p a c", c=16), ldchan(), writes=[BRE.b[0]])
                    S.dma("sp", BIM[:], s5b[j, 1].rearrange("p (a c) -> p a c", c=16), ldchan(), writes=[BIM.b[0]])
                    for k, dst in enumerate((EQ, EPQ, EQz, EPQz)):
                        S.dma("sp", EST[:], emat[k], ldchan(), writes=[EST.b[0]])
                        S.op("pool", lambda e, dst=dst: e.tensor_copy(out=dst[:], in_=EST[:]), reads=[EST.b[0]], writes=[dst.b[0]])
                    sm = [SM.b[0]]
                    DL, LR, LI = SM[:, 0, :], SM[:, 1, :], SM[:, 2, :]
                    act(DL, A3[:, 64:96], AF.Exp, [A3.b[0]], sm)
                    tt("dve", LR, A3[:, 0:32], DL, ALU.mult, [A3.b[0]] + sm, sm)
                    tt("dve", LI, A3[:, 32:64], DL, ALU.mult, [A3.b[0]] + sm, sm)
                    tau = TAU[:]
                    for sg, (Pr, Pi) in ((1.0, (PAr, PAi)), (-1.0, (PBr, PBi))):
                        stt(ANG[:], tau, sg, SM.v([[0, 16], [1, 32]], off=2 * 32), ALU.mult, ALU.mult, [TAU.b[0]] + sm, [ANG.b[0]])
                        stt(MG[:], tau, sg, SM.v([[0, 16], [1, 32]], off=1 * 32), ALU.mult, ALU.mult, [TAU.b[0]] + sm, [MG.b[0]])
                        act(MG[:], MG[:], AF.Exp, [MG.b[0]], [MG.b[0]])
                        for which, dstt in ((0, SNt), (1, CSt)):
                            if which == 1:
                                ts("dve", ANG[:], ANG[:], math.pi / 2, None, ALU.add, ALU.bypass, [ANG.b[0]], [ANG.b[0]])
                            ts("dve", T1[:], ANG[:], 1.0 / TWO_PI, MAGIC, ALU.mult, ALU.add, [ANG.b[0]], [T1.b[0]])
                            ts("dve", T1[:], T1[:], -MAGIC, None, ALU.add, ALU.bypass, [T1.b[0]], [T1.b[0]])
                            stt(T2[:], T1[:], -TWO_PI, ANG[:], ALU.mult, ALU.add, [T1.b[0], ANG.b[0]], [T2.b[0]])
                            ts("dve", T2[:], T2[:], -PI_S, PI_S, ALU.max, ALU.min, [T2.b[0]], [T2.b[0]])
                            act(dstt[:], T2[:], AF.Sin, [T2.b[0]], [dstt.b[0]])
                        tt("dve", Pr[:].rearrange("p a b -> p (a b)"), MG[:], CSt[:], ALU.mult, [MG.b[0], CSt.b[0]], [Pr.b[0]])
                        tt("dve", Pi[:].rearrange("p a b -> p (a b)"), MG[:], SNt[:], ALU.mult, [MG.b[0], SNt.b[0]], [Pi.b[0]])
                    ar, ai = A3[:, 0:32], A3[:, 32:64]
                    NR, DEN, t1, t2, CRE, CIM = [SM[:, 3 + k, :] for k in range(6)]
                    a3 = [A3.b[0]]
                    ts("dve", NR, PAr[:, 8, :], -1.0, None, ALU.add, ALU.bypass, [PAr.b[0]], sm)
                    NI = PAi[:, 8, :]
                    tt("dve", t1, ar, ar, ALU.mult, a3, sm)
                    tt("dve", t2, ai, ai, ALU.mult, a3, sm)
                    tt("dve", DEN, t1, t2, ALU.add, sm, sm)
                    S.op("dve", lambda e: e.reciprocal(out=DEN, in_=DEN), reads=sm, writes=sm)
                    tt("dve", t1, NR, ar, ALU.mult, sm + a3, sm)
                    tt("dve", t2, NI, ai, ALU.mult, [PAi.b[0]] + a3, sm)
                    tt("dve", t1, t1, t2, ALU.add, sm, sm)
                    tt("dve", CRE, t1, DEN, ALU.mult, sm, sm)
                    tt("dve", t1, NI, ar, ALU.mult, [PAi.b[0]] + a3, sm)
                    tt("dve", t2, NR, ai, ALU.mult, sm + a3, sm)
                    tt("dve", t1, t1, t2, ALU.subtract, sm, sm)
                    tt("dve", CIM, t1, DEN, ALU.mult, sm, sm)
                    creb = SM.v([[1, 32], [0, 16]], off=7 * 32)
                    cimb = SM.v([[1, 32], [0, 16]], off=8 * 32)
                    TA3 = T1[:].rearrange("p (a c) -> p a c", c=16)
                    tt("dve", BBR[:], creb, BRE[:], ALU.mult, sm + [BRE.b[0]], [BBR.b[0]])
                    tt("dve", TA3, cimb, BIM[:], ALU.mult, sm + [BIM.b[0]], [T1.b[0]])
                    tt("dve", BBR[:], BBR[:], TA3, ALU.subtract, [BBR.b[0], T1.b[0]], [BBR.b[0]])
                    tt("dve", BBI[:], creb, BIM[:], ALU.mult, sm + [BIM.b[0]], [BBI.b[0]])
                    tt("dve", TA3, cimb, BRE[:], ALU.mult, sm + [BRE.b[0]], [T1.b[0]])
                    tt("dve", BBI[:], BBI[:], TA3, ALU.add, [BBI.b[0], T1.b[0]], [BBI.b[0]])
                    mub = [MUr.b[0], MUi.b[0]]
                    S.op("pool", lambda e: e.tensor_copy(out=MUr[:, 0, :], in_=PAr[:, 15, :]), reads=[PAr.b[0]], writes=[MUr.b[0]])
                    S.op("pool", lambda e: e.tensor_copy(out=MUi[:, 0, :], in_=PAi[:, 15, :]), reads=[PAi.b[0]], writes=[MUi.b[0]])
                    for k in range(1, 8):
                        re_, im_ = MUr[:, k - 1, :], MUi[:, k - 1, :]
                        tt("dve", t1, re_, re_, ALU.mult, mub, sm)
                        tt("dve", t2, im_, im_, ALU.mult, mub, sm)
                        tt("dve", MUr[:, k, :], t1, t2, ALU.subtract, sm, [MUr.b[0]])
                        tt("dve", t1, re_, im_, ALU.mult, mub, sm)
                        ts("dve", MUi[:, k, :], t1, 2.0, None, ALU.mult, ALU.bypass, sm, [MUi.b[0]])
                    ts("dve", MUn[:], MUi[:], -1.0, None, ALU.mult, ALU.bypass, [MUi.b[0]], [MUn.b[0]])
                    S.barrier()
                with ExitStack() as cs:
                    TA = Tile(S, cs, "s5TA", [128, 512], F32)
                    UR, UI, VR, VIN = [Tile(S, cs, "s5" + n, [128, 4, 128], F32) for n in "UR UI VR VIN".split()]
                    TW2 = [Tile(S, cs, "s5TW", [128, 8, 128], BF16) for _ in range(2)]
                    PRm2 = [[Tile(S, cs, "s5PRm%d" % h, [128, 4, 128], BF16) for h in range(2)] for _ in range(2)]
                    PNm2 = [[Tile(S, cs, "s5PNm%d" % h, [128, 4, 128], BF16) for h in range(2)] for _ in range(2)]
                    QRT2 = [Tile(S, cs, "s5QRT", [128, 4, 128], BF16) for _ in range(2)]
                    QIT2 = [Tile(S, cs, "s5QIT", [128, 4, 128], BF16) for _ in range(2)]
                    QMR2 = [Tile(S, cs, "s5QMR", [128, 4, 128], BF16) for _ in range(2)]
                    QMI2 = [Tile(S, cs, "s5QMI", [128, 4, 128], BF16) for _ in range(2)]
                    H0c2 = [Tile(S, cs, "s5H0c", [128, 2, 64], F32) for _ in range(2)]
                    OSc2 = [Tile(S, cs, "s5OSc", [128, 2, 64], F32) for _ in range(2)]
                    UB3 = [[Tile(S, cs, "s5UB%d" % h, [128, NB], BF16) for h in range(2)] for _ in range(3)]
                    HPr2 = [Tile(S, cs, "s5HPr", [128, NB], BF16) for _ in range(2)]
                    HPi2 = [Tile(S, cs, "s5HPi", [128, NB], BF16) for _ in range(2)]
                    Ar, Ai = [Tile(S, cs, "s5sc" + n, [128, NB], F32) for n in "Ar Ai".split()]
                    Br, Bi = [Tile(S, cs, "s5sc" + n, [128, 128], F32) for n in "Br Bi".split()]
                    Yb2 = [Tile(S, cs, "s5Yb", [128, 8, NB], BF16, nb=8) for _ in range(2)]
                    TG1 = Tile(S, cs, "s5TG", [128, NB], F32)
                    TG2 = [TG1, TG1]
                    for t_ in HPr2 + HPi2:
                        S.op("pool", lambda e, t_=t_: e.memset(t_[:], 0.0), writes=[t_.b[0]])
                    allxb = lambda kc: [XB.b[kc * NST + st] for st in range(NST)]
                    rrb = {"i": 0}

                    def bankT():
                        return S.banks[6]

                    def ctable(Pr, Pi, s_, Are, Aim, OUTr, OUTi, kc, neg_im):
                        pr = Pr.v([[32, 8], [1, 4], [0, 16]], off=s_ * 32 + 4 * kc)
                        pi = Pi.v([[32, 8], [1, 4], [0, 16]], off=s_ * 32 + 4 * kc)
                        are = Are.v([[0, 8], [16, 4], [1, 16]], off=4 * kc * 16)
                        aim = Aim.v([[0, 8], [16, 4], [1, 16]], off=4 * kc * 16)
                        o_r = OUTr.v([[16, 8], [128, 4], [1, 16]])
                        o_i = OUTi.v([[16, 8], [128, 4], [1, 16]])
                        ta = TA.v([[64, 8], [16, 4], [1, 16]])
                        rd = [Pr.b[0], Pi.b[0], Are.b[0], Aim.b[0]]
                        tt("pool", o_r, pr, are, ALU.mult, rd, [OUTr.b[0]])
                        tt("pool", ta, pi, aim, ALU.mult, rd, [TA.b[0]])
                        tt("pool", o_r, o_r, ta, ALU.subtract, [OUTr.b[0], TA.b[0]], [OUTr.b[0]])
                        tt("pool", o_i, pr, aim, ALU.mult, rd, [OUTi.b[0]])
                        tt("pool", ta, pi, are, ALU.mult, rd, [TA.b[0]])
                        tt("pool", o_i, o_i, ta, ALU.add, [OUTi.b[0], TA.b[0]], [OUTi.b[0]])
                        if neg_im:
                            fl = OUTi[:].rearrange("p a b -> p (a b)")
                            act(fl, fl, AF.Identity, [OUTi.b[0]], [OUTi.b[0]], scale=-1.0)

                    def tables(kc):
                        cp = kc % 2
                        TW, PRm, PNm, QRT, QIT, H0c = TW2[cp], PRm2[cp], PNm2[cp], QRT2[cp], QIT2[cp], H0c2[cp]
                        ctable(PBr, PBi, 7, BBR, BBI, UR, UI, kc, False)
                        ctable(PAr, PAi, 7, CR, CI, VR, VIN, kc, True)
                        for h in range(2):
                            pt = bankT()
                            hs = slice(64 * h, 64 * h + 64)
                            for pl in range(4):
                                S.op("pe", lambda e, pl=pl, pt=pt, hs=hs: e.matmul(pt[:, pl * 128:(pl + 1) * 128], lhsT=UR[hs, pl, :], rhs=VR[hs, pl, :],
                                                                                 start=True, stop=False),
                                     reads=[UR.b[0], VR.b[0]], writes=[pt.b[0]])
                                S.op("pe", lambda e, pl=pl, pt=pt, hs=hs: e.matmul(pt[:, pl * 128:(pl + 1) * 128], lhsT=UI[hs, pl, :], rhs=VIN[hs, pl, :],
                                                                                 start=False, stop=True),
                                     reads=[UI.b[0], VIN.b[0]], writes=[pt.b[0]])
                            tt("dve", TW.v([[256, 4], [1, 128]], off=h * 128), pt.v([[128, 4], [1, 128]]),
                               CN.v([[0, 4], [1, 128]], off=C_TM), ALU.mult, [pt.b[0]] + cb, [TW.b[0]])
                        ctable(PAr, PAi, 8, CR, CI, UR, UI, kc, True)
                        for h in range(2):
                            rm = CN[:, C_RM + h:C_RM + h + 1]
                            act(PRm[h][:], UR[:], AF.Identity, [UR.b[0]] + cb, [PRm[h].b[0]], scale=rm)
                            act(PNm[h][:], UI[:], AF.Identity, [UI.b[0]] + cb, [PNm[h].b[0]], scale=rm)
                        ctable(PBr, PBi, 0, BBR, BBI, VR, VIN, kc, False)
                        QMR, QMI = QMR2[cp], QMI2[cp]
                        mur_b = MUr.v([[1, 4], [0, 128]], off=4 * kc)
                        mui_b = MUi.v([[1, 4], [0, 128]], off=4 * kc)
                        ta3 = TA.v([[128, 4], [1, 128]])
                        mbb = [MUr.b[0], MUi.b[0]]
                        tt("pool", UR[:], VR[:], mur_b, ALU.mult, [VR.b[0]] + mbb, [UR.b[0]])
                        tt("pool", ta3, VIN[:], mui_b, ALU.mult, [VIN.b[0]] + mbb, [TA.b[0]])
                        tt("pool", UR[:], UR[:], ta3, ALU.subtract, [UR.b[0], TA.b[0]], [UR.b[0]])
                        tt("pool", UI[:], VR[:], mui_b, ALU.mult, [VR.b[0]] + mbb, [UI.b[0]])
                        tt("pool", ta3, VIN[:], mur_b, ALU.mult, [VIN.b[0]] + mbb, [TA.b[0]])
                        tt("pool", UI[:], UI[:], ta3, ALU.add, [UI.b[0], TA.b[0]], [UI.b[0]])
                        for src, dst in ((VR, QRT), (VIN, QIT), (UR, QMR), (UI, QMI)):
                            pq = bankT()
                            for pl in range(4):
                                S.op("pe", lambda e, pl=pl, src=src, pq=pq: e.transpose(out=pq[:, pl * 128:(pl + 1) * 128], in_=src[:, pl, :],
                                                                                      identity=CN[:, C_ID:C_ID + 128]),
                                     reads=[src.b[0]] + cb, writes=[pq.b[0]])
                            act(dst[:].rearrange("p a b -> p (a b)"), pq[:, 0:512], AF.Identity, [pq.b[0]], [dst.b[0]])
                        for ri in range(2):
                            S.dma("sp", H0c[:, ri, :], s5h0[j, ri, :, kc * 64:(kc + 1) * 64], ldchan(), writes=[H0c.b[0]])

                    def pair_a1(kc, pl):
                        cp = kc % 2
                        p = 4 * kc + pl
                        pp = p % 2
                        QRT, QIT = QRT2[cp], QIT2[cp]
                        UB = UB3[p % 3]
                        for h in range(2):
                            pu = S.banks[0]
                            for i in range(8):
                                eqt, pb0, npp = (EQ, 32 * pl, 32) if pl < 3 else (EQz, 64, 64)
                                S.op("pe", lambda e, i=i, h=h, pu=pu, eqt=eqt, pb0=pb0, npp=npp: e.matmul(
                                    pu[:, 0:NB], lhsT=eqt.v([[1, 128]], off=(h * 8 + i) * 128, p0=pb0, np_=npp),
                                    rhs=XB.v([[8, NB]], off=kc * NT + i, p0=pb0, np_=npp), start=(i == 0), stop=(i == 7)),
                                    reads=[eqt.b[0]] + allxb(kc), writes=[pu.b[0]])
                            act(UB[h][:], pu[:, 0:NB], AF.Identity, [pu.b[0]], [UB[h].b[0]])
                        pdr, pdi = S.banks[1 + 2 * pp], S.banks[2 + 2 * pp]
                        for qt, qm, pd in ((QRT, QMR2[cp], pdr), (QIT, QMI2[cp], pdi)):
                            for h in range(2):
                                hs = slice(64 * h, 64 * h + 64)
                                ev_ = UB[h].v([[2, 128]], off=0)
                                od_ = UB[h].v([[2, 128]], off=1)
                                rd = [qt.b[0], qm.b[0], UB[h].b[0]]
                                S.op("pe", lambda e, qm=qm, pd=pd, hs=hs, ev_=ev_: e.matmul(pd[hs, 0:128], lhsT=qm[:, pl, hs], rhs=ev_,
                                                                                       start=True, stop=False), reads=rd, writes=[pd.b[0]])
                                S.op("pe", lambda e, qt=qt, pd=pd, hs=hs, od_=od_: e.matmul(pd[hs, 0:128], lhsT=qt[:, pl, hs], rhs=od_,
                                                                                       start=False, stop=True), reads=rd, writes=[pd.b[0]])
                                S.op("pe", lambda e, qt=qt, pd=pd, hs=hs, ev_=ev_: e.matmul(pd[hs, 128:256], lhsT=qt[:, pl, hs], rhs=ev_,
                                                                                       start=True, stop=True), reads=rd, writes=[pd.b[0]])
                                S.op("pe", lambda e, qt=qt, pd=pd, hs=hs, h=h: e.matmul(pd[hs, 256:NB], lhsT=qt[:, pl, hs], rhs=UB[h][:, 256:NB],
                                                                                   start=True, stop=True), reads=rd, writes=[pd.b[0]])

                    def pair_a2(kc, pl):
                        cp = kc % 2
                        p = 4 * kc + pl
                        pp = p % 2
                        H0c, OSc = H0c2[cp], OSc2[cp]
                        HPr, HPi = HPr2[pp], HPi2[pp]
                        pdr, pdi = S.banks[1 + 2 * pp], S.banks[2 + 2 * pp]
                        act(Ar[:], pdr[:, 0:NB], AF.Identity, [pdr.b[0]], [Ar.b[0]])
                        act(Ai[:], pdi[:, 0:NB], AF.Identity, [pdi.b[0]], [Ai.b[0]])
                        h0r, h0i = H0c[:, 0, pl * 16:(pl + 1) * 16], H0c[:, 1, pl * 16:(pl + 1) * 16]
                        mur = lambda k: MUr[:, k, p:p + 1]
                        mui = lambda k: MUi[:, k, p:p + 1]
                        mun = lambda k: MUn[:, k, p:p + 1]
                        mb = [MUr.b[0], MUi.b[0], MUn.b[0]]
                        sr, si = Ar[:, 256:NB], Ai[:, 256:NB]
                        stt(sr, h0r, mur(0), sr, ALU.mult, ALU.add, [H0c.b[0], Ar.b[0]] + mb, [Ar.b[0]])
                        stt(sr, h0i, mun(0), sr, ALU.mult, ALU.add, [H0c.b[0], Ar.b[0]] + mb, [Ar.b[0]])
                        stt(si, h0i, mur(0), si, ALU.mult, ALU.add, [H0c.b[0], Ai.b[0]] + mb, [Ai.b[0]])
                        stt(si, h0r, mui(0), si, ALU.mult, ALU.add, [H0c.b[0], Ai.b[0]] + mb, [Ai.b[0]])
                        src_r, src_i, dst_r, dst_i = Ar, Ai, Br, Bi
                        for k in range(7):
                            s_ = 1 << k
                            lo = 0 if k == 0 else s_ // 2
                            rs = [src_r.b[0], src_i.b[0]] + mb
                            stt(dst_r[:, s_:128], src_r[:, 0:128 - s_], mur(k + 1), src_r[:, s_:128], ALU.mult, ALU.add, rs, [dst_r.b[0]])
                            stt(dst_r[:, s_:128], src_i[:, 0:128 - s_], mun(k + 1), dst_r[:, s_:128], ALU.mult, ALU.add, rs + [dst_r.b[0]], [dst_r.b[0]])
                            stt(dst_i[:, s_:128], src_i[:, 0:128 - s_], mur(k + 1), src_i[:, s_:128], ALU.mult, ALU.add, rs, [dst_i.b[0]])
                            stt(dst_i[:, s_:128], src_r[:, 0:128 - s_], mui(k + 1), dst_i[:, s_:128], ALU.mult, ALU.add, rs + [dst_i.b[0]], [dst_i.b[0]])
                            S.op("dve", lambda e, lo=lo, s_=s_, a=src_r, b=dst_r: e.tensor_copy(out=b[:, lo:s_], in_=a[:, lo:s_]),
                                 reads=[src_r.b[0]], writes=[dst_r.b[0]])
                            S.op("dve", lambda e, lo=lo, s_=s_, a=src_i, b=dst_i: e.tensor_copy(out=b[:, lo:s_], in_=a[:, lo:s_]),
                                 reads=[src_i.b[0]], writes=[dst_i.b[0]])
                            src_r, src_i, dst_r, dst_i = dst_r, dst_i, src_r, src_i
                        er, ei = Ar[:, 129:256], Ai[:, 129:256]
                        bb_ = [Br.b[0], Bi.b[0]] + mb
                        stt(er, Br[:, 0:127], mur(0), er, ALU.mult, ALU.add, bb_ + [Ar.b[0]], [Ar.b[0]])
                        stt(er, Bi[:, 0:127], mun(0), er, ALU.mult, ALU.add, bb_ + [Ar.b[0]], [Ar.b[0]])
                        stt(ei, Bi[:, 0:127], mur(0), ei, ALU.mult, ALU.add, bb_ + [Ai.b[0]], [Ai.b[0]])
                        stt(ei, Br[:, 0:127], mui(0), ei, ALU.mult, ALU.add, bb_ + [Ai.b[0]], [Ai.b[0]])
                        act(HPr.v([[2, 128]], off=1), Ar[:, 128:256], AF.Identity, [Ar.b[0]], [HPr.b[0]])
                        act(HPi.v([[2, 128]], off=1), Ai[:, 128:256], AF.Identity, [Ai.b[0]], [HPi.b[0]])
                        act(HPr.v([[2, 127]], off=2), Br[:, 0:127], AF.Identity, [Br.b[0]], [HPr.b[0]])
                        act(HPi.v([[2, 127]], off=2), Bi[:, 0:127], AF.Identity, [Bi.b[0]], [HPi.b[0]])
                        act(HPr[:, 256:NB], h0r, AF.Identity, [H0c.b[0]], [HPr.b[0]])
                        act(HPi[:, 256:NB], h0i, AF.Identity, [H0c.b[0]], [HPi.b[0]])
                        S.op("pool", lambda e: e.tensor_copy(out=OP[:, 0, p:p + 1], in_=Br[:, 127:128]), reads=[Br.b[0]], writes=[OP.b[0]])
                        S.op("pool", lambda e: e.tensor_copy(out=OP[:, 1, p:p + 1], in_=Bi[:, 127:128]), reads=[Bi.b[0]], writes=[OP.b[0]])
                        S.op("pool", lambda e: e.tensor_copy(out=OSc[:, 0, pl * 16:(pl + 1) * 16], in_=Ar[:, 256:NB]), reads=[Ar.b[0]], writes=[OSc.b[0]])
                        S.op("pool", lambda e: e.tensor_copy(out=OSc[:, 1, pl * 16:(pl + 1) * 16], in_=Ai[:, 256:NB]), reads=[Ai.b[0]], writes=[OSc.b[0]])

                    def pair_b(kc, pl):
                        cp = kc % 2
                        p = 4 * kc + pl
                        pp = p % 2
                        TW, PRm, PNm = TW2[cp], PRm2[cp], PNm2[cp]
                        UB, HPr, HPi = UB3[p % 3], HPr2[pp], HPi2[pp]
                        Yb = Yb2[cp]
                        for h in range(2):
                            gm = 2 * pl + h
                            g = 8 * kc + gm
                            py = S.banks[5]
                            TG = TG2[h]
                            S.op("pe", lambda e, gm=gm, h=h, py=py: e.matmul(py[:, 0:NB], lhsT=TW[:, gm, :], rhs=UB[h][:], start=True, stop=False),
                                 reads=[TW.b[0], UB[h].b[0]], writes=[py.b[0]])
                            S.op("pe", lambda e, h=h, py=py: e.matmul(py[:, 0:NB], lhsT=PRm[h][:, pl, :], rhs=HPr[:], start=False, stop=False),
                                 reads=[PRm[h].b[0], HPr.b[0]], writes=[py.b[0]])
                            S.op("pe", lambda e, h=h, py=py: e.matmul(py[:, 0:NB], lhsT=PNm[h][:, pl, :], rhs=HPi[:], start=False, stop=True),
                                 reads=[PNm[h].b[0], HPi.b[0]], writes=[py.b[0]])
                            stt(TG[:], UB[h][:], DB[:, g:g + 1], py[:, 0:NB], ALU.mult, ALU.add, [UB[h].b[0], DB.b[0], py.b[0]], [TG.b[0]])
                            act(Yb[:, gm, :], TG[:], AF.Gelu_apprx_tanh, [TG.b[0]], [Yb.b[gm]])

                    def unblock(kc):
                        OSc = OSc2[kc % 2]
                        Yb = Yb2[kc % 2]
                        for ri in range(2):
                            S.dma("sp", s5s[j, ri, :, kc * 64:(kc + 1) * 64], OSc[:, ri, :], stchan(), reads=[OSc.b[0]])
                        for i in range(8):
                            pb = S.banks[7]
                            q2, ipar = i // 2, i % 2
                            for gm in range(8):
                                ept, pb0, npp = (EPQ, 32 * q2, 32) if q2 < 3 else (EPQz, 64, 64)
                                S.op("pe", lambda e, gm=gm, pb=pb, ept=ept, pb0=pb0, npp=npp, ipar=ipar, Yb=Yb: e.matmul(
                                    pb[:, 0:NB], lhsT=ept.v([[1, 128]], off=(ipar * 8 + gm) * 128, p0=pb0, np_=npp),
                                    rhs=Yb.v([[1, NB]], off=gm * NB, p0=pb0, np_=npp), start=(gm == 0), stop=(gm == 7)),
                                    reads=[ept.b[0], Yb.b[gm]], writes=[pb.b[0]])
                            act(XB.v([[8, NB]], off=kc * NT + i), pb[:, 0:NB], AF.Identity, [pb.b[0]], allxb(kc))

                    def split(lst, k):
                        n_ = len(lst)
                        return [lst[(i * n_) // k:((i + 1) * n_) // k] for i in range(k)]

                    tables(0)
                    pa1 = lambda p_: S.record(lambda: pair_a1(p_ // 4, p_ % 4)) if p_ < 32 else []
                    pa2 = lambda p_: S.record(lambda: pair_a2(p_ // 4, p_ % 4)) if p_ < 32 else []
                    S.replay(pa1(0))
                    tl0 = split(S.record(lambda: tables(1)), 2)
                    S.replay(pa1(1), pa2(0), tl0[0])
                    extra = {0: [tl0[1]]}
                    for r in range(32):
                        kc, pl = r // 4, r % 4
                        if pl == 0 and 1 <= kc and kc + 1 < 8:
                            tl = split(S.record(lambda: tables(kc + 1)), 2)
                            extra.setdefault(r, []).append(tl[0])
                            extra.setdefault(r + 1, []).append(tl[1])
                        if pl == 0 and kc >= 1:
                            ul = split(S.record(lambda: unblock(kc - 1)), 3)
                            for q_ in range(3):
                                extra.setdefault(r + q_, []).append(ul[q_])
                        lb_ = S.record(lambda: pair_b(kc, pl))
                        ex_ = extra.get(r, [])
                        S.replay(pa1(r + 2), pa2(r + 1), lb_, *ex_, weights=[0.4, 1.0, 1.0] + [1.0] * len(ex_))
                    unblock(7)
                    for ri in range(2):
                        S.dma("sp", s5p[j, ri], OP[:, ri, :], stchan(), reads=[OP.b[0]])
                    S.barrier()
                MX = [Tile(S, ph, "s5MX%d" % i, [128, 512], F32) for i in range(2)]
                SGt = [Tile(S, ph, "s5SG%d" % i, [128, 512], F32) for i in range(2)]
                LSQ = Tile(S, ph, "s5LSQ", [128, KC, 512], BF16)
                LM2 = Tile(S, ph, "s5LM2", [128, 512], F32)
                LRS = Tile(S, ph, "s5LRS", [128, 512], F32)
                kk_ = {"i": 0}

                def glu_st(st):
                    t0, n = STS[st]
                    for nn in range(8):
                        w = W.get(("wglu", l, st, nn))
                        pv, pg = S.banks[2 * (nn % 2)], S.banks[2 * (nn % 2) + 1]
                        for vg, bk in ((0, pv), (1, pg)):
                            for kc in range(KC):
                                S.op("pe", lambda e, vg=vg, kc=kc, bk=bk, w=w: e.matmul(
                                    bk[:, 0:n], lhsT=w[:, (vg * 8 + kc) * 128:(vg * 8 + kc + 1) * 128], rhs=XB[:, kc, t0:t0 + n],
                                    start=(kc == 0), stop=(kc == KC - 1)),
                                    reads=[w.b[0]] + xbb(st, [kc]), writes=[bk.b[0]])
                        sg, mx = SGt[kk_["i"] % 2], MX[kk_["i"] % 2]
                        kk_["i"] += 1
                        act(sg[:, 0:n], pg[:, 0:n], AF.Sigmoid, [pg.b[0]], [sg.b[0]])
                        tt("dve", mx[:, 0:n], pv[:, 0:n], sg[:, 0:n], ALU.mult, [pv.b[0], sg.b[0]], [mx.b[0]])
                        xs = XF[:, nn, t0:t0 + n]
                        stt(xs, xs, ALPHA, mx[:, 0:n], ALU.mult, ALU.add, xfb(st, [nn]) + [mx.b[0]], xfb(st, [nn]))

                for r in range(NST + 1):
                    l1 = S.record(lambda: glu_st(r)) if r < NST else []
                    l2 = S.record(lambda: layer_norm((LSQ, None, LM2, LRS), r - 1, l * V_LAYER + 0, l * V_LAYER + 8,
                                                     banks=(S.banks[4], S.banks[5]))) if r >= 1 else []
                    S.replay(l1, l2)
                S.barrier()

        for l in range(n_layers):
            vb = l * V_LAYER
            if l % 2 == 0:
                hgrn_phase(l)
            else:
                s5_phase(l)
            ffn_phase(l, last=(l == n_layers - 1))
        for ev in S.pending.values():
            S._wait("sp", ev)
        S.barrier()
        build_program.stats = (S.nop, dict(S.cnt))
    return nc


def _chunkT(W, kcs=8):
    K, N = W.shape
    return W.reshape(K // 128, 128, N).transpose(1, 0, 2)


def _make_consts():
    c = np.zeros((128, NCONST), np.float32)
    p = np.arange(128)
    c[:, C_ID:C_ID + 128] = np.eye(128, dtype=np.float32)
    c[:, C_CAUS:C_CAUS + 128] = (p[:, None] <= p[None, :])
    c[:, C_BD:C_BD + 128] = (p[:, None] <= p[None, :]) & (p[:, None] // 8 == p[None, :] // 8)
    c[:, C_SEG:C_SEG + 16] = (p[:, None] // 8 == np.arange(16)[None, :])
    c[:, C_R512:C_R512 + 512] = (np.arange(512) % 128 != 0)[None, :]
    c[:, C_R8:C_R8 + 128] = (np.arange(128) % 8 != 0)[None, :]
    c[:, C_TM:C_TM + 128] = (p[None, :] // 16 >= p[:, None] // 16)
    c[:, C_RM] = p < 64
    c[:, C_RM + 1] = p >= 64
    tau = np.arange(16, dtype=np.float32) - 7.0
    c[:, C_TAU:C_TAU + 512] = np.repeat(tau, 32)[None, :]
    return c


def _make_emat():
    EQ = np.zeros((128, 2, 8, 8, 16), np.float32)
    EP = np.zeros((128, 2, 8, 128), np.float32)
    for p in range(128):
        r = p % 32
        par, c = r // 16, r % 16
        for i in range(8):
            EQ[p, par, i, i, c] = 1.0
        for gm in range(8):
            EP[p, par, gm, 16 * gm + c] = 1.0
    EQ = EQ.reshape(128, 2048)
    EP = EP.reshape(128, 2048)
    EQz, EPz = EQ.copy(), EP.copy()
    EQz[64:96] = 0
    EPz[64:96] = 0
    return np.stack([EQ, EP, EQz, EPz])


def _vec(v):
    return np.ascontiguousarray(v.reshape(8, 128).T)


def prep_shared(inp):
    f = np.float32
    sh = {}
    sh["consts"] = _make_consts()
    sh["emat"] = _make_emat()
    vecs = np.zeros((128, NVEC), f)
    for l in range(4):
        for k, name in enumerate(["ln_mix_w", "ln_mix_b", "ln_ffn_w", "ln_ffn_b", "ple_norm_w"]):
            vecs[:, l * V_LAYER + 8 * k: l * V_LAYER + 8 * k + 8] = _vec(inp[name][l])
    for j in range(2):
        vecs[:, V_LB + 8 * j: V_LB + 8 * j + 8] = _vec(inp["hg_lower_bounds"][j])
        vecs[:, V_GN + j] = inp["hg_gnorm_w"][j]
    sh["vecs"] = vecs
    whg = np.zeros((2, 8, 2, 128, 2048), f)
    who = np.zeros((2, 8, 128, 1024), f)
    for j in range(2):
        Wt = _chunkT(inp["hg_w_in"][j])
        for h in range(8):
            for ab in range(2):
                for sub in range(2):
                    part = ab * 2 + sub
                    blk = Wt[:, :, part * 1024 + h * 128: part * 1024 + (h + 1) * 128]
                    whg[j, h, ab, :, sub * 1024:(sub + 1) * 1024] = blk.reshape(128, 1024)
        Wo = _chunkT(inp["hg_w_out"][j])
        for n in range(8):
            who[j, n] = Wo[:, :, n * 128:(n + 1) * 128].reshape(128, 1024)
    sh["whg"], sh["who"] = whg, who
    wglu = np.zeros((2, 8, 128, 2048), f)
    for j in range(2):
        Wt = _chunkT(inp["s5_w_glu"][j])
        for n in range(8):
            for vg in range(2):
                wglu[j, n, :, vg * 1024:(vg + 1) * 1024] = Wt[:, :, vg * 1024 + n * 128: vg * 1024 + (n + 1) * 128].reshape(128, 1024)
    sh["wglu"] = wglu
    wgu = np.zeros((4, 22, 128, 2048), f)
    wdn = np.zeros((4, 3, 8, 128, 1024), f)
    wple = np.zeros((4, 8, 128, 1280), f)
    for l in range(4):
        Wt = _chunkT(inp["ffn_w_gate_up"][l])
        for c in range(22):
            for gu in range(2):
                wgu[l, c, :, gu * 1024:(gu + 1) * 1024] = Wt[:, :, gu * 2816 + c * 128: gu * 2816 + (c + 1) * 128].reshape(128, 1024)
        Wd = _chunkT(inp["ffn_w_down"][l])
        for ps, (c0, c1) in enumerate(FPASS):
            for n in range(8):
                wdn[l, ps, n, :, 0:(c1 - c0) * 128] = Wd[:, c0:c1, n * 128:(n + 1) * 128].reshape(128, -1)
        Wg = _chunkT(inp["ple_w_gate"][l])
        Wp = _chunkT(inp["ple_w_proj"][l])
        for n in range(8):
            wple[l, n, :, 0:1024] = Wg[:, :, n * 128:(n + 1) * 128].reshape(128, 1024)
            wple[l, n, :, 1024:1280] = Wp[:, :, n * 128:(n + 1) * 128].reshape(128, 256)
    sh["wgu"], sh["wdn"], sh["wple"] = wgu, wdn, wple
    s5a = np.zeros((2, 128, 96), f)
    s5b = np.zeros((2, 2, 128, 512), f)
    s5c = np.zeros((2, 2, 128, 512), f)
    s5d = np.zeros((2, 128, 64), f)

    def gp(a):
        sh_ = a.shape
        a = a.reshape((32, 2) + sh_[1:])
        a = np.moveaxis(a, 0, 2)
        return a.reshape((128, 32) + sh_[2:])
    for j in range(2):
        s5a[j, :, 0:32] = gp(inp["s5_a_re"][j])
        s5a[j, :, 32:64] = gp(inp["s5_a_im"][j])
        s5a[j, :, 64:96] = gp(np.repeat(inp["s5_log_step"][j][:, None], 64, axis=1))
        s5b[j, 0] = gp(inp["s5_b_re"][j]).reshape(128, 512)
        s5b[j, 1] = gp(inp["s5_b_im"][j]).reshape(128, 512)
        s5c[j, 0] = gp(inp["s5_c_re"][j].transpose(0, 2, 1)).reshape(128, 512)
        s5c[j, 1] = gp(inp["s5_c_im"][j].transpose(0, 2, 1)).reshape(128, 512)
        dd = inp["s5_d"][j].reshape(64, 16)
        s5d[j] = np.tile(dd.T, (8, 1))
    sh["s5a"], sh["s5b"], sh["s5c"], sh["s5d"] = s5a, s5b, s5c, s5d
    return sh


def prep_core(inp, c):
    m = {}
    xs = inp["x_sample"][16 * c:16 * c + 16].reshape(128, D)
    m["xT"] = np.ascontiguousarray(np.concatenate([inp["x_prompt"][c].T, xs.T], axis=1))
    pp = inp["p_prompt"][:, c].transpose(0, 2, 1)
    ps = inp["p_sample"][:, 16 * c:16 * c + 16].reshape(4, 128, 256).transpose(0, 2, 1)
    m["pT"] = np.ascontiguousarray(np.concatenate([pp, ps], axis=2))
    m["hg0"] = np.ascontiguousarray(inp["state_hgrn"][:, 16 * c:16 * c + 16])
    h0 = np.stack([inp["state_s5_re"][:, 16 * c:16 * c + 16], inp["state_s5_im"][:, 16 * c:16 * c + 16]], axis=1)
    h0 = h0.reshape(2, 2, 16, 32, 2, 64).transpose(0, 1, 4, 5, 3, 2)
    m["s5h0"] = np.ascontiguousarray(h0.reshape(2, 2, 128, 512))
    return m


def assemble(results):
    n = len(results)
    f = np.float32
    y_p = np.zeros((n, NPR, D), f)
    y_s = np.zeros((n * 16, 8, D), f)
    hg_p = np.zeros((2, n, 8, 128, 128), f)
    re_p = np.zeros((2, n, 64, 64), f)
    im_p = np.zeros((2, n, 64, 64), f)
    hg_s = np.zeros((2, n * 16, 8, 128, 128), f)
    re_s = np.zeros((2, n * 16, 64, 64), f)
    im_s = np.zeros((2, n * 16, 64, 64), f)
    for c, r in enumerate(results):
        yT = r["yT"]
        y_p[c] = yT[:, :NPR].T
        y_s[16 * c:16 * c + 16] = yT[:, NPR:].T.reshape(16, 8, D)
        hg_p[:, c] = r["hgp"].transpose(0, 2, 1, 3)
        hg_s[:, 16 * c:16 * c + 16] = r["hgs"]
        sp = r["s5p"].reshape(2, 2, 2, 64, 32)
        sp = sp.transpose(0, 1, 4, 2, 3).reshape(2, 2, 64, 64)
        re_p[:, c], im_p[:, c] = sp[:, 0], sp[:, 1]
        ss = r["s5s"].reshape(2, 2, 2, 64, 32, 16)
        ss = ss.transpose(0, 1, 5, 4, 2, 3).reshape(2, 2, 16, 64, 64)
        re_s[:, 16 * c:16 * c + 16], im_s[:, 16 * c:16 * c + 16] = ss[:, 0], ss[:, 1]
    return (y_p, y_s, hg_p, re_p, im_p, hg_s, re_s, im_s)


_NC_CACHE = {}


def kernel(**inputs):
    inp = {k: np.asarray(v) for k, v in inputs.items()}
    sh = prep_shared(inp)
    in_maps = []
    for c in range(8):
        m = dict(sh)
        m.update(prep_core(inp, c))
        in_maps.append(m)
    if "nc" not in _NC_CACHE:
        _NC_CACHE["nc"] = build_program(4)
    res = run_bass_kernel_spmd(_NC_CACHE["nc"], in_maps, core_ids=list(range(8)))
    return assemble(res.results)
```

```python
import math
from contextlib import ExitStack

import numpy as np
import concourse.bass as bass
import concourse.mybir as mybir
from concourse.ap import AP
from concourse.bass_utils import run_bass_kernel_spmd

F32 = mybir.dt.float32
BF16 = mybir.dt.bfloat16
ALU = mybir.AluOpType
AF = mybir.ActivationFunctionType

D = 1024
KC = 8
NT = 2176
NPR = 2048
NSM = 128
NB = 272
DEPTH = 4
STS = [(0, 512), (512, 512), (1024, 512), (1536, 512), (2048, 128)]
NST = len(STS)
FPASS = [(0, 8), (8, 15), (15, 22)]
ALPHA = float((2 * DEPTH) ** 0.25)
LN_EPS = 1e-5
RMS_EPS = 1e-6
QK = float(128 ** -0.5)
TWO_PI = 2.0 * math.pi
MAGIC = 12582912.0

C_ID, C_CAUS, C_BD, C_SEG, C_TM, C_RM, C_R8 = 0, 128, 256, 384, 400, 528, 530
NCP = 658
C_R512, C_TAU = 658, 1170
NCONST = 1682
V_LAYER = 40
V_LB = 160
V_GN = 176
NVEC = 178


class Buf:
    __slots__ = ("name", "w", "r")

    def __init__(self, name):
        self.name = name
        self.w = None
        self.r = []


class Chan:
    _uid = [0]

    def __init__(self, S, name):
        Chan._uid[0] += 1
        self.sem = S.es.enter_context(S.nc.semaphore("c_%s_%d" % (name, Chan._uid[0])))
        self.cnt = 0


class Tile:
    _uid = [0]

    def __init__(self, S, es, name, shape, dtype, nb=1, psum=False):
        Tile._uid[0] += 1
        name = "%s_%d" % (name, Tile._uid[0])
        if psum:
            self.h = es.enter_context(S.nc.psum_tensor(name, shape, dtype))
        else:
            self.h = es.enter_context(S.nc.sbuf_tensor(name, shape, dtype))
        self.b = [Buf(name + str(i)) for i in range(nb)]
        self.shape = shape
        a = self.h[:]
        self.tensor = a.tensor
        self.pstep = a.ap[0][0]

    def __getitem__(self, k):
        return self.h[k]

    def v(self, dims, off=0, p0=0, np_=128):
        return AP(self.tensor, p0 * self.pstep + off, [[self.pstep, np_]] + [list(d) for d in dims])


class HView:
    def __init__(self, tile, t0, bufs):
        self.t, self.t0, self.b = tile, t0, bufs

    def __getitem__(self, k):
        p, r, c = k
        return self.t[p, r, self.t0 + c.start:self.t0 + c.stop]


class Sched:
    def __init__(self, nc, es):
        self.nc = nc
        self.es = es
        self.eng = {"pe": nc.tensor, "act": nc.scalar, "dve": nc.vector, "pool": nc.gpsimd, "sp": nc.sync}
        self.esem = {}
        self.cnt = {}
        for k in self.eng:
            self.esem[k] = es.enter_context(nc.semaphore("s_" + k))
            self.cnt[k] = 0
        self.seen = {k: {} for k in self.eng}
        self.pending = {}
        self.banks = []
        self.bank_i = 0
        self.nop = 0

    def _wait(self, e, ev):
        if ev is None:
            return
        sem, val = ev
        d = self.seen[e]
        key = id(sem)
        if d.get(key, 0) >= val:
            return
        d[key] = val
        self.eng[e].wait_ge(sem, val)

    def _deps(self, e, reads, writes, same_ok=False):
        own = self.esem[e] if same_ok else None
        for b in reads:
            ev = b.w
            if ev is not None and ev[0] is not own:
                self._wait(e, ev)
        for b in writes:
            ev = b.w
            if ev is not None and ev[0] is not own:
                self._wait(e, ev)
            for ev in b.r:
                if ev[0] is not own:
                    self._wait(e, ev)

    def _record(self, ev, reads, writes):
        for b in reads:
            b.r.append(ev)
            if len(b.r) > 24:
                d = {}
                for s, v in b.r:
                    if d.get(id(s), (None, 0))[1] < v:
                        d[id(s)] = (s, v)
                b.r = list(d.values())
        for b in writes:
            b.w = ev
            b.r = []

    rec = None

    def op(self, e, fn, reads=(), writes=()):
        if self.rec is not None:
            self.rec.append(("op", e, fn, list(reads), list(writes)))
            return None
        self._deps(e, reads, writes, same_ok=(e == "pe"))
        ins = fn(self.eng[e])
        self.cnt[e] += 1
        ins.then_inc(self.esem[e], 1)
        ev = (self.esem[e], self.cnt[e])
        self._record(ev, reads, writes)
        self.nop += 1
        return ev

    def dma(self, q, out, in_, chan, reads=(), writes=()):
        if self.rec is not None:
            self.rec.append(("dma", q, out, in_, chan, list(reads), list(writes)))
            return None
        self._deps(q, reads, writes)
        if chan.cnt > 0:
            self._wait(q, (chan.sem, chan.cnt))
        ins = self.eng[q].dma_start(out=out, in_=in_)
        chan.cnt += 16
        ins.then_inc(chan.sem, 16)
        ev = (chan.sem, chan.cnt)
        self.pending[id(chan.sem)] = ev
        self._record(ev, reads, writes)
        return ev

    def record(self, fn):
        assert self.rec is None
        self.rec = []
        fn()
        r, self.rec = self.rec, None
        return r

    def replay(self, *lists, weights=None):
        if weights is None:
            weights = [1.0] * len(lists)
        weights = [w for l, w in zip(lists, weights) if l]
        lists = [l for l in lists if l]
        pos = [0] * len(lists)
        tot = sum(len(l) for l in lists)
        for _ in range(tot):
            k = min((i for i in range(len(lists)) if pos[i] < len(lists[i])),
                    key=lambda i: weights[i] * (pos[i] + 0.5) / len(lists[i]))
            it = lists[k][pos[k]]
            pos[k] += 1
            if it[0] == "op":
                self.op(it[1], it[2], it[3], it[4])
            else:
                self.dma(it[1], it[2], it[3], it[4], it[5], it[6])

    def barrier(self):
        for e in self.eng:
            for e2 in self.eng:
                if e2 != e and self.cnt[e2] > 0:
                    self._wait(e, (self.esem[e2], self.cnt[e2]))
            for ev in self.pending.values():
                self._wait(e, ev)

    def bank(self):
        t = self.banks[self.bank_i % len(self.banks)]
        self.bank_i += 1
        return t


class WStream:
    NRING = 4
    LA = 2

    def __init__(self, S, es, plan, width=2048, nring=None, la=None):
        self.S = S
        self.plan = plan
        if nring is not None:
            self.NRING = nring
        if la is not None:
            self.LA = la
        self.ring = [Tile(S, es, "wring%d" % i, [128, width], BF16) for i in range(self.NRING)]
        self.chan = [Chan(S, "wr%d" % i) for i in range(self.NRING)]
        self.issued = 0
        self.slot = {}
        self.next = 0

    def _issue(self, c):
        S = self.S
        tag, src, E, dest = self.plan[c]
        r = self.ring[c % self.NRING]
        S.dma("pool", r[:, 0:E], src, self.chan[c % self.NRING], writes=[r.b[0]])
        self.slot[c] = r

    def get(self, tag):
        c = self.next
        assert self.plan[c][0] == tag, (self.plan[c][0], tag)
        self.next += 1
        while self.issued < len(self.plan) and self.issued <= c + self.LA:
            if self.issued > c and (self.issued - max(c - 1, 0)) + 1 > self.NRING:
                break
            self._issue(self.issued)
            self.issued += 1
        return self.slot.pop(c)


def build_program(n_layers=4):
    nc = bass.Bass("TRN2", target_bir_lowering=False)

    def din(name, shape):
        return nc.dram_tensor(name, shape, F32, kind="ExternalInput").ap()

    def dout(name, shape):
        return nc.dram_tensor(name, shape, F32, kind="ExternalOutput").ap()

    xT = din("xT", [D, NT])
    pT = din("pT", [4, 256, NT])
    hg0 = din("hg0", [2, 16, 8, 128, 128])
    s5h0 = din("s5h0", [2, 2, 128, 32 * 16])
    consts = din("consts", [128, NCONST])
    vecs = din("vecs", [128, NVEC])
    emat = din("emat", [4, 128, 2048])
    whg = din("whg", [2, 8, 2, 128, 2048])
    who = din("who", [2, 8, 128, 1024])
    wglu = din("wglu", [2, 8, 128, 2048])
    wgu = din("wgu", [4, 22, 128, 2048])
    wdn = din("wdn", [4, 3, 8, 128, 1024])
    wple = din("wple", [4, 8, 128, 1280])
    s5a = din("s5a", [2, 128, 96])
    s5b = din("s5b", [2, 2, 128, 512])
    s5c = din("s5c", [2, 2, 128, 512])
    s5d = din("s5d", [2, 128, 64])

    yT = dout("yT", [D, NT])
    hgp = dout("hgp", [2, 128, 8, 128])
    hgs = dout("hgs", [2, 16, 8, 128, 128])
    s5p = dout("s5p", [2, 2, 128, 32])
    s5s = dout("s5s", [2, 2, 128, 32 * 16])

    with ExitStack() as es:
        S = Sched(nc, es)
        S.banks = [Tile(S, es, "bank%d" % i, [128, 512], F32, psum=True) for i in range(8)]

        XF = Tile(S, es, "XF", [128, KC, NT], F32, nb=KC * NST)
        XB = Tile(S, es, "XB", [128, KC, NT], BF16, nb=KC * NST)
        CN = Tile(S, es, "CN", [128, NCP], F32)
        VEC = Tile(S, es, "VEC", [128, NVEC], F32)
        ONESB = Tile(S, es, "ONESB", [128, 128], BF16)
        ONESD = Tile(S, es, "ONESD", [128, 128], BF16)
        LBT = Tile(S, es, "LBT", [128, 2, 4, 8], F32)
        EPS = Tile(S, es, "EPS", [128, 2], F32)

        def xfb(st, kcs=range(KC)):
            return [XF.b[kc * NST + st] for kc in kcs]

        def xbb(st, kcs=range(KC)):
            return [XB.b[kc * NST + st] for kc in kcs]

        plan = []
        for l in range(n_layers):
            j = l // 2
            if l % 2 == 0:
                for st in range(NST):
                    for h in range(8):
                        plan.append((("hgA", l, st, h), whg[j, h, 0], 2048, None))
                        plan.append((("hgB", l, st, h), whg[j, h, 1], 2048, None))
                    for n in range(8):
                        plan.append((("who", l, st, n), who[j, n], 1024, None))
            else:
                for st in range(NST):
                    for n in range(8):
                        plan.append((("wglu", l, st, n), wglu[j, n], 2048, None))
            for ps, (c0, c1) in enumerate(FPASS):
                for c in range(c0, c1):
                    plan.append((("wgu", l, c), wgu[l, c], 2048, None))
                E = (c1 - c0) * 128
                if ps < len(FPASS) - 1:
                    for n in range(8):
                        plan.append((("wdn", l, ps, n), wdn[l, ps, n, :, 0:E], E, None))
                else:
                    for st in range(NST):
                        for n in range(8):
                            plan.append((("wdn", l, ps, st, n), wdn[l, ps, n, :, 0:E], E, None))
        W = WStream(S, es, plan)

        ldc = [Chan(S, "ld%d" % i) for i in range(4)]
        stc = [Chan(S, "st%d" % i) for i in range(4)]
        ctr = {"ld": 0, "st": 0}

        def ldchan():
            ctr["ld"] += 1
            return ldc[ctr["ld"] % 4]

        def stchan():
            ctr["st"] += 1
            return stc[ctr["st"] % 4]

        S.dma("sp", CN[:], consts[:, 0:NCP], ldchan(), writes=[CN.b[0]])
        S.dma("sp", VEC[:], vecs, ldchan(), writes=[VEC.b[0]])
        for st, (t0, n) in enumerate(STS):
            S.dma("sp", XF[:, :, t0:t0 + n], xT[:, t0:t0 + n].rearrange("(k p) t -> p k t", p=128),
                  ldchan(), writes=xfb(st))
        S.op("pool", lambda e: e.memset(ONESB[:], 1.0), writes=[ONESB.b[0]])
        S.op("pool", lambda e: e.memset(ONESD[:], 1.0 / D), writes=[ONESD.b[0]])
        S.op("pool", lambda e: e.memset(EPS[:, 0:1], LN_EPS), writes=[EPS.b[0]])
        S.op("pool", lambda e: e.memset(EPS[:, 1:2], RMS_EPS), writes=[EPS.b[0]])
        for bk in S.banks:
            S.op("dve", lambda e, bk=bk: e.memset(bk[:], 0.0), writes=[bk.b[0]])
        for st, (t0, n) in enumerate(STS):
            S.op("act", lambda e, t0=t0, n=n: e.activation(out=XB[:, :, t0:t0 + n], in_=XF[:, :, t0:t0 + n], func=AF.Identity),
                 reads=xfb(st), writes=xbb(st))
        with ExitStack() as ph:
            tmp = Tile(S, ph, "lbtmp", [128, 4, 8], F32)
            b0 = VEC[:, V_LB:V_LB + 8]
            b1 = VEC[:, V_LB + 8:V_LB + 16]
            vb = [VEC.b[0]]
            tb = [tmp.b[0]]
            S.op("dve", lambda e: e.tensor_tensor(out=tmp[:, 0, :], in0=b0, in1=b1, op=ALU.subtract), reads=vb, writes=tb)
            S.op("act", lambda e: e.activation(out=tmp[:, 1, :], in_=tmp[:, 0, :], func=AF.Sigmoid), reads=tb, writes=tb)
            S.op("act", lambda e: e.activation(out=tmp[:, 2, :], in_=tmp[:, 0, :], func=AF.Sigmoid, scale=-1.0), reads=tb, writes=tb)
            S.op("dve", lambda e: e.tensor_tensor(out=LBT[:, 0, 0, :], in0=tmp[:, 1, :], in1=tmp[:, 1, :], op=ALU.subtract),
                 reads=tb, writes=[LBT.b[0]])
            S.op("dve", lambda e: e.tensor_tensor(out=tmp[:, 3, :], in0=tmp[:, 1, :], in1=tmp[:, 2, :], op=ALU.add), reads=tb, writes=tb)
            S.op("dve", lambda e: e.tensor_tensor(out=LBT[:, 1, 0, :], in0=tmp[:, 3, :], in1=tmp[:, 1, :], op=ALU.subtract),
                 reads=tb, writes=[LBT.b[0]])
            for jj in range(2):
                S.op("dve", lambda e, jj=jj: e.tensor_scalar(out=LBT[:, jj, 1, :], in0=LBT[:, jj, 0, :], scalar1=-1.0, scalar2=1.0,
                                                             op0=ALU.mult, op1=ALU.add), reads=[LBT.b[0]], writes=[LBT.b[0]])
                S.op("dve", lambda e, jj=jj: e.tensor_scalar(out=LBT[:, jj, 3, :], in0=LBT[:, jj, 1, :], scalar1=0.5, scalar2=None,
                                                             op0=ALU.mult, op1=ALU.bypass), reads=[LBT.b[0]], writes=[LBT.b[0]])
                S.op("dve", lambda e, jj=jj: e.tensor_tensor(out=LBT[:, jj, 2, :], in0=LBT[:, jj, 0, :], in1=LBT[:, jj, 3, :], op=ALU.add),
                     reads=[LBT.b[0]], writes=[LBT.b[0]])
            S.barrier()

        def mm_group(bank, n, lhs_list, rhs_list, reads):
            k = len(lhs_list)
            for i in range(k):
                S.op("pe", lambda e, i=i: e.matmul(bank[:, 0:n], lhsT=lhs_list[i], rhs=rhs_list[i],
                                                   start=(i == 0), stop=(i == k - 1)),
                     reads=reads[i], writes=[bank.b[0]])

        def colsum_bcast(bank, n, src_tile_ap_fn, reads_fn, lhs=None):
            lhs = lhs or ONESB
            for kc in range(KC):
                S.op("pe", lambda e, kc=kc: e.matmul(bank[:, 0:n], lhsT=lhs[:], rhs=src_tile_ap_fn(kc),
                                                     start=(kc == 0), stop=(kc == KC - 1)),
                     reads=[lhs.b[0]] + reads_fn(kc), writes=[bank.b[0]])

        def layer_norm(ph_t, st, wcol, bcol, banks=None):
            t0, n = STS[st]
            SQ, MEAN, M2, RSTD = ph_t
            xf3 = XF[:, :, t0:t0 + n]
            xb3 = XB[:, :, t0:t0 + n]
            S.op("act", lambda e: e.activation(out=xb3, in_=xf3, func=AF.Identity), reads=xfb(st), writes=xbb(st))
            S.op("act", lambda e: e.activation(out=SQ[:, :, 0:n], in_=xf3, func=AF.Square), reads=xfb(st), writes=SQ.b)
            p1 = banks[0] if banks else S.bank()
            colsum_bcast(p1, n, lambda kc: XB[:, kc, t0:t0 + n], lambda kc: xbb(st, [kc]), lhs=ONESD)
            p2 = banks[1] if banks else S.bank()
            colsum_bcast(p2, n, lambda kc: SQ[:, kc, 0:n], lambda kc: SQ.b, lhs=ONESD)
            S.op("act", lambda e: e.activation(out=M2[:, 0:n], in_=p1[:, 0:n], func=AF.Square), reads=[p1.b[0]], writes=[M2.b[0]])
            S.op("dve", lambda e: e.tensor_tensor(out=RSTD[:, 0:n], in0=p2[:, 0:n], in1=M2[:, 0:n], op=ALU.subtract),
                 reads=[p2.b[0], M2.b[0]], writes=[RSTD.b[0]])
            S.op("act", lambda e: e.activation(out=RSTD[:, 0:n], in_=RSTD[:, 0:n], func=AF.Ln, bias=EPS[:, 0:1]),
                 reads=[RSTD.b[0], EPS.b[0]], writes=[RSTD.b[0]])
            S.op("act", lambda e: e.activation(out=p2[:, 0:n], in_=RSTD[:, 0:n], func=AF.Exp, scale=-0.5),
                 reads=[RSTD.b[0]], writes=[p2.b[0]])
            mb = p1.v([[0, KC], [1, n]])
            rb = p2.v([[0, KC], [1, n]])
            S.op("dve", lambda e: e.tensor_tensor(out=xf3, in0=xf3, in1=mb, op=ALU.subtract),
                 reads=xfb(st) + [p1.b[0]], writes=xfb(st))
            S.op("dve", lambda e: e.tensor_tensor(out=xf3, in0=xf3, in1=rb, op=ALU.mult),
                 reads=xfb(st) + [p2.b[0]], writes=xfb(st))
            for kc in range(KC):
                S.op("act", lambda e, kc=kc: e.activation(out=XF[:, kc, t0:t0 + n], in_=XF[:, kc, t0:t0 + n], func=AF.Identity,
                                                          scale=VEC[:, wcol + kc:wcol + kc + 1], bias=VEC[:, bcol + kc:bcol + kc + 1]),
                     reads=xfb(st, [kc]) + [VEC.b[0]], writes=xfb(st, [kc]))
            S.op("dve", lambda e: e.tensor_copy(out=xb3, in_=xf3), reads=xfb(st), writes=xbb(st))

        def ln_phase(wcol, bcol):
            with ExitStack() as ph:
                sets = []
                for i in range(2):
                    sets.append((Tile(S, ph, "lnSQ", [128, KC, 512], BF16), Tile(S, ph, "lnMEAN", [128, 512], F32),
                                 Tile(S, ph, "lnM2", [128, 512], F32), Tile(S, ph, "lnRSTD", [128, 512], F32)))
                for st in range(NST):
                    layer_norm(sets[st % 2], st, wcol, bcol)
                S.barrier()

        def ffn_phase(l, last):
            vbl = l * V_LAYER
            vb = vbl + 32
            with ExitStack() as ph:
                H = Tile(S, ph, "ffH", [128, 8, NT], BF16, nb=8 * NST)
                SGt = [Tile(S, ph, "ffSG%d" % i, [128, 512], F32) for i in range(2)]
                plan2 = [(("wple", l, st, n), wple[l, n], 1280, None) for st in range(NST) for n in range(8)]
                W2 = WStream(S, ph, plan2, width=1280, nring=3, la=1)
                M2 = Tile(S, ph, "ffM2", [128, 512], F32)
                RSTD = Tile(S, ph, "ffRSTD", [128, 512], F32)
                PS = Tile(S, ph, "plPS", [128, 2, 512], F32)
                PB = Tile(S, ph, "plPB", [128, 2, 512], BF16)
                E = Tile(S, ph, "plE", [128, KC, 512], F32, nb=KC)
                RS = Tile(S, ph, "plRS", [128, 512], F32)
                k = 0

                def stage2_evac(ps, nn, st, po):
                    t0, n = STS[st]
                    xs = XF[:, nn, t0:t0 + n]
                    if ps == 0:
                        S.op("dve", lambda e: e.scalar_tensor_tensor(out=xs, in0=xs, scalar=ALPHA, in1=po[:, 0:n],
                                                                    op0=ALU.mult, op1=ALU.add),
                             reads=xfb(st, [nn]) + [po.b[0]], writes=xfb(st, [nn]))
                    else:
                        S.op("dve", lambda e: e.tensor_tensor(out=xs, in0=xs, in1=po[:, 0:n], op=ALU.add),
                             reads=xfb(st, [nn]) + [po.b[0]], writes=xfb(st, [nn]))

                def hsq(st):
                    return HView(H, STS[st][0], [H.b[ci * NST + st] for ci in range(8)])

                def ple_st(st):
                    t0, n = STS[st]
                    SQ = hsq(st)
                    S.dma("sp", PS[:, :, 0:n], pT[l, :, t0:t0 + n].rearrange("(k p) t -> p k t", p=128), ldchan(),
                          writes=[PS.b[0]])
                    S.op("act", lambda e: e.activation(out=PB[:, :, 0:n], in_=PS[:, :, 0:n], func=AF.Identity), reads=[PS.b[0]], writes=[PB.b[0]])
                    for nn in range(8):
                        w = W2.get(("wple", l, st, nn))
                        pp = S.banks[4 + 2 * (nn % 2)]
                        for k2 in range(2):
                            S.op("pe", lambda e, k2=k2, w=w, pp=pp: e.matmul(pp[:, 0:n], lhsT=w[:, 1024 + k2 * 128: 1024 + (k2 + 1) * 128],
                                                                            rhs=PB[:, k2, 0:n], start=(k2 == 0), stop=(k2 == 1)),
                                 reads=[w.b[0], PB.b[0]], writes=[pp.b[0]])
                        pg = S.banks[5 + 2 * (nn % 2)]
                        for kc in range(KC):
                            S.op("pe", lambda e, kc=kc, w=w, pg=pg: e.matmul(pg[:, 0:n], lhsT=w[:, kc * 128:(kc + 1) * 128],
                                                                            rhs=XB[:, kc, t0:t0 + n], start=(kc == 0), stop=(kc == KC - 1)),
                                 reads=[w.b[0]] + xbb(st, [kc]), writes=[pg.b[0]])
                        sg = SGt[nn % 2]
                        S.op("act", lambda e, sg=sg, pg=pg: e.activation(out=sg[:, 0:n], in_=pg[:, 0:n], func=AF.Sigmoid),
                             reads=[pg.b[0]], writes=[sg.b[0]])
                        S.op("dve", lambda e, sg=sg, pp=pp, nn=nn: e.tensor_tensor(out=E[:, nn, 0:n], in0=sg[:, 0:n], in1=pp[:, 0:n],
                                                                                op=ALU.mult),
                             reads=[sg.b[0], pp.b[0]], writes=[E.b[nn]])
                    S.op("act", lambda e: e.activation(out=SQ[:, :, 0:n], in_=E[:, :, 0:n], func=AF.Square), reads=E.b, writes=SQ.b)
                    p2 = S.banks[4]
                    colsum_bcast(p2, n, lambda kc: SQ[:, kc, 0:n], lambda kc: SQ.b)
                    S.op("act", lambda e: e.activation(out=RS[:, 0:n], in_=p2[:, 0:n], func=AF.Ln, scale=1.0 / D, bias=EPS[:, 1:2]),
                         reads=[p2.b[0], EPS.b[0]], writes=[RS.b[0]])
                    S.op("act", lambda e: e.activation(out=p2[:, 0:n], in_=RS[:, 0:n], func=AF.Exp, scale=-0.5),
                         reads=[RS.b[0]], writes=[p2.b[0]])
                    S.op("dve", lambda e: e.tensor_tensor(out=E[:, :, 0:n], in0=E[:, :, 0:n], in1=p2.v([[0, KC], [1, n]]), op=ALU.mult),
                         reads=E.b + [p2.b[0]], writes=E.b)
                    for nn in range(8):
                        xs = XF[:, nn, t0:t0 + n]
                        S.op("dve", lambda e, nn=nn, xs=xs: e.scalar_tensor_tensor(out=xs, in0=E[:, nn, 0:n],
                                                                                  scalar=VEC[:, vb + nn:vb + nn + 1], in1=xs,
                                                                                  op0=ALU.mult, op1=ALU.add),
                             reads=[E.b[nn], VEC.b[0]] + xfb(st, [nn]), writes=xfb(st, [nn]))
                    if last:
                        S.dma("sp", yT[:, t0:t0 + n].rearrange("(k p) t -> p k t", p=128), XF[:, :, t0:t0 + n], stchan(),
                              reads=xfb(st))
                    else:
                        S.op("act", lambda e: e.activation(out=XB[:, :, t0:t0 + n], in_=XF[:, :, t0:t0 + n], func=AF.Identity),
                             reads=xfb(st), writes=xbb(st))

                for ps, (c0, c1) in enumerate(FPASS):
                    ncp = c1 - c0
                    for ci in range(ncp):
                        w = W.get(("wgu", l, c0 + ci))
                        for st, (t0, n) in enumerate(STS):
                            pg = S.bank()
                            pu = S.bank()
                            for gu, bk in ((0, pg), (1, pu)):
                                for kc in range(KC):
                                    S.op("pe", lambda e, gu=gu, kc=kc, bk=bk: e.matmul(
                                        bk[:, 0:n], lhsT=w[:, (gu * 8 + kc) * 128:(gu * 8 + kc + 1) * 128],
                                        rhs=XB[:, kc, t0:t0 + n], start=(kc == 0), stop=(kc == KC - 1)),
                                        reads=[w.b[0]] + xbb(st, [kc]), writes=[bk.b[0]])
                            sg = SGt[k % 2]
                            k += 1
                            S.op("act", lambda e, sg=sg, pg=pg: e.activation(out=sg[:, 0:n], in_=pg[:, 0:n], func=AF.Silu),
                                 reads=[pg.b[0]], writes=[sg.b[0]])
                            S.op("dve", lambda e, sg=sg, pu=pu, ci=ci: e.tensor_tensor(out=H[:, ci, t0:t0 + n], in0=sg[:, 0:n],
                                                                                    in1=pu[:, 0:n], op=ALU.mult),
                                 reads=[sg.b[0], pu.b[0]], writes=[H.b[ci * NST + st]])
                    if ps < len(FPASS) - 1:
                        for nn in range(8):
                            w = W.get(("wdn", l, ps, nn))
                            for st, (t0, n) in enumerate(STS):
                                po = S.bank()
                                for ci in range(ncp):
                                    S.op("pe", lambda e, ci=ci: e.matmul(po[:, 0:n], lhsT=w[:, ci * 128:(ci + 1) * 128],
                                                                        rhs=H[:, ci, t0:t0 + n], start=(ci == 0), stop=(ci == ncp - 1)),
                                         reads=[w.b[0], H.b[ci * NST + st]], writes=[po.b[0]])
                                stage2_evac(ps, nn, st, po)
                    else:
                        def s2_st(st):
                            t0, n = STS[st]
                            for nn in range(8):
                                w = W.get(("wdn", l, ps, st, nn))
                                po = S.banks[nn % 2]
                                for ci in range(ncp):
                                    S.op("pe", lambda e, ci=ci, w=w, po=po: e.matmul(po[:, 0:n], lhsT=w[:, ci * 128:(ci + 1) * 128],
                                                                                    rhs=H[:, ci, t0:t0 + n], start=(ci == 0), stop=(ci == ncp - 1)),
                                         reads=[w.b[0], H.b[ci * NST + st]], writes=[po.b[0]])
                                stage2_evac(ps, nn, st, po)

                        for r in range(NST + 2):
                            l1 = S.record(lambda: s2_st(r)) if r < NST else []
                            l2 = S.record(lambda: layer_norm((hsq(r - 1), None, M2, RSTD), r - 1, vbl + 16, vbl + 24,
                                                             banks=(S.banks[2], S.banks[3]))) if 0 <= r - 1 < NST else []
                            l3 = S.record(lambda: ple_st(r - 2)) if 0 <= r - 2 < NST else []
                            S.replay(l1, l2, l3, weights=[0.6, 1.0, 1.0])
                S.barrier()

        def hgrn_phase(l):
            j = l // 2
            lbA = lambda h: LBT[:, j, 2, h:h + 1]
            omA = lambda h: LBT[:, j, 3, h:h + 1]
            gw = VEC[:, V_GN + j:V_GN + j + 1]
            with ExitStack() as ph:
                def t32(name, w=512, nb=1):
                    return Tile(S, ph, name, [128, w], F32, nb=nb)

                def t16(name, w=512, nb=1):
                    return Tile(S, ph, name, [128, w], BF16, nb=nb)
                SGf, Fm, LF = [t32("hg" + n) for n in "SGf Fm LF".split()]
                KK2 = [t32("hgKK") for i in range(2)]
                Bc2 = [t32("hgBc") for i in range(2)]
                Q12 = [t32("hgQ1") for i in range(2)]
                RV = t32("hgRV", 16)
                BR, EB, KD = [t32("hg" + n) for n in "BR EB KD".split()]
                EQ = BR
                QT = t16("hgQT")
                KT = [t16("hgKT%d" % i) for i in range(4)]
                KTf = t32("hgKTf")
                KDT2 = [t16("hgKDT") for i in range(2)]
                QD2 = [t16("hgQD") for i in range(2)]
                AT2 = [t16("hgATm") for i in range(2)]
                VT3 = [t16("hgVT") for i in range(3)]
                SGG3 = [t16("hgSGG") for i in range(3)]
                EBE2 = [t32("hgEBE", 16) for i in range(2)]
                OSQ = t16("hgOSQ")
                RS, T1 = t32("hgRS"), t32("hgT1")
                ST4 = Tile(S, ph, "hgST4", [128, 3, 128], F32)
                STB4 = Tile(S, ph, "hgSTB4", [128, 4, 128], BF16)
                OG = Tile(S, ph, "hgOG", [128, 8, 512], BF16, nb=8)
                ST = Tile(S, ph, "hgS", [128, 8, 128], F32, nb=8)
                S0 = Tile(S, ph, "hgS0", [128, 8, 128], F32)
                S0B = Tile(S, ph, "hgS0B", [128, 8, 128], BF16)
                SN = Tile(S, ph, "hgSN", [128, 8, 128], F32)
                VM2 = [t16("hgVM", 2048) for i in range(2)]
                R512 = t32("hgR512")
                S.dma("sp", R512[:], consts[:, C_R512:C_R512 + 512], ldchan(), writes=[R512.b[0]])
                S.op("pool", lambda e: e.memset(ST[:], 0.0), writes=ST.b)
                S.op("pool", lambda e: e.memset(RV[:], 0.0), writes=[RV.b[0]])
                for kt in KT:
                    S.op("pool", lambda e, kt=kt: e.memset(kt[:], 0.0), writes=[kt.b[0]])
                cb = [CN.b[0]]
                rr = {"i": 0}

                def bank6():
                    b = S.banks[rr["i"] % 6]
                    rr["i"] += 1
                    return b

                def stage_a(st, h, par):
                    stage_a1(st, h, par)
                    stage_a2(st, h, par)

                def stage_a1(st, h, par):
                    t0, n = STS[st]
                    sample = (st == NST - 1)
                    ntile = n // 128
                    QD, ATm, VT, SGG, EBE = QD2[par], AT2[par], VT3[h % 3], SGG3[h % 3], EBE2[par]
                    KK, Bc, Q1 = KK2[par], Bc2[par], Q12[par]
                    wA = W.get(("hgA", l, st, h))
                    wB = W.get(("hgB", l, st, h))
                    pq, pf, pgt, pv = S.banks[0], S.banks[1], S.banks[2], S.banks[3]
                    for wsel, sub, bk in ((wA, 1, pf), (wA, 0, pq), (wB, 1, pgt)):
                        for kc in range(KC):
                            S.op("pe", lambda e, wsel=wsel, sub=sub, bk=bk, kc=kc: e.matmul(
                                bk[:, 0:n], lhsT=wsel[:, (sub * 8 + kc) * 128:(sub * 8 + kc + 1) * 128],
                                rhs=XB[:, kc, t0:t0 + n], start=(kc == 0), stop=(kc == KC - 1)),
                                reads=[wsel.b[0]] + xbb(st, [kc]), writes=[bk.b[0]])
                    for tt_ in range(ntile):
                        for kc in range(KC):
                            S.op("pe", lambda e, tt_=tt_, kc=kc: e.matmul(
                                pv[:, tt_ * 128:(tt_ + 1) * 128], lhsT=XB[:, kc, t0 + tt_ * 128:t0 + (tt_ + 1) * 128],
                                rhs=wB[:, kc * 128:(kc + 1) * 128], start=(kc == 0), stop=(kc == KC - 1)),
                                reads=[wB.b[0]] + xbb(st, [kc]), writes=[pv.b[0]])
                    S.op("act", lambda e: e.activation(out=SGf[:, 0:n], in_=pf[:, 0:n], func=AF.Tanh, scale=0.5), reads=[pf.b[0]], writes=[SGf.b[0]])
                    S.op("act", lambda e: e.activation(out=Q1[:, 0:n], in_=pq[:, 0:n], func=AF.Silu), reads=[pq.b[0]], writes=[Q1.b[0]])
                    S.op("act", lambda e: e.activation(out=SGG[:, 0:n], in_=pgt[:, 0:n], func=AF.Silu), reads=[pgt.b[0]], writes=[SGG.b[0]])
                    S.op("act", lambda e: e.activation(out=VT[:, 0:n], in_=pv[:, 0:n], func=AF.Identity), reads=[pv.b[0]], writes=[VT.b[0]])
                    S.op("dve", lambda e: e.tensor_scalar(out=Fm[:, 0:n], in0=SGf[:, 0:n], scalar1=omA(h), scalar2=lbA(h),
                                                          op0=ALU.mult, op1=ALU.add),
                         reads=[SGf.b[0], LBT.b[0]], writes=[Fm.b[0]])
                    S.op("act", lambda e: e.activation(out=LF[:, 0:n], in_=Fm[:, 0:n], func=AF.Ln), reads=[Fm.b[0]], writes=[LF.b[0]])
                    S.op("pool", lambda e: e.tensor_scalar(out=KK[:, 0:n], in0=Fm[:, 0:n], scalar1=-1.0, scalar2=1.0,
                                                           op0=ALU.mult, op1=ALU.add), reads=[Fm.b[0]], writes=[KK.b[0]])
                    rst = CN[:, C_R8:C_R8 + 128] if sample else R512[:, 0:n]
                    S.op("dve", lambda e: e.tensor_tensor_scan(out=Bc[:, 0:n], data0=rst, data1=LF[:, 0:n], initial=0.0,
                                                               op0=ALU.mult, op1=ALU.add),
                         reads=[LF.b[0], R512.b[0]] + cb, writes=[Bc.b[0]])

                def stage_a2(st, h, par):
                    t0, n = STS[st]
                    sample = (st == NST - 1)
                    ntile = n // 128
                    QD, ATm, VT, SGG, EBE = QD2[par], AT2[par], VT3[h % 3], SGG3[h % 3], EBE2[par]
                    KK, Bc, Q1 = KK2[par], Bc2[par], Q12[par]
                    KDT, VM = KDT2[par], VM2[par]
                    psu = S.banks[6 + par]
                    S.op("act", lambda e: e.activation(out=EB[:, 0:n], in_=Bc[:, 0:n], func=AF.Exp), reads=[Bc.b[0]], writes=[EB.b[0]])
                    S.op("dve", lambda e: e.scalar_tensor_tensor(out=QD[:, 0:n], in0=Q1[:, 0:n], scalar=QK, in1=EB[:, 0:n],
                                                                op0=ALU.mult, op1=ALU.mult),
                         reads=[Q1.b[0], EB.b[0]], writes=[QD.b[0]])
                    L = 8 if sample else 128
                    nseg = n // L
                    bend = Bc.v([[L, nseg], [0, L]], off=L - 1)
                    b3 = Bc.v([[L, nseg], [1, L]])
                    S.op("pool", lambda e: e.tensor_tensor(out=KD.v([[L, nseg], [1, L]]), in0=bend, in1=b3, op=ALU.subtract),
                         reads=[Bc.b[0]], writes=[KD.b[0]])
                    S.op("act", lambda e: e.activation(out=KD[:, 0:n], in_=KD[:, 0:n], func=AF.Exp), reads=[KD.b[0]], writes=[KD.b[0]])
                    S.op("pool", lambda e: e.tensor_tensor(out=KD[:, 0:n], in0=KD[:, 0:n], in1=KK[:, 0:n], op=ALU.mult),
                         reads=[KD.b[0], KK.b[0]], writes=[KD.b[0]])
                    S.op("act", lambda e: e.activation(out=EBE[:, 0:nseg], in_=Bc.v([[L, nseg]], off=L - 1), func=AF.Exp),
                         reads=[Bc.b[0]], writes=[EBE.b[0]])
                    pk = S.banks[5]
                    for tt_ in range(ntile):
                        S.op("pe", lambda e, tt_=tt_: e.transpose(out=pk[:, tt_ * 128:(tt_ + 1) * 128], in_=KD[:, tt_ * 128:(tt_ + 1) * 128],
                                                                  identity=CN[:, C_ID:C_ID + 128]),
                             reads=[KD.b[0]] + cb, writes=[pk.b[0]])
                    S.op("act", lambda e: e.activation(out=KDT[:, 0:n], in_=pk[:, 0:n], func=AF.Identity), reads=[pk.b[0]], writes=[KDT.b[0]])
                    pa = S.banks[5]
                    if not sample:
                        for tt_ in range(ntile):
                            cs = slice(tt_ * 128, (tt_ + 1) * 128)
                            S.op("pe", lambda e, cs=cs: e.matmul(psu[:, cs], lhsT=KDT[:, cs], rhs=VT[:, cs], start=True, stop=True),
                                 reads=[KDT.b[0], VT.b[0]], writes=[psu.b[0]])
                        S.op("pool", lambda e: e.tensor_copy(out=RV.v([[4, ntile], [1, 3]], off=1),
                                                             in_=Bc.v([[128, ntile], [32, 3]], off=31)),
                             reads=[Bc.b[0]], writes=[RV.b[0]])
                        S.op("pool", lambda e: e.tensor_tensor(out=BR.v([[32, 16], [1, 32]]), in0=Bc.v([[32, 16], [1, 32]]),
                                                               in1=RV.v([[1, 16], [0, 32]]), op=ALU.subtract),
                             reads=[Bc.b[0], RV.b[0]], writes=[BR.b[0]])
                        S.op("act", lambda e: e.activation(out=EQ[:, 0:n], in_=BR[:, 0:n], func=AF.Exp), reads=[BR.b[0]], writes=[EQ.b[0]])
                        S.op("dve", lambda e: e.scalar_tensor_tensor(out=QT[:, 0:n], in0=Q1[:, 0:n], scalar=QK, in1=EQ[:, 0:n],
                                                                    op0=ALU.mult, op1=ALU.mult),
                             reads=[Q1.b[0], EQ.b[0]], writes=[QT.b[0]])
                        for i in range(4):
                            wd_ = 32 * (i + 1)
                            kf = KTf
                            S.op("pool", lambda e, i=i, wd_=wd_, kf=kf: e.tensor_tensor(
                                out=kf.v([[128, ntile], [1, wd_]]), in0=RV.v([[4, ntile], [0, wd_]], off=i),
                                in1=Bc.v([[128, ntile], [1, wd_]]), op=ALU.subtract),
                                reads=[RV.b[0], Bc.b[0]], writes=[kf.b[0]])
                            S.op("act", lambda e, wd_=wd_, kf=kf: e.activation(out=kf.v([[128, ntile], [1, wd_]]),
                                                                              in_=kf.v([[128, ntile], [1, wd_]]), func=AF.Exp),
                                 reads=[kf.b[0]], writes=[kf.b[0]])
                            S.op("dve", lambda e, i=i, wd_=wd_, kf=kf: e.tensor_tensor(
                                out=KT[i].v([[128, ntile], [1, wd_]]), in0=kf.v([[128, ntile], [1, wd_]]),
                                in1=KK.v([[128, ntile], [1, wd_]]), op=ALU.mult),
                                reads=[kf.b[0], KK.b[0]], writes=[KT[i].b[0]])
                        for tt_ in range(ntile):
                            for i in range(4):
                                c0 = tt_ * 128 + 32 * i
                                S.op("pe", lambda e, tt_=tt_, i=i, c0=c0: e.matmul(pa[:, c0:c0 + 32], lhsT=KT[i][:, tt_ * 128:(tt_ + 1) * 128],
                                                                                 rhs=QT[:, c0:c0 + 32], start=True, stop=True),
                                     reads=[KT[i].b[0], QT.b[0]], writes=[pa.b[0]])
                        S.op("dve", lambda e: e.tensor_tensor(out=ATm.v([[128, ntile], [1, 128]]), in0=pa.v([[128, ntile], [1, 128]]),
                                                              in1=CN.v([[0, ntile], [1, 128]], off=C_CAUS), op=ALU.mult),
                             reads=[pa.b[0]] + cb, writes=[ATm.b[0]])
                    else:
                        kf = KTf
                        S.op("act", lambda e: e.activation(out=kf[:, 0:n], in_=Bc[:, 0:n], func=AF.Exp, scale=-1.0),
                             reads=[Bc.b[0]], writes=[kf.b[0]])
                        S.op("dve", lambda e: e.tensor_tensor(out=KT[0][:, 0:n], in0=kf[:, 0:n], in1=KK[:, 0:n], op=ALU.mult),
                             reads=[kf.b[0], KK.b[0]], writes=[KT[0].b[0]])
                        S.op("pe", lambda e: e.matmul(pa[:, 0:128], lhsT=KT[0][:, 0:128], rhs=QD[:, 0:128], start=True, stop=True),
                             reads=[KT[0].b[0], QD.b[0]], writes=[pa.b[0]])
                        S.op("dve", lambda e: e.tensor_tensor(out=ATm[:, 0:128], in0=pa[:, 0:128], in1=CN[:, C_BD:C_BD + 128], op=ALU.mult),
                             reads=[pa.b[0]] + cb, writes=[ATm.b[0]])
                        S.op("dve", lambda e: e.tensor_tensor(out=VM.v([[128, 16], [1, 128]]), in0=VT.v([[0, 16], [1, 128]]),
                                                              in1=CN.v([[1, 16], [0, 128]], off=C_SEG), op=ALU.mult),
                             reads=[VT.b[0]] + cb, writes=[VM.b[0]])

                def stage_b(st, h, par):
                    t0, n = STS[st]
                    sample = (st == NST - 1)
                    ntile = n // 128
                    QD, ATm, VT, SGG, EBE = QD2[par], AT2[par], VT3[h % 3], SGG3[h % 3], EBE2[par]
                    psu = S.banks[6 + par]
                    KDT, VM = KDT2[par], VM2[par]
                    po = S.banks[4]
                    if not sample:
                        S.op("act", lambda e: e.activation(out=STB4[:, 0, :], in_=ST[:, h, :], func=AF.Identity),
                             reads=[ST.b[h]], writes=[STB4.b[0]])
                        for tt_ in range(ntile):
                            src = ST[:, h, :] if tt_ == 0 else ST4[:, tt_ - 1, :]
                            dst = ST[:, h, :] if tt_ == ntile - 1 else ST4[:, tt_, :]
                            S.op("dve", lambda e, tt_=tt_, src=src, dst=dst: e.scalar_tensor_tensor(
                                out=dst, in0=src, scalar=EBE[:, tt_:tt_ + 1], in1=psu[:, tt_ * 128:(tt_ + 1) * 128],
                                op0=ALU.mult, op1=ALU.add),
                                reads=[ST.b[h], ST4.b[0], EBE.b[0], psu.b[0], STB4.b[0]], writes=[ST.b[h], ST4.b[0]])
                        S.op("act", lambda e: e.activation(out=STB4[:, 1:4, :], in_=ST4[:, 0:3, :], func=AF.Identity),
                             reads=[ST4.b[0]], writes=[STB4.b[0]])
                        for tt_ in range(ntile):
                            cs = slice(tt_ * 128, (tt_ + 1) * 128)
                            S.op("pe", lambda e, cs=cs: e.matmul(po[:, cs], lhsT=VT[:, cs], rhs=ATm[:, cs], start=True, stop=False),
                                 reads=[VT.b[0], ATm.b[0]], writes=[po.b[0]])
                            S.op("pe", lambda e, cs=cs, tt_=tt_: e.matmul(po[:, cs], lhsT=STB4[:, tt_, :], rhs=QD[:, cs], start=False, stop=True),
                                 reads=[STB4.b[0], QD.b[0]], writes=[po.b[0]])
                        if st == NST - 2:
                            S.dma("sp", hgp[j, :, h, :], ST[:, h, :], stchan(), reads=[ST.b[h]])
                    else:
                        S.op("pe", lambda e: e.matmul(po[:, 0:128], lhsT=VT[:, 0:128], rhs=ATm[:, 0:128], start=True, stop=False),
                             reads=[VT.b[0], ATm.b[0]], writes=[po.b[0]])
                        for hf in range(2):
                            S.dma("sp", S0[:], hg0[j, hf * 8:(hf + 1) * 8, h].rearrange("b k v -> k b v"), ldchan(), writes=[S0.b[0]])
                            S.op("act", lambda e: e.activation(out=S0B[:], in_=S0[:], func=AF.Identity), reads=[S0.b[0]], writes=[S0B.b[0]])
                            for bb in range(8):
                                b = hf * 8 + bb
                                S.op("pe", lambda e, bb=bb, b=b: e.matmul(po[:, b * 8:(b + 1) * 8], lhsT=S0B[:, bb, :],
                                                                       rhs=QD[:, b * 8:(b + 1) * 8], start=False, stop=(b == 15)),
                                     reads=[S0B.b[0], QD.b[0]], writes=[po.b[0]])
                            pss2 = [S.banks[6], S.banks[7]]
                            for q in range(2):
                                S.op("pe", lambda e, q=q, hf=hf: e.matmul(pss2[q][:, 0:512], lhsT=KDT[:, 0:128],
                                                                       rhs=VM[:, hf * 1024 + q * 512: hf * 1024 + (q + 1) * 512],
                                                                       start=True, stop=True),
                                     reads=[KDT.b[0], VM.b[0]], writes=[pss2[q].b[0]])
                            S.op("pool", lambda e, hf=hf: e.tensor_tensor(out=SN[:], in0=S0[:],
                                                                       in1=EBE.v([[1, 8], [0, 128]], off=hf * 8), op=ALU.mult),
                                 reads=[S0.b[0], EBE.b[0]], writes=[SN.b[0]])
                            for q in range(2):
                                S.op("dve", lambda e, q=q: e.tensor_tensor(out=SN[:, q * 4:(q + 1) * 4, :], in0=SN[:, q * 4:(q + 1) * 4, :],
                                                                        in1=pss2[q].v([[128, 4], [1, 128]]), op=ALU.add),
                                     reads=[SN.b[0], pss2[q].b[0]], writes=[SN.b[0]])
                            S.dma("sp", hgs[j, hf * 8:(hf + 1) * 8, h].rearrange("b k v -> k b v"), SN[:], stchan(), reads=[SN.b[0]])
                    S.op("act", lambda e: e.activation(out=OSQ[:, 0:n], in_=po[:, 0:n], func=AF.Square), reads=[po.b[0]], writes=[OSQ.b[0]])
                    pss = psu if not sample else S.banks[6]
                    S.op("pe", lambda e: e.matmul(pss[:, 0:n], lhsT=ONESB[:], rhs=OSQ[:, 0:n], start=True, stop=True),
                         reads=[ONESB.b[0], OSQ.b[0]], writes=[pss.b[0]])
                    S.op("act", lambda e: e.activation(out=RS[:, 0:n], in_=pss[:, 0:n], func=AF.Ln, scale=1.0 / 128, bias=EPS[:, 1:2]),
                         reads=[pss.b[0], EPS.b[0]], writes=[RS.b[0]])
                    S.op("act", lambda e: e.activation(out=RS[:, 0:n], in_=RS[:, 0:n], func=AF.Exp, scale=-0.5),
                         reads=[RS.b[0]], writes=[RS.b[0]])
                    S.op("dve", lambda e: e.tensor_tensor(out=T1[:, 0:n], in0=po[:, 0:n], in1=RS[:, 0:n], op=ALU.mult),
                         reads=[po.b[0], RS.b[0]], writes=[T1.b[0]])
                    S.op("dve", lambda e: e.scalar_tensor_tensor(out=OG[:, h, 0:n], in0=T1[:, 0:n], scalar=gw, in1=SGG[:, 0:n],
                                                                op0=ALU.mult, op1=ALU.mult),
                         reads=[T1.b[0], SGG.b[0], VEC.b[0]], writes=[OG.b[h]])

                def out_proj(st):
                    t0, n = STS[st]
                    for nn in range(8):
                        w = W.get(("who", l, st, nn))
                        po = bank6()
                        for h in range(8):
                            S.op("pe", lambda e, h=h: e.matmul(po[:, 0:n], lhsT=w[:, h * 128:(h + 1) * 128], rhs=OG[:, h, 0:n],
                                                              start=(h == 0), stop=(h == 7)),
                                 reads=[w.b[0], OG.b[h]], writes=[po.b[0]])
                        xs = XF[:, nn, t0:t0 + n]
                        S.op("dve", lambda e, xs=xs: e.scalar_tensor_tensor(out=xs, in0=xs, scalar=ALPHA, in1=po[:, 0:n],
                                                                           op0=ALU.mult, op1=ALU.add),
                             reads=xfb(st, [nn]) + [po.b[0]], writes=xfb(st, [nn]))

                for st in range(NST):
                    if True:
                        stage_a1(st, 0, 0)
                        S.replay(S.record(lambda: stage_a1(st, 1, 1)), S.record(lambda: stage_a2(st, 0, 0)))
                        for h in range(8):
                            l1 = S.record(lambda: stage_a1(st, h + 2, (h + 2) % 2)) if h + 2 < 8 else []
                            l2 = S.record(lambda: stage_a2(st, h + 1, (h + 1) % 2)) if h + 1 < 8 else []
                            l3 = S.record(lambda: stage_b(st, h, h % 2))
                            S.replay(l1, l2, l3, weights=[0.6, 1.0, 1.0])
                    out_proj(st)
                    layer_norm((OG, None, SGf, Fm), st, l * V_LAYER + 0, l * V_LAYER + 8)
                S.barrier()

        def s5_phase(l):
            j = l // 2
            PI_S = 3.1415925

            def tt(eng, out, a, b, op, r, w):
                S.op(eng, lambda e: e.tensor_tensor(out=out, in0=a, in1=b, op=op), reads=r, writes=w)

            def ts(eng, out, a, s1, s2, op0, op1, r, w):
                S.op(eng, lambda e: e.tensor_scalar(out=out, in0=a, scalar1=s1, scalar2=s2, op0=op0, op1=op1), reads=r, writes=w)

            def stt(out, a, sc, b, op0, op1, r, w):
                S.op("dve", lambda e: e.scalar_tensor_tensor(out=out, in0=a, scalar=sc, in1=b, op0=op0, op1=op1), reads=r, writes=w)

            def act(out, in_, func, r, w, **kw):
                S.op("act", lambda e: e.activation(out=out, in_=in_, func=func, **kw), reads=r, writes=w)

            with ExitStack() as ph:
                EQ = Tile(S, ph, "s5EQ", [128, 2048], BF16)
                EPQ = Tile(S, ph, "s5EPQ", [128, 2048], BF16)
                EQz = Tile(S, ph, "s5EQz", [128, 2048], BF16)
                EPQz = Tile(S, ph, "s5EPQz", [128, 2048], BF16)
                PAr, PAi, PBr, PBi = [Tile(S, ph, "s5P" + n, [128, 16, 32], F32) for n in "Ar Ai Br Bi".split()]
                BBR, BBI, CR, CI = [Tile(S, ph, "s5" + n, [128, 32, 16], F32) for n in "BBR BBI CR CI".split()]
                MUr, MUi, MUn = [Tile(S, ph, "s5MU" + n, [128, 8, 32], F32) for n in "r i n".split()]
                DB = Tile(S, ph, "s5DB", [128, 64], F32)
                OP = Tile(S, ph, "s5OP", [128, 2, 32], F32)
                cb = [CN.b[0]]
                S.dma("sp", DB[:], s5d[j], ldchan(), writes=[DB.b[0]])
                S.dma("sp", CR[:], s5c[j, 0].rearrange("p (a c) -> p a c", c=16), ldchan(), writes=[CR.b[0]])
                S.dma("sp", CI[:], s5c[j, 1].rearrange("p (a c) -> p a c", c=16), ldchan(), writes=[CI.b[0]])
                with ExitStack() as su:
                    A3 = Tile(S, su, "s5A3", [128, 96], F32)
                    BRE = Tile(S, su, "s5BRE", [128, 32, 16], F32)
                    BIM = Tile(S, su, "s5BIM", [128, 32, 16], F32)
                    EST = Tile(S, su, "s5EST", [128, 2048], F32)
                    ANG, T1, T2, SNt, CSt, MG = [Tile(S, su, "s5" + n, [128, 512], F32) for n in "ANG T1 T2 SN CS MG".split()]
                    SM = Tile(S, su, "s5SM", [128, 16, 32], F32)
                    TAU = Tile(S, su, "s5TAU", [128, 512], F32)
                    S.dma("sp", TAU[:], consts[:, C_TAU:C_TAU + 512], ldchan(), writes=[TAU.b[0]])
                    S.dma("sp", A3[:], s5a[j], ldchan(), writes=[A3.b[0]])
                    S.dma("sp", BRE[:], s5b[j, 0].rearrange("p (a c) -> p a c", c=16), ldchan(), writes=[BRE.b[0]])
                    S.dma("sp", BIM[:], s5b[j, 1].rearrange("p (a c) -> p a c", c=16), ldchan(), writes=[BIM.b[0]])
                    for k, dst in enumerate((EQ, EPQ, EQz, EPQz)):
                        S.dma("sp", EST[:], emat[k], ldchan(), writes=[EST.b[0]])
                        S.op("pool", lambda e, dst=dst: e.tensor_copy(out=dst[:], in_=EST[:]), reads=[EST.b[0]], writes=[dst.b[0]])
                    sm = [SM.b[0]]
                    DL, LR, LI = SM[:, 0, :], SM[:, 1, :], SM[:, 2, :]
                    act(DL, A3[:, 64:96], AF.Exp, [A3.b[0]], sm)
                    tt("dve", LR, A3[:, 0:32], DL, ALU.mult, [A3.b[0]] + sm, sm)
                    tt("dve", LI, A3[:, 32:64], DL, ALU.mult, [A3.b[0]] + sm, sm)
                    tau = TAU[:]
                    for sg, (Pr, Pi) in ((1.0, (PAr, PAi)), (-1.0, (PBr, PBi))):
                        stt(ANG[:], tau, sg, SM.v([[0, 16], [1, 32]], off=2 * 32), ALU.mult, ALU.mult, [TAU.b[0]] + sm, [ANG.b[0]])
                        stt(MG[:], tau, sg, SM.v([[0, 16], [1, 32]], off=1 * 32), ALU.mult, ALU.mult, [TAU.b[0]] + sm, [MG.b[0]])
                        act(MG[:], MG[:], AF.Exp, [MG.b[0]], [MG.b[0]])
                        for which, dstt in ((0, SNt), (1, CSt)):
                            if which == 1:
                                ts("dve", ANG[:], ANG[:], math.pi / 2, None, ALU.add, ALU.bypass, [ANG.b[0]], [ANG.b[0]])
                            ts("dve", T1[:], ANG[:], 1.0 / TWO_PI, MAGIC, ALU.mult, ALU.add, [ANG.b[0]], [T1.b[0]])
                            ts("dve", T1[:], T1[:], -MAGIC, None, ALU.add, ALU.bypass, [T1.b[0]], [T1.b[0]])
                            stt(T2[:], T1[:], -TWO_PI, ANG[:], ALU.mult, ALU.add, [T1.b[0], ANG.b[0]], [T2.b[0]])
                            ts("dve", T2[:], T2[:], -PI_S, PI_S, ALU.max, ALU.min, [T2.b[0]], [T2.b[0]])
                            act(dstt[:], T2[:], AF.Sin, [T2.b[0]], [dstt.b[0]])
                        tt("dve", Pr[:].rearrange("p a b -> p (a b)"), MG[:], CSt[:], ALU.mult, [MG.b[0], CSt.b[0]], [Pr.b[0]])
                        tt("dve", Pi[:].rearrange("p a b -> p (a b)"), MG[:], SNt[:], ALU.mult, [MG.b[0], SNt.b[0]], [Pi.b[0]])
                    ar, ai = A3[:, 0:32], A3[:, 32:64]
                    NR, DEN, t1, t2, CRE, CIM = [SM[:, 3 + k, :] for k in range(6)]
                    a3 = [A3.b[0]]
                    ts("dve", NR, PAr[:, 8, :], -1.0, None, ALU.add, ALU.bypass, [PAr.b[0]], sm)
                    NI = PAi[:, 8, :]
                    tt("dve", t1, ar, ar, ALU.mult, a3, sm)
                    tt("dve", t2, ai, ai, ALU.mult, a3, sm)
                    tt("dve", DEN, t1, t2, ALU.add, sm, sm)
                    S.op("dve", lambda e: e.reciprocal(out=DEN, in_=DEN), reads=sm, writes=sm)
                    tt("dve", t1, NR, ar, ALU.mult, sm + a3, sm)
                    tt("dve", t2, NI, ai, ALU.mult, [PAi.b[0]] + a3, sm)
                    tt("dve", t1, t1, t2, ALU.add, sm, sm)
                    tt("dve", CRE, t1, DEN, ALU.mult, sm, sm)
                    tt("dve", t1, NI, ar, ALU.mult, [PAi.b[0]] + a3, sm)
                    tt("dve", t2, NR, ai, ALU.mult, sm + a3, sm)
                    tt("dve", t1, t1, t2, ALU.subtract, sm, sm)
                    tt("dve", CIM, t1, DEN, ALU.mult, sm, sm)
                    creb = SM.v([[1, 32], [0, 16]], off=7 * 32)
                    cimb = SM.v([[1, 32], [0, 16]], off=8 * 32)
                    TA3 = T1[:].rearrange("p (a c) -> p a c", c=16)
                    tt("dve", BBR[:], creb, BRE[:], ALU.mult, sm + [BRE.b[0]], [BBR.b[0]])
                    tt("dve", TA3, cimb, BIM[:], ALU.mult, sm + [BIM.b[0]], [T1.b[0]])
                    tt("dve", BBR[:], BBR[:], TA3, ALU.subtract, [BBR.b[0], T1.b[0]], [BBR.b[0]])
                    tt("dve", BBI[:], creb, BIM[:], ALU.mult, sm + [BIM.b[0]], [BBI.b[0]])
                    tt("dve", TA3, cimb, BRE[:], ALU.mult, sm + [BRE.b[0]], [T1.b[0]])
                    tt("dve", BBI[:], BBI[:], TA3, ALU.add, [BBI.b[0], T1.b[0]], [BBI.b[0]])
                    mub = [MUr.b[0], MUi.b[0]]
                    S.op("pool", lambda e: e.tensor_copy(out=MUr[:, 0, :], in_=PAr[:, 15, :]), reads=[PAr.b[0]], writes=[MUr.b[0]])
                    S.op("pool", lambda e: e.tensor_copy(out=MUi[:, 0, :], in_=PAi[:, 15, :]), reads=[PAi.b[0]], writes=[MUi.b[0]])
                    for k in range(1, 8):
                        re_, im_ = MUr[:, k - 1, :], MUi[:, k - 1, :]
                        tt("dve", t1, re_, re_, ALU.mult, mub, sm)
                        tt("dve", t2, im_, im_, ALU.mult, mub, sm)
                        tt("dve", MUr[:, k, :], t1, t2, ALU.subtract, sm, [MUr.b[0]])
                        tt("dve", t1, re_, im_, ALU.mult, mub, sm)
                        ts("dve", MUi[:, k, :], t1, 2.0, None, ALU.mult, ALU.bypass, sm, [MUi.b[0]])
                    ts("dve", MUn[:], MUi[:], -1.0, None, ALU.mult, ALU.bypass, [MUi.b[0]], [MUn.b[0]])
                    S.barrier()
                with ExitStack() as cs:
                    TA = Tile(S, cs, "s5TA", [128, 512], F32)
                    UR, UI, VR, VIN = [Tile(S, cs, "s5" + n, [128, 4, 128], F32) for n in "UR UI VR VIN".split()]
                    TW2 = [Tile(S, cs, "s5TW", [128, 8, 128], BF16) for _ in range(2)]
                    PRm2 = [[Tile(S, cs, "s5PRm%d" % h, [128, 4, 128], BF16) for h in range(2)] for _ in range(2)]
                    PNm2 = [[Tile(S, cs, "s5PNm%d" % h, [128, 4, 128], BF16) for h in range(2)] for _ in range(2)]
                    QRT2 = [Tile(S, cs, "s5QRT", [128, 4, 128], BF16) for _ in range(2)]
                    QIT2 = [Tile(S, cs, "s5QIT", [128, 4, 128], BF16) for _ in range(2)]
                    QMR2 = [Tile(S, cs, "s5QMR", [128, 4, 128], BF16) for _ in range(2)]
                    QMI2 = [Tile(S, cs, "s5QMI", [128, 4, 128], BF16) for _ in range(2)]
                    H0c2 = [Tile(S, cs, "s5H0c", [128, 2, 64], F32) for _ in range(2)]
                    OSc2 = [Tile(S, cs, "s5OSc", [128, 2, 64], F32) for _ in range(2)]
                    UB3 = [[Tile(S, cs, "s5UB%d" % h, [128, NB], BF16) for h in range(2)] for _ in range(3)]
                    HPr2 = [Tile(S, cs, "s5HPr", [128, NB], BF16) for _ in range(2)]
                    HPi2 = [Tile(S, cs, "s5HPi", [128, NB], BF16) for _ in range(2)]
                    Ar, Ai = [Tile(S, cs, "s5sc" + n, [128, NB], F32) for n in "Ar Ai".split()]
                    Br, Bi = [Tile(S, cs, "s5sc" + n, [128, 128], F32) for n in "Br Bi".split()]
                    Yb2 = [Tile(S, cs, "s5Yb", [128, 8, NB], BF16, nb=8) for _ in range(2)]
                    TG1 = Tile(S, cs, "s5TG", [128, NB], F32)
                    TG2 = [TG1, TG1]
                    for t_ in HPr2 + HPi2:
                        S.op("pool", lambda e, t_=t_: e.memset(t_[:], 0.0), writes=[t_.b[0]])
                    allxb = lambda kc: [XB.b[kc * NST + st] for st in range(NST)]
                    rrb = {"i": 0}

                    def bankT():
                        return S.banks[6]

                    def ctable(Pr, Pi, s_, Are, Aim, OUTr, OUTi, kc, neg_im):
                        pr = Pr.v([[32, 8], [1, 4], [0, 16]], off=s_ * 32 + 4 * kc)
                        pi = Pi.v([[32, 8], [1, 4], [0, 16]], off=s_ * 32 + 4 * kc)
                        are = Are.v([[0, 8], [16, 4], [1, 16]], off=4 * kc * 16)
                        aim = Aim.v([[0, 8], [16, 4], [1, 16]], off=4 * kc * 16)
                        o_r = OUTr.v([[16, 8], [128, 4], [1, 16]])
                        o_i = OUTi.v([[16, 8], [128, 4], [1, 16]])
                        ta = TA.v([[64, 8], [16, 4], [1, 16]])
                        rd = [Pr.b[0], Pi.b[0], Are.b[0], Aim.b[0]]
                        tt("pool", o_r, pr, are, ALU.mult, rd, [OUTr.b[0]])
                        tt("pool", ta, pi, aim, ALU.mult, rd, [TA.b[0]])
                        tt("pool", o_r, o_r, ta, ALU.subtract, [OUTr.b[0], TA.b[0]], [OUTr.b[0]])
                        tt("pool", o_i, pr, aim, ALU.mult, rd, [OUTi.b[0]])
                        tt("pool", ta, pi, are, ALU.mult, rd, [TA.b[0]])
                        tt("pool", o_i, o_i, ta, ALU.add, [OUTi.b[0], TA.b[0]], [OUTi.b[0]])
                        if neg_im:
                            fl = OUTi[:].rearrange("p a b -> p (a b)")
                            act(fl, fl, AF.Identity, [OUTi.b[0]], [OUTi.b[0]], scale=-1.0)

                    def tables(kc):
                        cp = kc % 2
                        TW, PRm, PNm, QRT, QIT, H0c = TW2[cp], PRm2[cp], PNm2[cp], QRT2[cp], QIT2[cp], H0c2[cp]
                        ctable(PBr, PBi, 7, BBR, BBI, UR, UI, kc, False)
                        ctable(PAr, PAi, 7, CR, CI, VR, VIN, kc, True)
                        for h in range(2):
                            pt = bankT()
                            hs = slice(64 * h, 64 * h + 64)
                            for pl in range(4):
                                S.op("pe", lambda e, pl=pl, pt=pt, hs=hs: e.matmul(pt[:, pl * 128:(pl + 1) * 128], lhsT=UR[hs, pl, :], rhs=VR[hs, pl, :],
                                                                                 start=True, stop=False),
                                     reads=[UR.b[0], VR.b[0]], writes=[pt.b[0]])
                                S.op("pe", lambda e, pl=pl, pt=pt, hs=hs: e.matmul(pt[:, pl * 128:(pl + 1) * 128], lhsT=UI[hs, pl, :], rhs=VIN[hs, pl, :],
                                                                                 start=False, stop=True),
                                     reads=[UI.b[0], VIN.b[0]], writes=[pt.b[0]])
                            tt("dve", TW.v([[256, 4], [1, 128]], off=h * 128), pt.v([[128, 4], [1, 128]]),
                               CN.v([[0, 4], [1, 128]], off=C_TM), ALU.mult, [pt.b[0]] + cb, [TW.b[0]])
                        ctable(PAr, PAi, 8, CR, CI, UR, UI, kc, True)
                        for h in range(2):
                            rm = CN[:, C_RM + h:C_RM + h + 1]
                            act(PRm[h][:], UR[:], AF.Identity, [UR.b[0]] + cb, [PRm[h].b[0]], scale=rm)
                            act(PNm[h][:], UI[:], AF.Identity, [UI.b[0]] + cb, [PNm[h].b[0]], scale=rm)
                        ctable(PBr, PBi, 0, BBR, BBI, VR, VIN, kc, False)
                        QMR, QMI = QMR2[cp], QMI2[cp]
                        mur_b = MUr.v([[1, 4], [0, 128]], off=4 * kc)
                        mui_b = MUi.v([[1, 4], [0, 128]], off=4 * kc)
                        ta3 = TA.v([[128, 4], [1, 128]])
                        mbb = [MUr.b[0], MUi.b[0]]
                        tt("pool", UR[:], VR[:], mur_b, ALU.mult, [VR.b[0]] + mbb, [UR.b[0]])
                        tt("pool", ta3, VIN[:], mui_b, ALU.mult, [VIN.b[0]] + mbb, [TA.b[0]])
                        tt("pool", UR[:], UR[:], ta3, ALU.subtract, [UR.b[0], TA.b[0]], [UR.b[0]])
                        tt("pool", UI[:], VR[:], mui_b, ALU.mult, [VR.b[0]] + mbb, [UI.b[0]])
                        tt("pool", ta3, VIN[:], mur_b, ALU.mult, [VIN.b[0]] + mbb, [TA.b[0]])
                        tt("pool", UI[:], UI[:], ta3, ALU.add, [UI.b[0], TA.b[0]], [UI.b[0]])
                        for src, dst in ((VR, QRT), (VIN, QIT), (UR, QMR), (UI, QMI)):
                            pq = bankT()
                            for pl in range(4):
                                S.op("pe", lambda e, pl=pl, src=src, pq=pq: e.transpose(out=pq[:, pl * 128:(pl + 1) * 128], in_=src[:, pl, :],
                                                                                      identity=CN[:, C_ID:C_ID + 128]),
                                     reads=[src.b[0]] + cb, writes=[pq.b[0]])
                            act(dst[:].rearrange("p a b -> p (a b)"), pq[:, 0:512], AF.Identity, [pq.b[0]], [dst.b[0]])
                        for ri in range(2):
                            S.dma("sp", H0c[:, ri, :], s5h0[j, ri, :, kc * 64:(kc + 1) * 64], ldchan(), writes=[H0c.b[0]])

                    def pair_a1(kc, pl):
                        cp = kc % 2
                        p = 4 * kc + pl
                        pp = p % 2
                        QRT, QIT = QRT2[cp], QIT2[cp]
                        UB = UB3[p % 3]
                        for h in range(2):
                            pu = S.banks[0]
                            for i in range(8):
                                eqt, pb0, npp = (EQ, 32 * pl, 32) if pl < 3 else (EQz, 64, 64)
                                S.op("pe", lambda e, i=i, h=h, pu=pu, eqt=eqt, pb0=pb0, npp=npp: e.matmul(
                                    pu[:, 0:NB], lhsT=eqt.v([[1, 128]], off=(h * 8 + i) * 128, p0=pb0, np_=npp),
                                    rhs=XB.v([[8, NB]], off=kc * NT + i, p0=pb0, np_=npp), start=(i == 0), stop=(i == 7)),
                                    reads=[eqt.b[0]] + allxb(kc), writes=[pu.b[0]])
                            act(UB[h][:], pu[:, 0:NB], AF.Identity, [pu.b[0]], [UB[h].b[0]])
                        pdr, pdi = S.banks[1 + 2 * pp], S.banks[2 + 2 * pp]
                        for qt, qm, pd in ((QRT, QMR2[cp], pdr), (QIT, QMI2[cp], pdi)):
                            for h in range(2):
                                hs = slice(64 * h, 64 * h + 64)
                                ev_ = UB[h].v([[2, 128]], off=0)
                                od_ = UB[h].v([[2, 128]], off=1)
                                rd = [qt.b[0], qm.b[0], UB[h].b[0]]
                                S.op("pe", lambda e, qm=qm, pd=pd, hs=hs, ev_=ev_: e.matmul(pd[hs, 0:128], lhsT=qm[:, pl, hs], rhs=ev_,
                                                                                       start=True, stop=False), reads=rd, writes=[pd.b[0]])
                                S.op("pe", lambda e, qt=qt, pd=pd, hs=hs, od_=od_: e.matmul(pd[hs, 0:128], lhsT=qt[:, pl, hs], rhs=od_,
                                                                                       start=False, stop=True), reads=rd, writes=[pd.b[0]])
                                S.op("pe", lambda e, qt=qt, pd=pd, hs=hs, ev_=ev_: e.matmul(pd[hs, 128:256], lhsT=qt[:, pl, hs], rhs=ev_,
                                                                                       start=True, stop=True), reads=rd, writes=[pd.b[0]])
                                S.op("pe", lambda e, qt=qt, pd=pd, hs=hs, h=h: e.matmul(pd[hs, 256:NB], lhsT=qt[:, pl, hs], rhs=UB[h][:, 256:NB],
                                                                                   start=True, stop=True), reads=rd, writes=[pd.b[0]])

                    def pair_a2(kc, pl):
                        cp = kc % 2
                        p = 4 * kc + pl
                        pp = p % 2
                        H0c, OSc = H0c2[cp], OSc2[cp]
                        HPr, HPi = HPr2[pp], HPi2[pp]
                        pdr, pdi = S.banks[1 + 2 * pp], S.banks[2 + 2 * pp]
                        act(Ar[:], pdr[:, 0:NB], AF.Identity, [pdr.b[0]], [Ar.b[0]])
                        act(Ai[:], pdi[:, 0:NB], AF.Identity, [pdi.b[0]], [Ai.b[0]])
                        h0r, h0i = H0c[:, 0, pl * 16:(pl + 1) * 16], H0c[:, 1, pl * 16:(pl + 1) * 16]
                        mur = lambda k: MUr[:, k, p:p + 1]
                        mui = lambda k: MUi[:, k, p:p + 1]
                        mun = lambda k: MUn[:, k, p:p + 1]
                        mb = [MUr.b[0], MUi.b[0], MUn.b[0]]
                        sr, si = Ar[:, 256:NB], Ai[:, 256:NB]
                        stt(sr, h0r, mur(0), sr, ALU.mult, ALU.add, [H0c.b[0], Ar.b[0]] + mb, [Ar.b[0]])
                        stt(sr, h0i, mun(0), sr, ALU.mult, ALU.add, [H0c.b[0], Ar.b[0]] + mb, [Ar.b[0]])
                        stt(si, h0i, mur(0), si, ALU.mult, ALU.add, [H0c.b[0], Ai.b[0]] + mb, [Ai.b[0]])
                        stt(si, h0r, mui(0), si, ALU.mult, ALU.add, [H0c.b[0], Ai.b[0]] + mb, [Ai.b[0]])
                        src_r, src_i, dst_r, dst_i = Ar, Ai, Br, Bi
                        for k in range(7):
                            s_ = 1 << k
                            lo = 0 if k == 0 else s_ // 2
                            rs = [src_r.b[0], src_i.b[0]] + mb
                            stt(dst_r[:, s_:128], src_r[:, 0:128 - s_], mur(k + 1), src_r[:, s_:128], ALU.mult, ALU.add, rs, [dst_r.b[0]])
                            stt(dst_r[:, s_:128], src_i[:, 0:128 - s_], mun(k + 1), dst_r[:, s_:128], ALU.mult, ALU.add, rs + [dst_r.b[0]], [dst_r.b[0]])
                            stt(dst_i[:, s_:128], src_i[:, 0:128 - s_], mur(k + 1), src_i[:, s_:128], ALU.mult, ALU.add, rs, [dst_i.b[0]])
                            stt(dst_i[:, s_:128], src_r[:, 0:128 - s_], mui(k + 1), dst_i[:, s_:128], ALU.mult, ALU.add, rs + [dst_i.b[0]], [dst_i.b[0]])
                            S.op("dve", lambda e, lo=lo, s_=s_, a=src_r, b=dst_r: e.tensor_copy(out=b[:, lo:s_], in_=a[:, lo:s_]),
                                 reads=[src_r.b[0]], writes=[dst_r.b[0]])
                            S.op("dve", lambda e, lo=lo, s_=s_, a=src_i, b=dst_i: e.tensor_copy(out=b[:, lo:s_], in_=a[:, lo:s_]),
                                 reads=[src_i.b[0]], writes=[dst_i.b[0]])
                            src_r, src_i, dst_r, dst_i = dst_r, dst_i, src_r, src_i
                        er, ei = Ar[:, 129:256], Ai[:, 129:256]
                        bb_ = [Br.b[0], Bi.b[0]] + mb
                        stt(er, Br[:, 0:127], mur(0), er, ALU.mult, ALU.add, bb_ + [Ar.b[0]], [Ar.b[0]])
                        stt(er, Bi[:, 0:127], mun(0), er, ALU.mult, ALU.add, bb_ + [Ar.b[0]], [Ar.b[0]])
                        stt(ei, Bi[:, 0:127], mur(0), ei, ALU.mult, ALU.add, bb_ + [Ai.b[0]], [Ai.b[0]])
                        stt(ei, Br[:, 0:127], mui(0), ei, ALU.mult, ALU.add, bb_ + [Ai.b[0]], [Ai.b[0]])
                        act(HPr.v([[2, 128]], off=1), Ar[:, 128:256], AF.Identity, [Ar.b[0]], [HPr.b[0]])
                        act(HPi.v([[2, 128]], off=1), Ai[:, 128:256], AF.Identity, [Ai.b[0]], [HPi.b[0]])
                        act(HPr.v([[2, 127]], off=2), Br[:, 0:127], AF.Identity, [Br.b[0]], [HPr.b[0]])
                        act(HPi.v([[2, 127]], off=2), Bi[:, 0:127], AF.Identity, [Bi.b[0]], [HPi.b[0]])
                        act(HPr[:, 256:NB], h0r, AF.Identity, [H0c.b[0]], [HPr.b[0]])
                        act(HPi[:, 256:NB], h0i, AF.Identity, [H0c.b[0]], [HPi.b[0]])
                        S.op("pool", lambda e: e.tensor_copy(out=OP[:, 0, p:p + 1], in_=Br[:, 127:128]), reads=[Br.b[0]], writes=[OP.b[0]])
                        S.op("pool", lambda e: e.tensor_copy(out=OP[:, 1, p:p + 1], in_=Bi[:, 127:128]), reads=[Bi.b[0]], writes=[OP.b[0]])
                        S.op("pool", lambda e: e.tensor_copy(out=OSc[:, 0, pl * 16:(pl + 1) * 16], in_=Ar[:, 256:NB]), reads=[Ar.b[0]], writes=[OSc.b[0]])
                        S.op("pool", lambda e: e.tensor_copy(out=OSc[:, 1, pl * 16:(pl + 1) * 16], in_=Ai[:, 256:NB]), reads=[Ai.b[0]], writes=[OSc.b[0]])

                    def pair_b(kc, pl):
                        cp = kc % 2
                        p = 4 * kc + pl
                        pp = p % 2
                        TW, PRm, PNm = TW2[cp], PRm2[cp], PNm2[cp]
                        UB, HPr, HPi = UB3[p % 3], HPr2[pp], HPi2[pp]
                        Yb = Yb2[cp]
                        for h in range(2):
                            gm = 2 * pl + h
                            g = 8 * kc + gm
                            py = S.banks[5]
                            TG = TG2[h]
                            S.op("pe", lambda e, gm=gm, h=h, py=py: e.matmul(py[:, 0:NB], lhsT=TW[:, gm, :], rhs=UB[h][:], start=True, stop=False),
                                 reads=[TW.b[0], UB[h].b[0]], writes=[py.b[0]])
                            S.op("pe", lambda e, h=h, py=py: e.matmul(py[:, 0:NB], lhsT=PRm[h][:, pl, :], rhs=HPr[:], start=False, stop=False),
                                 reads=[PRm[h].b[0], HPr.b[0]], writes=[py.b[0]])
                            S.op("pe", lambda e, h=h, py=py: e.matmul(py[:, 0:NB], lhsT=PNm[h][:, pl, :], rhs=HPi[:], start=False, stop=True),
                                 reads=[PNm[h].b[0], HPi.b[0]], writes=[py.b[0]])
                            stt(TG[:], UB[h][:], DB[:, g:g + 1], py[:, 0:NB], ALU.mult, ALU.add, [UB[h].b[0], DB.b[0], py.b[0]], [TG.b[0]])
                            act(Yb[:, gm, :], TG[:], AF.Gelu_apprx_tanh, [TG.b[0]], [Yb.b[gm]])

                    def unblock(kc):
                        OSc = OSc2[kc % 2]
                        Yb = Yb2[kc % 2]
                        for ri in range(2):
                            S.dma("sp", s5s[j, ri, :, kc * 64:(kc + 1) * 64], OSc[:, ri, :], stchan(), reads=[OSc.b[0]])
                        for i in range(8):
                            pb = S.banks[7]
                            q2, ipar = i // 2, i % 2
                            for gm in range(8):
                                ept, pb0, npp = (EPQ, 32 * q2, 32) if q2 < 3 else (EPQz, 64, 64)
                                S.op("pe", lambda e, gm=gm, pb=pb, ept=ept, pb0=pb0, npp=npp, ipar=ipar, Yb=Yb: e.matmul(
                                    pb[:, 0:NB], lhsT=ept.v([[1, 128]], off=(ipar * 8 + gm) * 128, p0=pb0, np_=npp),
                                    rhs=Yb.v([[1, NB]], off=gm * NB, p0=pb0, np_=npp), start=(gm == 0), stop=(gm == 7)),
                                    reads=[ept.b[0], Yb.b[gm]], writes=[pb.b[0]])
                            act(XB.v([[8, NB]], off=kc * NT + i), pb[:, 0:NB], AF.Identity, [pb.b[0]], allxb(kc))

                    def split(lst, k):
                        n_ = len(lst)
                        return [lst[(i * n_) // k:((i + 1) * n_) // k] for i in range(k)]

                    tables(0)
                    pa1 = lambda p_: S.record(lambda: pair_a1(p_ // 4, p_ % 4)) if p_ < 32 else []
                    pa2 = lambda p_: S.record(lambda: pair_a2(p_ // 4, p_ % 4)) if p_ < 32 else []
                    S.replay(pa1(0))
                    tl0 = split(S.record(lambda: tables(1)), 2)
                    S.replay(pa1(1), pa2(0), tl0[0])
                    extra = {0: [tl0[1]]}
                    for r in range(32):
                        kc, pl = r // 4, r % 4
                        if pl == 0 and 1 <= kc and kc + 1 < 8:
                            tl = split(S.record(lambda: tables(kc + 1)), 2)
                            extra.setdefault(r, []).append(tl[0])
                            extra.setdefault(r + 1, []).append(tl[1])
                        if pl == 0 and kc >= 1:
                            ul = split(S.record(lambda: unblock(kc - 1)), 3)
                            for q_ in range(3):
                                extra.setdefault(r + q_, []).append(ul[q_])
                        lb_ = S.record(lambda: pair_b(kc, pl))
                        ex_ = extra.get(r, [])
                        S.replay(pa1(r + 2), pa2(r + 1), lb_, *ex_, weights=[0.6, 1.0, 1.0] + [1.0] * len(ex_))
                    unblock(7)
                    for ri in range(2):
                        S.dma("sp", s5p[j, ri], OP[:, ri, :], stchan(), reads=[OP.b[0]])
                    S.barrier()
                MX = [Tile(S, ph, "s5MX%d" % i, [128, 512], F32) for i in range(2)]
                SGt = [Tile(S, ph, "s5SG%d" % i, [128, 512], F32) for i in range(2)]
                LSQ = Tile(S, ph, "s5LSQ", [128, KC, 512], BF16)
                LM2 = Tile(S, ph, "s5LM2", [128, 512], F32)
                LRS = Tile(S, ph, "s5LRS", [128, 512], F32)
                kk_ = {"i": 0}

                def glu_st(st):
                    t0, n = STS[st]
                    for nn in range(8):
                        w = W.get(("wglu", l, st, nn))
                        pv, pg = S.banks[2 * (nn % 2)], S.banks[2 * (nn % 2) + 1]
                        for vg, bk in ((0, pv), (1, pg)):
                            for kc in range(KC):
                                S.op("pe", lambda e, vg=vg, kc=kc, bk=bk, w=w: e.matmul(
                                    bk[:, 0:n], lhsT=w[:, (vg * 8 + kc) * 128:(vg * 8 + kc + 1) * 128], rhs=XB[:, kc, t0:t0 + n],
                                    start=(kc == 0), stop=(kc == KC - 1)),
                                    reads=[w.b[0]] + xbb(st, [kc]), writes=[bk.b[0]])
                        sg, mx = SGt[kk_["i"] % 2], MX[kk_["i"] % 2]
                        kk_["i"] += 1
                        act(sg[:, 0:n], pg[:, 0:n], AF.Sigmoid, [pg.b[0]], [sg.b[0]])
                        tt("dve", mx[:, 0:n], pv[:, 0:n], sg[:, 0:n], ALU.mult, [pv.b[0], sg.b[0]], [mx.b[0]])
                        xs = XF[:, nn, t0:t0 + n]
                        stt(xs, xs, ALPHA, mx[:, 0:n], ALU.mult, ALU.add, xfb(st, [nn]) + [mx.b[0]], xfb(st, [nn]))

                for r in range(NST + 1):
                    l1 = S.record(lambda: glu_st(r)) if r < NST else []
                    l2 = S.record(lambda: layer_norm((LSQ, None, LM2, LRS), r - 1, l * V_LAYER + 0, l * V_LAYER + 8,
                                                     banks=(S.banks[4], S.banks[5]))) if r >= 1 else []
                    S.replay(l1, l2)
                S.barrier()

        for l in range(n_layers):
            vb = l * V_LAYER
            if l % 2 == 0:
                hgrn_phase(l)
            else:
                s5_phase(l)
            ffn_phase(l, last=(l == n_layers - 1))
        for ev in S.pending.values():
            S._wait("sp", ev)
        S.barrier()
        build_program.stats = (S.nop, dict(S.cnt))
    return nc


def _chunkT(W, kcs=8):
    K, N = W.shape
    return W.reshape(K // 128, 128, N).transpose(1, 0, 2)


def _make_consts():
    c = np.zeros((128, NCONST), np.float32)
    p = np.arange(128)
    c[:, C_ID:C_ID + 128] = np.eye(128, dtype=np.float32)
    c[:, C_CAUS:C_CAUS + 128] = (p[:, None] <= p[None, :])
    c[:, C_BD:C_BD + 128] = (p[:, None] <= p[None, :]) & (p[:, None] // 8 == p[None, :] // 8)
    c[:, C_SEG:C_SEG + 16] = (p[:, None] // 8 == np.arange(16)[None, :])
    c[:, C_R512:C_R512 + 512] = (np.arange(512) % 128 != 0)[None, :]
    c[:, C_R8:C_R8 + 128] = (np.arange(128) % 8 != 0)[None, :]
    c[:, C_TM:C_TM + 128] = (p[None, :] // 16 >= p[:, None] // 16)
    c[:, C_RM] = p < 64
    c[:, C_RM + 1] = p >= 64
    tau = np.arange(16, dtype=np.float32) - 7.0
    c[:, C_TAU:C_TAU + 512] = np.repeat(tau, 32)[None, :]
    return c


def _make_emat():
    EQ = np.zeros((128, 2, 8, 8, 16), np.float32)
    EP = np.zeros((128, 2, 8, 128), np.float32)
    for p in range(128):
        r = p % 32
        par, c = r // 16, r % 16
        for i in range(8):
            EQ[p, par, i, i, c] = 1.0
        for gm in range(8):
            EP[p, par, gm, 16 * gm + c] = 1.0
    EQ = EQ.reshape(128, 2048)
    EP = EP.reshape(128, 2048)
    EQz, EPz = EQ.copy(), EP.copy()
    EQz[64:96] = 0
    EPz[64:96] = 0
    return np.stack([EQ, EP, EQz, EPz])


def _vec(v):
    return np.ascontiguousarray(v.reshape(8, 128).T)


def prep_shared(inp):
    f = np.float32
    sh = {}
    sh["consts"] = _make_consts()
    sh["emat"] = _make_emat()
    vecs = np.zeros((128, NVEC), f)
    for l in range(4):
        for k, name in enumerate(["ln_mix_w", "ln_mix_b", "ln_ffn_w", "ln_ffn_b", "ple_norm_w"]):
            vecs[:, l * V_LAYER + 8 * k: l * V_LAYER + 8 * k + 8] = _vec(inp[name][l])
    for j in range(2):
        vecs[:, V_LB + 8 * j: V_LB + 8 * j + 8] = _vec(inp["hg_lower_bounds"][j])
        vecs[:, V_GN + j] = inp["hg_gnorm_w"][j]
    sh["vecs"] = vecs
    whg = np.zeros((2, 8, 2, 128, 2048), f)
    who = np.zeros((2, 8, 128, 1024), f)
    for j in range(2):
        Wt = _chunkT(inp["hg_w_in"][j])
        for h in range(8):
            for ab in range(2):
                for sub in range(2):
                    part = ab * 2 + sub
                    blk = Wt[:, :, part * 1024 + h * 128: part * 1024 + (h + 1) * 128]
                    whg[j, h, ab, :, sub * 1024:(sub + 1) * 1024] = blk.reshape(128, 1024)
        Wo = _chunkT(inp["hg_w_out"][j])
        for n in range(8):
            who[j, n] = Wo[:, :, n * 128:(n + 1) * 128].reshape(128, 1024)
    sh["whg"], sh["who"] = whg, who
    wglu = np.zeros((2, 8, 128, 2048), f)
    for j in range(2):
        Wt = _chunkT(inp["s5_w_glu"][j])
        for n in range(8):
            for vg in range(2):
                wglu[j, n, :, vg * 1024:(vg + 1) * 1024] = Wt[:, :, vg * 1024 + n * 128: vg * 1024 + (n + 1) * 128].reshape(128, 1024)
    sh["wglu"] = wglu
    wgu = np.zeros((4, 22, 128, 2048), f)
    wdn = np.zeros((4, 3, 8, 128, 1024), f)
    wple = np.zeros((4, 8, 128, 1280), f)
    for l in range(4):
        Wt = _chunkT(inp["ffn_w_gate_up"][l])
        for c in range(22):
            for gu in range(2):
                wgu[l, c, :, gu * 1024:(gu + 1) * 1024] = Wt[:, :, gu * 2816 + c * 128: gu * 2816 + (c + 1) * 128].reshape(128, 1024)
        Wd = _chunkT(inp["ffn_w_down"][l])
        for ps, (c0, c1) in enumerate(FPASS):
            for n in range(8):
                wdn[l, ps, n, :, 0:(c1 - c0) * 128] = Wd[:, c0:c1, n * 128:(n + 1) * 128].reshape(128, -1)
        Wg = _chunkT(inp["ple_w_gate"][l])
        Wp = _chunkT(inp["ple_w_proj"][l])
        for n in range(8):
            wple[l, n, :, 0:1024] = Wg[:, :, n * 128:(n + 1) * 128].reshape(128, 1024)
            wple[l, n, :, 1024:1280] = Wp[:, :, n * 128:(n + 1) * 128].reshape(128, 256)
    sh["wgu"], sh["wdn"], sh["wple"] = wgu, wdn, wple
    s5a = np.zeros((2, 128, 96), f)
    s5b = np.zeros((2, 2, 128, 512), f)
    s5c = np.zeros((2, 2, 128, 512), f)
    s5d = np.zeros((2, 128, 64), f)

    def gp(a):
        sh_ = a.shape
        a = a.reshape((32, 2) + sh_[1:])
        a = np.moveaxis(a, 0, 2)
        return a.reshape((128, 32) + sh_[2:])
    for j in range(2):
        s5a[j, :, 0:32] = gp(inp["s5_a_re"][j])
        s5a[j, :, 32:64] = gp(inp["s5_a_im"][j])
        s5a[j, :, 64:96] = gp(np.repeat(inp["s5_log_step"][j][:, None], 64, axis=1))
        s5b[j, 0] = gp(inp["s5_b_re"][j]).reshape(128, 512)
        s5b[j, 1] = gp(inp["s5_b_im"][j]).reshape(128, 512)
        s5c[j, 0] = gp(inp["s5_c_re"][j].transpose(0, 2, 1)).reshape(128, 512)
        s5c[j, 1] = gp(inp["s5_c_im"][j].transpose(0, 2, 1)).reshape(128, 512)
        dd = inp["s5_d"][j].reshape(64, 16)
        s5d[j] = np.tile(dd.T, (8, 1))
    sh["s5a"], sh["s5b"], sh["s5c"], sh["s5d"] = s5a, s5b, s5c, s5d
    return sh


def prep_core(inp, c):
    m = {}
    xs = inp["x_sample"][16 * c:16 * c + 16].reshape(128, D)
    m["xT"] = np.ascontiguousarray(np.concatenate([inp["x_prompt"][c].T, xs.T], axis=1))
    pp = inp["p_prompt"][:, c].transpose(0, 2, 1)
    ps = inp["p_sample"][:, 16 * c:16 * c + 16].reshape(4, 128, 256).transpose(0, 2, 1)
    m["pT"] = np.ascontiguousarray(np.concatenate([pp, ps], axis=2))
    m["hg0"] = np.ascontiguousarray(inp["state_hgrn"][:, 16 * c:16 * c + 16])
    h0 = np.stack([inp["state_s5_re"][:, 16 * c:16 * c + 16], inp["state_s5_im"][:, 16 * c:16 * c + 16]], axis=1)
    h0 = h0.reshape(2, 2, 16, 32, 2, 64).transpose(0, 1, 4, 5, 3, 2)
    m["s5h0"] = np.ascontiguousarray(h0.reshape(2, 2, 128, 512))
    return m


def assemble(results):
    n = len(results)
    f = np.float32
    y_p = np.zeros((n, NPR, D), f)
    y_s = np.zeros((n * 16, 8, D), f)
    hg_p = np.zeros((2, n, 8, 128, 128), f)
    re_p = np.zeros((2, n, 64, 64), f)
    im_p = np.zeros((2, n, 64, 64), f)
    hg_s = np.zeros((2, n * 16, 8, 128, 128), f)
    re_s = np.zeros((2, n * 16, 64, 64), f)
    im_s = np.zeros((2, n * 16, 64, 64), f)
    for c, r in enumerate(results):
        yT = r["yT"]
        y_p[c] = yT[:, :NPR].T
        y_s[16 * c:16 * c + 16] = yT[:, NPR:].T.reshape(16, 8, D)
        hg_p[:, c] = r["hgp"].transpose(0, 2, 1, 3)
        hg_s[:, 16 * c:16 * c + 16] = r["hgs"]
        sp = r["s5p"].reshape(2, 2, 2, 64, 32)
        sp = sp.transpose(0, 1, 4, 2, 3).reshape(2, 2, 64, 64)
        re_p[:, c], im_p[:, c] = sp[:, 0], sp[:, 1]
        ss = r["s5s"].reshape(2, 2, 2, 64, 32, 16)
        ss = ss.transpose(0, 1, 5, 4, 2, 3).reshape(2, 2, 16, 64, 64)
        re_s[:, 16 * c:16 * c + 16], im_s[:, 16 * c:16 * c + 16] = ss[:, 0], ss[:, 1]
    return (y_p, y_s, hg_p, re_p, im_p, hg_s, re_s, im_s)


_NC_CACHE = {}


def kernel(**inputs):
    inp = {k: np.asarray(v) for k, v in inputs.items()}
    sh = prep_shared(inp)
    in_maps = []
    for c in range(8):
        m = dict(sh)
        m.update(prep_core(inp, c))
        in_maps.append(m)
    if "nc" not in _NC_CACHE:
        _NC_CACHE["nc"] = build_program(4)
    res = run_bass_kernel_spmd(_NC_CACHE["nc"], in_maps, core_ids=list(range(8)))
    return assemble(res.results)
```

```python
import math
from contextlib import ExitStack

import numpy as np
import concourse.bass as bass
import concourse.mybir as mybir
from concourse.ap import AP
from concourse.bass_utils import run_bass_kernel_spmd

F32 = mybir.dt.float32
BF16 = mybir.dt.bfloat16
ALU = mybir.AluOpType
AF = mybir.ActivationFunctionType

D = 1024
KC = 8
NT = 2176
NPR = 2048
NSM = 128
NB = 272
DEPTH = 4
STS = [(0, 512), (512, 512), (1024, 512), (1536, 512), (2048, 128)]
NST = len(STS)
FPASS = [(0, 8), (8, 15), (15, 22)]
ALPHA = float((2 * DEPTH) ** 0.25)
LN_EPS = 1e-5
RMS_EPS = 1e-6
QK = float(128 ** -0.5)
TWO_PI = 2.0 * math.pi
MAGIC = 12582912.0

C_ID, C_CAUS, C_BD, C_SEG, C_TM, C_RM, C_R8 = 0, 128, 256, 384, 400, 528, 530
NCP = 658
C_R512, C_TAU = 658, 1170
NCONST = 1682
V_LAYER = 40
V_LB = 160
V_GN = 176
NVEC = 178


class Buf:
    __slots__ = ("name", "w", "r")

    def __init__(self, name):
        self.name = name
        self.w = None
        self.r = []


class Chan:
    _uid = [0]

    def __init__(self, S, name):
        Chan._uid[0] += 1
        self.sem = S.es.enter_context(S.nc.semaphore("c_%s_%d" % (name, Chan._uid[0])))
        self.cnt = 0


class Tile:
    _uid = [0]

    def __init__(self, S, es, name, shape, dtype, nb=1, psum=False):
        Tile._uid[0] += 1
        name = "%s_%d" % (name, Tile._uid[0])
        if psum:
            self.h = es.enter_context(S.nc.psum_tensor(name, shape, dtype))
        else:
            self.h = es.enter_context(S.nc.sbuf_tensor(name, shape, dtype))
        self.b = [Buf(name + str(i)) for i in range(nb)]
        self.shape = shape
        a = self.h[:]
        self.tensor = a.tensor
        self.pstep = a.ap[0][0]

    def __getitem__(self, k):
        return self.h[k]

    def v(self, dims, off=0, p0=0, np_=128):
        return AP(self.tensor, p0 * self.pstep + off, [[self.pstep, np_]] + [list(d) for d in dims])


class HView:
    def __init__(self, tile, t0, bufs):
        self.t, self.t0, self.b = tile, t0, bufs

    def __getitem__(self, k):
        p, r, c = k
        return self.t[p, r, self.t0 + c.start:self.t0 + c.stop]


class Sched:
    def __init__(self, nc, es):
        self.nc = nc
        self.es = es
        self.eng = {"pe": nc.tensor, "act": nc.scalar, "dve": nc.vector, "pool": nc.gpsimd, "sp": nc.sync}
        self.esem = {}
        self.cnt = {}
        for k in self.eng:
            self.esem[k] = es.enter_context(nc.semaphore("s_" + k))
            self.cnt[k] = 0
        self.seen = {k: {} for k in self.eng}
        self.pending = {}
        self.banks = []
        self.bank_i = 0
        self.nop = 0

    def _wait(self, e, ev):
        if ev is None:
            return
        sem, val = ev
        d = self.seen[e]
        key = id(sem)
        if d.get(key, 0) >= val:
            return
        d[key] = val
        self.eng[e].wait_ge(sem, val)

    def _deps(self, e, reads, writes, same_ok=False):
        own = self.esem[e] if same_ok else None
        for b in reads:
            ev = b.w
            if ev is not None and ev[0] is not own:
                self._wait(e, ev)
        for b in writes:
            ev = b.w
            if ev is not None and ev[0] is not own:
                self._wait(e, ev)
            for ev in b.r:
                if ev[0] is not own:
                    self._wait(e, ev)

    def _record(self, ev, reads, writes):
        for b in reads:
            b.r.append(ev)
            if len(b.r) > 24:
                d = {}
                for s, v in b.r:
                    if d.get(id(s), (None, 0))[1] < v:
                        d[id(s)] = (s, v)
                b.r = list(d.values())
        for b in writes:
            b.w = ev
            b.r = []

    rec = None

    def op(self, e, fn, reads=(), writes=()):
        if self.rec is not None:
            self.rec.append(("op", e, fn, list(reads), list(writes)))
            return None
        self._deps(e, reads, writes, same_ok=(e == "pe"))
        ins = fn(self.eng[e])
        self.cnt[e] += 1
        ins.then_inc(self.esem[e], 1)
        ev = (self.esem[e], self.cnt[e])
        self._record(ev, reads, writes)
        self.nop += 1
        return ev

    def dma(self, q, out, in_, chan, reads=(), writes=()):
        if self.rec is not None:
            self.rec.append(("dma", q, out, in_, chan, list(reads), list(writes)))
            return None
        self._deps(q, reads, writes)
        if chan.cnt > 0:
            self._wait(q, (chan.sem, chan.cnt))
        ins = self.eng[q].dma_start(out=out, in_=in_)
        chan.cnt += 16
        ins.then_inc(chan.sem, 16)
        ev = (chan.sem, chan.cnt)
        self.pending[id(chan.sem)] = ev
        self._record(ev, reads, writes)
        return ev

    def record(self, fn):
        assert self.rec is None
        self.rec = []
        fn()
        r, self.rec = self.rec, None
        return r

    def replay(self, *lists, weights=None):
        if weights is None:
            weights = [1.0] * len(lists)
        weights = [w for l, w in zip(lists, weights) if l]
        lists = [l for l in lists if l]
        pos = [0] * len(lists)
        tot = sum(len(l) for l in lists)
        for _ in range(tot):
            k = min((i for i in range(len(lists)) if pos[i] < len(lists[i])),
                    key=lambda i: weights[i] * (pos[i] + 0.5) / len(lists[i]))
            it = lists[k][pos[k]]
            pos[k] += 1
            if it[0] == "op":
                self.op(it[1], it[2], it[3], it[4])
            else:
                self.dma(it[1], it[2], it[3], it[4], it[5], it[6])

    def barrier(self):
        for e in self.eng:
            for e2 in self.eng:
                if e2 != e and self.cnt[e2] > 0:
                    self._wait(e, (self.esem[e2], self.cnt[e2]))
            for ev in self.pending.values():
                self._wait(e, ev)

    def bank(self):
        t = self.banks[self.bank_i % len(self.banks)]
        self.bank_i += 1
        return t


class WStream:
    NRING = 4
    LA = 2

    def __init__(self, S, es, plan, width=2048, nring=None, la=None):
        self.S = S
        self.plan = plan
        if nring is not None:
            self.NRING = nring
        if la is not None:
            self.LA = la
        self.ring = [Tile(S, es, "wring%d" % i, [128, width], BF16) for i in range(self.NRING)]
        self.chan = [Chan(S, "wr%d" % i) for i in range(self.NRING)]
        self.issued = 0
        self.slot = {}
        self.next = 0

    def _issue(self, c):
        S = self.S
        tag, src, E, dest = self.plan[c]
        r = self.ring[c % self.NRING]
        S.dma("pool", r[:, 0:E], src, self.chan[c % self.NRING], writes=[r.b[0]])
        self.slot[c] = r

    def get(self, tag):
        c = self.next
        assert self.plan[c][0] == tag, (self.plan[c][0], tag)
        self.next += 1
        while self.issued < len(self.plan) and self.issued <= c + self.LA:
            if self.issued > c and (self.issued - max(c - 1, 0)) + 1 > self.NRING:
                break
            self._issue(self.issued)
            self.issued += 1
        return self.slot.pop(c)


def build_program(n_layers=4):
    nc = bass.Bass("TRN2", target_bir_lowering=False)

    def din(name, shape):
        return nc.dram_tensor(name, shape, F32, kind="ExternalInput").ap()

    def dout(name, shape):
        return nc.dram_tensor(name, shape, F32, kind="ExternalOutput").ap()

    xT = din("xT", [D, NT])
    pT = din("pT", [4, 256, NT])
    hg0 = din("hg0", [2, 16, 8, 128, 128])
    s5h0 = din("s5h0", [2, 2, 128, 32 * 16])
    consts = din("consts", [128, NCONST])
    vecs = din("vecs", [128, NVEC])
    emat = din("emat", [4, 128, 2048])
    whg = din("whg", [2, 8, 2, 128, 2048])
    who = din("who", [2, 8, 128, 1024])
    wglu = din("wglu", [2, 8, 128, 2048])
    wgu = din("wgu", [4, 22, 128, 2048])
    wdn = din("wdn", [4, 3, 8, 128, 1024])
    wple = din("wple", [4, 8, 128, 1280])
    s5a = din("s5a", [2, 128, 96])
    s5b = din("s5b", [2, 2, 128, 512])
    s5c = din("s5c", [2, 2, 128, 512])
    s5d = din("s5d", [2, 128, 64])

    yT = dout("yT", [D, NT])
    hgp = dout("hgp", [2, 128, 8, 128])
    hgs = dout("hgs", [2, 16, 8, 128, 128])
    s5p = dout("s5p", [2, 2, 128, 32])
    s5s = dout("s5s", [2, 2, 128, 32 * 16])

    with ExitStack() as es:
        S = Sched(nc, es)
        S.banks = [Tile(S, es, "bank%d" % i, [128, 512], F32, psum=True) for i in range(8)]

        XF = Tile(S, es, "XF", [128, KC, NT], F32, nb=KC * NST)
        XB = Tile(S, es, "XB", [128, KC, NT], BF16, nb=KC * NST)
        CN = Tile(S, es, "CN", [128, NCP], F32)
        VEC = Tile(S, es, "VEC", [128, NVEC], F32)
        ONESB = Tile(S, es, "ONESB", [128, 128], BF16)
        ONESD = Tile(S, es, "ONESD", [128, 128], BF16)
        LBT = Tile(S, es, "LBT", [128, 2, 4, 8], F32)
        EPS = Tile(S, es, "EPS", [128, 2], F32)

        def xfb(st, kcs=range(KC)):
            return [XF.b[kc * NST + st] for kc in kcs]

        def xbb(st, kcs=range(KC)):
            return [XB.b[kc * NST + st] for kc in kcs]

        plan = []
        for l in range(n_layers):
            j = l // 2
            if l % 2 == 0:
                for st in range(NST):
                    for h in range(8):
                        plan.append((("hgA", l, st, h), whg[j, h, 0], 2048, None))
                        plan.append((("hgB", l, st, h), whg[j, h, 1], 2048, None))
                    for n in range(8):
                        plan.append((("who", l, st, n), who[j, n], 1024, None))
            else:
                for st in range(NST):
                    for n in range(8):
                        plan.append((("wglu", l, st, n), wglu[j, n], 2048, None))
            for ps, (c0, c1) in enumerate(FPASS):
                for c in range(c0, c1):
                    plan.append((("wgu", l, c), wgu[l, c], 2048, None))
                E = (c1 - c0) * 128
                if ps < len(FPASS) - 1:
                    for n in range(8):
                        plan.append((("wdn", l, ps, n), wdn[l, ps, n, :, 0:E], E, None))
                else:
                    for st in range(NST):
                        for n in range(8):
                            plan.append((("wdn", l, ps, st, n), wdn[l, ps, n, :, 0:E], E, None))
        W = WStream(S, es, plan)

        ldc = [Chan(S, "ld%d" % i) for i in range(4)]
        stc = [Chan(S, "st%d" % i) for i in range(4)]
        ctr = {"ld": 0, "st": 0}

        def ldchan():
            ctr["ld"] += 1
            return ldc[ctr["ld"] % 4]

        def stchan():
            ctr["st"] += 1
            return stc[ctr["st"] % 4]

        S.dma("sp", CN[:], consts[:, 0:NCP], ldchan(), writes=[CN.b[0]])
        S.dma("sp", VEC[:], vecs, ldchan(), writes=[VEC.b[0]])
        for st, (t0, n) in enumerate(STS):
            S.dma("sp", XF[:, :, t0:t0 + n], xT[:, t0:t0 + n].rearrange("(k p) t -> p k t", p=128),
                  ldchan(), writes=xfb(st))
        S.op("pool", lambda e: e.memset(ONESB[:], 1.0), writes=[ONESB.b[0]])
        S.op("pool", lambda e: e.memset(ONESD[:], 1.0 / D), writes=[ONESD.b[0]])
        S.op("pool", lambda e: e.memset(EPS[:, 0:1], LN_EPS), writes=[EPS.b[0]])
        S.op("pool", lambda e: e.memset(EPS[:, 1:2], RMS_EPS), writes=[EPS.b[0]])
        for bk in S.banks:
            S.op("dve", lambda e, bk=bk: e.memset(bk[:], 0.0), writes=[bk.b[0]])
        for st, (t0, n) in enumerate(STS):
            S.op("act", lambda e, t0=t0, n=n: e.activation(out=XB[:, :, t0:t0 + n], in_=XF[:, :, t0:t0 + n], func=AF.Identity),
                 reads=xfb(st), writes=xbb(st))
        with ExitStack() as ph:
            tmp = Tile(S, ph, "lbtmp", [128, 4, 8], F32)
            b0 = VEC[:, V_LB:V_LB + 8]
            b1 = VEC[:, V_LB + 8:V_LB + 16]
            vb = [VEC.b[0]]
            tb = [tmp.b[0]]
            S.op("dve", lambda e: e.tensor_tensor(out=tmp[:, 0, :], in0=b0, in1=b1, op=ALU.subtract), reads=vb, writes=tb)
            S.op("act", lambda e: e.activation(out=tmp[:, 1, :], in_=tmp[:, 0, :], func=AF.Sigmoid), reads=tb, writes=tb)
            S.op("act", lambda e: e.activation(out=tmp[:, 2, :], in_=tmp[:, 0, :], func=AF.Sigmoid, scale=-1.0), reads=tb, writes=tb)
            S.op("dve", lambda e: e.tensor_tensor(out=LBT[:, 0, 0, :], in0=tmp[:, 1, :], in1=tmp[:, 1, :], op=ALU.subtract),
                 reads=tb, writes=[LBT.b[0]])
            S.op("dve", lambda e: e.tensor_tensor(out=tmp[:, 3, :], in0=tmp[:, 1, :], in1=tmp[:, 2, :], op=ALU.add), reads=tb, writes=tb)
            S.op("dve", lambda e: e.tensor_tensor(out=LBT[:, 1, 0, :], in0=tmp[:, 3, :], in1=tmp[:, 1, :], op=ALU.subtract),
                 reads=tb, writes=[LBT.b[0]])
            for jj in range(2):
                S.op("dve", lambda e, jj=jj: e.tensor_scalar(out=LBT[:, jj, 1, :], in0=LBT[:, jj, 0, :], scalar1=-1.0, scalar2=1.0,
                                                             op0=ALU.mult, op1=ALU.add), reads=[LBT.b[0]], writes=[LBT.b[0]])
                S.op("dve", lambda e, jj=jj: e.tensor_scalar(out=LBT[:, jj, 3, :], in0=LBT[:, jj, 1, :], scalar1=0.5, scalar2=None,
                                                             op0=ALU.mult, op1=ALU.bypass), reads=[LBT.b[0]], writes=[LBT.b[0]])
                S.op("dve", lambda e, jj=jj: e.tensor_tensor(out=LBT[:, jj, 2, :], in0=LBT[:, jj, 0, :], in1=LBT[:, jj, 3, :], op=ALU.add),
                     reads=[LBT.b[0]], writes=[LBT.b[0]])
            S.barrier()

        def mm_group(bank, n, lhs_list, rhs_list, reads):
            k = len(lhs_list)
            for i in range(k):
                S.op("pe", lambda e, i=i: e.matmul(bank[:, 0:n], lhsT=lhs_list[i], rhs=rhs_list[i],
                                                   start=(i == 0), stop=(i == k - 1)),
                     reads=reads[i], writes=[bank.b[0]])

        def colsum_bcast(bank, n, src_tile_ap_fn, reads_fn, lhs=None):
            lhs = lhs or ONESB
            for kc in range(KC):
                S.op("pe", lambda e, kc=kc: e.matmul(bank[:, 0:n], lhsT=lhs[:], rhs=src_tile_ap_fn(kc),
                                                     start=(kc == 0), stop=(kc == KC - 1)),
                     reads=[lhs.b[0]] + reads_fn(kc), writes=[bank.b[0]])

        def layer_norm(ph_t, st, wcol, bcol, banks=None):
            t0, n = STS[st]
            SQ, MEAN, M2, RSTD = ph_t
            xf3 = XF[:, :, t0:t0 + n]
            xb3 = XB[:, :, t0:t0 + n]
            S.op("act", lambda e: e.activation(out=xb3, in_=xf3, func=AF.Identity), reads=xfb(st), writes=xbb(st))
            S.op("act", lambda e: e.activation(out=SQ[:, :, 0:n], in_=xf3, func=AF.Square), reads=xfb(st), writes=SQ.b)
            p1 = banks[0] if banks else S.bank()
            colsum_bcast(p1, n, lambda kc: XB[:, kc, t0:t0 + n], lambda kc: xbb(st, [kc]), lhs=ONESD)
            p2 = banks[1] if banks else S.bank()
            colsum_bcast(p2, n, lambda kc: SQ[:, kc, 0:n], lambda kc: SQ.b, lhs=ONESD)
            S.op("act", lambda e: e.activation(out=M2[:, 0:n], in_=p1[:, 0:n], func=AF.Square), reads=[p1.b[0]], writes=[M2.b[0]])
            S.op("dve", lambda e: e.tensor_tensor(out=RSTD[:, 0:n], in0=p2[:, 0:n], in1=M2[:, 0:n], op=ALU.subtract),
                 reads=[p2.b[0], M2.b[0]], writes=[RSTD.b[0]])
            S.op("act", lambda e: e.activation(out=RSTD[:, 0:n], in_=RSTD[:, 0:n], func=AF.Ln, bias=EPS[:, 0:1]),
                 reads=[RSTD.b[0], EPS.b[0]], writes=[RSTD.b[0]])
            S.op("act", lambda e: e.activation(out=p2[:, 0:n], in_=RSTD[:, 0:n], func=AF.Exp, scale=-0.5),
                 reads=[RSTD.b[0]], writes=[p2.b[0]])
            mb = p1.v([[0, KC], [1, n]])
            rb = p2.v([[0, KC], [1, n]])
            S.op("dve", lambda e: e.tensor_tensor(out=xf3, in0=xf3, in1=mb, op=ALU.subtract),
                 reads=xfb(st) + [p1.b[0]], writes=xfb(st))
            S.op("dve", lambda e: e.tensor_tensor(out=xf3, in0=xf3, in1=rb, op=ALU.mult),
                 reads=xfb(st) + [p2.b[0]], writes=xfb(st))
            for kc in range(KC):
                S.op("act", lambda e, kc=kc: e.activation(out=XF[:, kc, t0:t0 + n], in_=XF[:, kc, t0:t0 + n], func=AF.Identity,
                                                          scale=VEC[:, wcol + kc:wcol + kc + 1], bias=VEC[:, bcol + kc:bcol + kc + 1]),
                     reads=xfb(st, [kc]) + [VEC.b[0]], writes=xfb(st, [kc]))
            S.op("dve", lambda e: e.tensor_copy(out=xb3, in_=xf3), reads=xfb(st), writes=xbb(st))

        def ln_phase(wcol, bcol):
            with ExitStack() as ph:
                sets = []
                for i in range(2):
                    sets.append((Tile(S, ph, "lnSQ", [128, KC, 512], BF16), Tile(S, ph, "lnMEAN", [128, 512], F32),
                                 Tile(S, ph, "lnM2", [128, 512], F32), Tile(S, ph, "lnRSTD", [128, 512], F32)))
                for st in range(NST):
                    layer_norm(sets[st % 2], st, wcol, bcol)
                S.barrier()

        def ffn_phase(l, last):
            vbl = l * V_LAYER
            vb = vbl + 32
            with ExitStack() as ph:
                H = Tile(S, ph, "ffH", [128, 8, NT], BF16, nb=8 * NST)
                SGt = [Tile(S, ph, "ffSG%d" % i, [128, 512], F32) for i in range(2)]
                plan2 = [(("wple", l, st, n), wple[l, n], 1280, None) for st in range(NST) for n in range(8)]
                W2 = WStream(S, ph, plan2, width=1280, nring=3, la=1)
                M2 = Tile(S, ph, "ffM2", [128, 512], F32)
                RSTD = Tile(S, ph, "ffRSTD", [128, 512], F32)
                PS = Tile(S, ph, "plPS", [128, 2, 512], F32)
                PB = Tile(S, ph, "plPB", [128, 2, 512], BF16)
                E = Tile(S, ph, "plE", [128, KC, 512], F32, nb=KC)
                RS = Tile(S, ph, "plRS", [128, 512], F32)
                k = 0

                def stage2_evac(ps, nn, st, po):
                    t0, n = STS[st]
                    xs = XF[:, nn, t0:t0 + n]
                    if ps == 0:
                        S.op("dve", lambda e: e.scalar_tensor_tensor(out=xs, in0=xs, scalar=ALPHA, in1=po[:, 0:n],
                                                                    op0=ALU.mult, op1=ALU.add),
                             reads=xfb(st, [nn]) + [po.b[0]], writes=xfb(st, [nn]))
                    else:
                        S.op("dve", lambda e: e.tensor_tensor(out=xs, in0=xs, in1=po[:, 0:n], op=ALU.add),
                             reads=xfb(st, [nn]) + [po.b[0]], writes=xfb(st, [nn]))

                def hsq(st):
                    return HView(H, STS[st][0], [H.b[ci * NST + st] for ci in range(8)])

                def ple_st(st):
                    t0, n = STS[st]
                    SQ = hsq(st)
                    S.dma("sp", PS[:, :, 0:n], pT[l, :, t0:t0 + n].rearrange("(k p) t -> p k t", p=128), ldchan(),
                          writes=[PS.b[0]])
                    S.op("act", lambda e: e.activation(out=PB[:, :, 0:n], in_=PS[:, :, 0:n], func=AF.Identity), reads=[PS.b[0]], writes=[PB.b[0]])
                    for nn in range(8):
                        w = W2.get(("wple", l, st, nn))
                        pp = S.banks[4 + 2 * (nn % 2)]
                        for k2 in range(2):
                            S.op("pe", lambda e, k2=k2, w=w, pp=pp: e.matmul(pp[:, 0:n], lhsT=w[:, 1024 + k2 * 128: 1024 + (k2 + 1) * 128],
                                                                            rhs=PB[:, k2, 0:n], start=(k2 == 0), stop=(k2 == 1)),
                                 reads=[w.b[0], PB.b[0]], writes=[pp.b[0]])
                        pg = S.banks[5 + 2 * (nn % 2)]
                        for kc in range(KC):
                            S.op("pe", lambda e, kc=kc, w=w, pg=pg: e.matmul(pg[:, 0:n], lhsT=w[:, kc * 128:(kc + 1) * 128],
                                                                            rhs=XB[:, kc, t0:t0 + n], start=(kc == 0), stop=(kc == KC - 1)),
                                 reads=[w.b[0]] + xbb(st, [kc]), writes=[pg.b[0]])
                        sg = SGt[nn % 2]
                        S.op("act", lambda e, sg=sg, pg=pg: e.activation(out=sg[:, 0:n], in_=pg[:, 0:n], func=AF.Sigmoid),
                             reads=[pg.b[0]], writes=[sg.b[0]])
                        S.op("dve", lambda e, sg=sg, pp=pp, nn=nn: e.tensor_tensor(out=E[:, nn, 0:n], in0=sg[:, 0:n], in1=pp[:, 0:n],
                                                                                op=ALU.mult),
                             reads=[sg.b[0], pp.b[0]], writes=[E.b[nn]])
                    S.op("act", lambda e: e.activation(out=SQ[:, :, 0:n], in_=E[:, :, 0:n], func=AF.Square), reads=E.b, writes=SQ.b)
                    p2 = S.banks[4]
                    colsum_bcast(p2, n, lambda kc: SQ[:, kc, 0:n], lambda kc: SQ.b)
                    S.op("act", lambda e: e.activation(out=RS[:, 0:n], in_=p2[:, 0:n], func=AF.Ln, scale=1.0 / D, bias=EPS[:, 1:2]),
                         reads=[p2.b[0], EPS.b[0]], writes=[RS.b[0]])
                    S.op("act", lambda e: e.activation(out=p2[:, 0:n], in_=RS[:, 0:n], func=AF.Exp, scale=-0.5),
                         reads=[RS.b[0]], writes=[p2.b[0]])
                    S.op("dve", lambda e: e.tensor_tensor(out=E[:, :, 0:n], in0=E[:, :, 0:n], in1=p2.v([[0, KC], [1, n]]), op=ALU.mult),
                         reads=E.b + [p2.b[0]], writes=E.b)
                    for nn in range(8):
                        xs = XF[:, nn, t0:t0 + n]
                        S.op("dve", lambda e, nn=nn, xs=xs: e.scalar_tensor_tensor(out=xs, in0=E[:, nn, 0:n],
                                                                                  scalar=VEC[:, vb + nn:vb + nn + 1], in1=xs,
                                                                                  op0=ALU.mult, op1=ALU.add),
                             reads=[E.b[nn], VEC.b[0]] + xfb(st, [nn]), writes=xfb(st, [nn]))
                    if last:
                        S.dma("sp", yT[:, t0:t0 + n].rearrange("(k p) t -> p k t", p=128), XF[:, :, t0:t0 + n], stchan(),
                              reads=xfb(st))
                    else:
                        S.op("act", lambda e: e.activation(out=XB[:, :, t0:t0 + n], in_=XF[:, :, t0:t0 + n], func=AF.Identity),
                             reads=xfb(st), writes=xbb(st))

                for ps, (c0, c1) in enumerate(FPASS):
                    ncp = c1 - c0
                    for ci in range(ncp):
                        w = W.get(("wgu", l, c0 + ci))
                        for st, (t0, n) in enumerate(STS):
                            pg = S.bank()
                            pu = S.bank()
                            for gu, bk in ((0, pg), (1, pu)):
                                for kc in range(KC):
                                    S.op("pe", lambda e, gu=gu, kc=kc, bk=bk: e.matmul(
                                        bk[:, 0:n], lhsT=w[:, (gu * 8 + kc) * 128:(gu * 8 + kc + 1) * 128],
                                        rhs=XB[:, kc, t0:t0 + n], start=(kc == 0), stop=(kc == KC - 1)),
                                        reads=[w.b[0]] + xbb(st, [kc]), writes=[bk.b[0]])
                            sg = SGt[k % 2]
                            k += 1
                            S.op("act", lambda e, sg=sg, pg=pg: e.activation(out=sg[:, 0:n], in_=pg[:, 0:n], func=AF.Silu),
                                 reads=[pg.b[0]], writes=[sg.b[0]])
                            S.op("dve", lambda e, sg=sg, pu=pu, ci=ci: e.tensor_tensor(out=H[:, ci, t0:t0 + n], in0=sg[:, 0:n],
                                                                                    in1=pu[:, 0:n], op=ALU.mult),
                                 reads=[sg.b[0], pu.b[0]], writes=[H.b[ci * NST + st]])
                    if ps < len(FPASS) - 1:
                        for nn in range(8):
                            w = W.get(("wdn", l, ps, nn))
                            for st, (t0, n) in enumerate(STS):
                                po = S.bank()
                                for ci in range(ncp):
                                    S.op("pe", lambda e, ci=ci: e.matmul(po[:, 0:n], lhsT=w[:, ci * 128:(ci + 1) * 128],
                                                                        rhs=H[:, ci, t0:t0 + n], start=(ci == 0), stop=(ci == ncp - 1)),
                                         reads=[w.b[0], H.b[ci * NST + st]], writes=[po.b[0]])
                                stage2_evac(ps, nn, st, po)
                    else:
                        def s2_st(st):
                            t0, n = STS[st]
                            for nn in range(8):
                                w = W.get(("wdn", l, ps, st, nn))
                                po = S.banks[nn % 2]
                                for ci in range(ncp):
                                    S.op("pe", lambda e, ci=ci, w=w, po=po: e.matmul(po[:, 0:n], lhsT=w[:, ci * 128:(ci + 1) * 128],
                                                                                    rhs=H[:, ci, t0:t0 + n], start=(ci == 0), stop=(ci == ncp - 1)),
                                         reads=[w.b[0], H.b[ci * NST + st]], writes=[po.b[0]])
                                stage2_evac(ps, nn, st, po)

                        for r in range(NST + 2):
                            l1 = S.record(lambda: s2_st(r)) if r < NST else []
                            l2 = S.record(lambda: layer_norm((hsq(r - 1), None, M2, RSTD), r - 1, vbl + 16, vbl + 24,
                                                             banks=(S.banks[2], S.banks[3]))) if 0 <= r - 1 < NST else []
                            l3 = S.record(lambda: ple_st(r - 2)) if 0 <= r - 2 < NST else []
                            S.replay(l1, l2, l3, weights=[0.6, 1.0, 1.0])
                S.barrier()

        def hgrn_phase(l):
            j = l // 2
            lbA = lambda h: LBT[:, j, 2, h:h + 1]
            omA = lambda h: LBT[:, j, 3, h:h + 1]
            gw = VEC[:, V_GN + j:V_GN + j + 1]
            with ExitStack() as ph:
                def t32(name, w=512, nb=1):
                    return Tile(S, ph, name, [128, w], F32, nb=nb)

                def t16(name, w=512, nb=1):
                    return Tile(S, ph, name, [128, w], BF16, nb=nb)
                SGf, Fm, LF = [t32("hg" + n) for n in "SGf Fm LF".split()]
                KK2 = [t32("hgKK") for i in range(2)]
                Bc2 = [t32("hgBc") for i in range(2)]
                Q12 = [t32("hgQ1") for i in range(2)]
                RV = t32("hgRV", 16)
                BR, EB, KD = [t32("hg" + n) for n in "BR EB KD".split()]
                EQ = BR
                QT = t16("hgQT")
                KT = [t16("hgKT%d" % i) for i in range(4)]
                KTf = t32("hgKTf")
                KDT2 = [t16("hgKDT") for i in range(2)]
                QD2 = [t16("hgQD") for i in range(2)]
                AT2 = [t16("hgATm") for i in range(2)]
                VT3 = [t16("hgVT") for i in range(3)]
                SGG3 = [t16("hgSGG") for i in range(3)]
                EBE2 = [t32("hgEBE", 16) for i in range(2)]
                OSQ = t16("hgOSQ")
                RS, T1 = t32("hgRS"), t32("hgT1")
                ST4 = Tile(S, ph, "hgST4", [128, 3, 128], F32)
                STB4 = Tile(S, ph, "hgSTB4", [128, 4, 128], BF16)
                OG = Tile(S, ph, "hgOG", [128, 8, 512], BF16, nb=8)
                ST = Tile(S, ph, "hgS", [128, 8, 128], F32, nb=8)
                S0 = Tile(S, ph, "hgS0", [128, 8, 128], F32)
                S0B = Tile(S, ph, "hgS0B", [128, 8, 128], BF16)
                SN = Tile(S, ph, "hgSN", [128, 8, 128], F32)
                VM2 = [t16("hgVM", 2048) for i in range(2)]
                R512 = t32("hgR512")
                S.dma("sp", R512[:], consts[:, C_R512:C_R512 + 512], ldchan(), writes=[R512.b[0]])
                S.op("pool", lambda e: e.memset(ST[:], 0.0), writes=ST.b)
                S.op("pool", lambda e: e.memset(RV[:], 0.0), writes=[RV.b[0]])
                for kt in KT:
                    S.op("pool", lambda e, kt=kt: e.memset(kt[:], 0.0), writes=[kt.b[0]])
                cb = [CN.b[0]]
                rr = {"i": 0}

                def bank6():
                    b = S.banks[rr["i"] % 6]
                    rr["i"] += 1
                    return b

                def stage_a(st, h, par):
                    stage_a1(st, h, par)
                    stage_a2(st, h, par)

                def stage_a1(st, h, par):
                    t0, n = STS[st]
                    sample = (st == NST - 1)
                    ntile = n // 128
                    QD, ATm, VT, SGG, EBE = QD2[par], AT2[par], VT3[h % 3], SGG3[h % 3], EBE2[par]
                    KK, Bc, Q1 = KK2[par], Bc2[par], Q12[par]
                    wA = W.get(("hgA", l, st, h))
                    wB = W.get(("hgB", l, st, h))
                    pq, pf, pgt, pv = S.banks[0], S.banks[1], S.banks[2], S.banks[3]
                    for wsel, sub, bk in ((wA, 1, pf), (wA, 0, pq), (wB, 1, pgt)):
                        for kc in range(KC):
                            S.op("pe", lambda e, wsel=wsel, sub=sub, bk=bk, kc=kc: e.matmul(
                                bk[:, 0:n], lhsT=wsel[:, (sub * 8 + kc) * 128:(sub * 8 + kc + 1) * 128],
                                rhs=XB[:, kc, t0:t0 + n], start=(kc == 0), stop=(kc == KC - 1)),
                                reads=[wsel.b[0]] + xbb(st, [kc]), writes=[bk.b[0]])
                    for tt_ in range(ntile):
                        for kc in range(KC):
                            S.op("pe", lambda e, tt_=tt_, kc=kc: e.matmul(
                                pv[:, tt_ * 128:(tt_ + 1) * 128], lhsT=XB[:, kc, t0 + tt_ * 128:t0 + (tt_ + 1) * 128],
                                rhs=wB[:, kc * 128:(kc + 1) * 128], start=(kc == 0), stop=(kc == KC - 1)),
                                reads=[wB.b[0]] + xbb(st, [kc]), writes=[pv.b[0]])
                    S.op("act", lambda e: e.activation(out=SGf[:, 0:n], in_=pf[:, 0:n], func=AF.Tanh, scale=0.5), reads=[pf.b[0]], writes=[SGf.b[0]])
                    S.op("act", lambda e: e.activation(out=Q1[:, 0:n], in_=pq[:, 0:n], func=AF.Silu), reads=[pq.b[0]], writes=[Q1.b[0]])
                    S.op("act", lambda e: e.activation(out=SGG[:, 0:n], in_=pgt[:, 0:n], func=AF.Silu), reads=[pgt.b[0]], writes=[SGG.b[0]])
                    S.op("act", lambda e: e.activation(out=VT[:, 0:n], in_=pv[:, 0:n], func=AF.Identity), reads=[pv.b[0]], writes=[VT.b[0]])
                    S.op("dve", lambda e: e.tensor_scalar(out=Fm[:, 0:n], in0=SGf[:, 0:n], scalar1=omA(h), scalar2=lbA(h),
                                                          op0=ALU.mult, op1=ALU.add),
                         reads=[SGf.b[0], LBT.b[0]], writes=[Fm.b[0]])
                    S.op("act", lambda e: e.activation(out=LF[:, 0:n], in_=Fm[:, 0:n], func=AF.Ln), reads=[Fm.b[0]], writes=[LF.b[0]])
                    S.op("pool", lambda e: e.tensor_scalar(out=KK[:, 0:n], in0=Fm[:, 0:n], scalar1=-1.0, scalar2=1.0,
                                                           op0=ALU.mult, op1=ALU.add), reads=[Fm.b[0]], writes=[KK.b[0]])
                    rst = CN[:, C_R8:C_R8 + 128] if sample else R512[:, 0:n]
                    S.op("dve", lambda e: e.tensor_tensor_scan(out=Bc[:, 0:n], data0=rst, data1=LF[:, 0:n], initial=0.0,
                                                               op0=ALU.mult, op1=ALU.add),
                         reads=[LF.b[0], R512.b[0]] + cb, writes=[Bc.b[0]])

                def stage_a2(st, h, par):
                    t0, n = STS[st]
                    sample = (st == NST - 1)
                    ntile = n // 128
                    QD, ATm, VT, SGG, EBE = QD2[par], AT2[par], VT3[h % 3], SGG3[h % 3], EBE2[par]
                    KK, Bc, Q1 = KK2[par], Bc2[par], Q12[par]
                    KDT, VM = KDT2[par], VM2[par]
                    psu = S.banks[6 + par]
                    S.op("act", lambda e: e.activation(out=EB[:, 0:n], in_=Bc[:, 0:n], func=AF.Exp), reads=[Bc.b[0]], writes=[EB.b[0]])
                    S.op("dve", lambda e: e.scalar_tensor_tensor(out=QD[:, 0:n], in0=Q1[:, 0:n], scalar=QK, in1=EB[:, 0:n],
                                                                op0=ALU.mult, op1=ALU.mult),
                         reads=[Q1.b[0], EB.b[0]], writes=[QD.b[0]])
                    L = 8 if sample else 128
                    nseg = n // L
                    bend = Bc.v([[L, nseg], [0, L]], off=L - 1)
                    b3 = Bc.v([[L, nseg], [1, L]])
                    S.op("pool", lambda e: e.tensor_tensor(out=KD.v([[L, nseg], [1, L]]), in0=bend, in1=b3, op=ALU.subtract),
                         reads=[Bc.b[0]], writes=[KD.b[0]])
                    S.op("act", lambda e: e.activation(out=KD[:, 0:n], in_=KD[:, 0:n], func=AF.Exp), reads=[KD.b[0]], writes=[KD.b[0]])
                    S.op("pool", lambda e: e.tensor_tensor(out=KD[:, 0:n], in0=KD[:, 0:n], in1=KK[:, 0:n], op=ALU.mult),
                         reads=[KD.b[0], KK.b[0]], writes=[KD.b[0]])
                    S.op("act", lambda e: e.activation(out=EBE[:, 0:nseg], in_=Bc.v([[L, nseg]], off=L - 1), func=AF.Exp),
                         reads=[Bc.b[0]], writes=[EBE.b[0]])
                    pk = S.banks[5]
                    for tt_ in range(ntile):
                        S.op("pe", lambda e, tt_=tt_: e.transpose(out=pk[:, tt_ * 128:(tt_ + 1) * 128], in_=KD[:, tt_ * 128:(tt_ + 1) * 128],
                                                                  identity=CN[:, C_ID:C_ID + 128]),
                             reads=[KD.b[0]] + cb, writes=[pk.b[0]])
                    S.op("act", lambda e: e.activation(out=KDT[:, 0:n], in_=pk[:, 0:n], func=AF.Identity), reads=[pk.b[0]], writes=[KDT.b[0]])
                    pa = S.banks[5]
                    if not sample:
                        for tt_ in range(ntile):
                            cs = slice(tt_ * 128, (tt_ + 1) * 128)
                            S.op("pe", lambda e, cs=cs: e.matmul(psu[:, cs], lhsT=KDT[:, cs], rhs=VT[:, cs], start=True, stop=True),
                                 reads=[KDT.b[0], VT.b[0]], writes=[psu.b[0]])
                        S.op("pool", lambda e: e.tensor_copy(out=RV.v([[4, ntile], [1, 3]], off=1),
                                                             in_=Bc.v([[128, ntile], [32, 3]], off=31)),
                             reads=[Bc.b[0]], writes=[RV.b[0]])
                        S.op("pool", lambda e: e.tensor_tensor(out=BR.v([[32, 16], [1, 32]]), in0=Bc.v([[32, 16], [1, 32]]),
                                                               in1=RV.v([[1, 16], [0, 32]]), op=ALU.subtract),
                             reads=[Bc.b[0], RV.b[0]], writes=[BR.b[0]])
                        S.op("act", lambda e: e.activation(out=EQ[:, 0:n], in_=BR[:, 0:n], func=AF.Exp), reads=[BR.b[0]], writes=[EQ.b[0]])
                        S.op("dve", lambda e: e.scalar_tensor_tensor(out=QT[:, 0:n], in0=Q1[:, 0:n], scalar=QK, in1=EQ[:, 0:n],
                                                                    op0=ALU.mult, op1=ALU.mult),
                             reads=[Q1.b[0], EQ.b[0]], writes=[QT.b[0]])
                        for i in range(4):
                            wd_ = 32 * (i + 1)
                            kf = KTf
                            S.op("pool", lambda e, i=i, wd_=wd_, kf=kf: e.tensor_tensor(
                                out=kf.v([[128, ntile], [1, wd_]]), in0=RV.v([[4, ntile], [0, wd_]], off=i),
                                in1=Bc.v([[128, ntile], [1, wd_]]), op=ALU.subtract),
                                reads=[RV.b[0], Bc.b[0]], writes=[kf.b[0]])
                            S.op("act", lambda e, wd_=wd_, kf=kf: e.activation(out=kf.v([[128, ntile], [1, wd_]]),
                                                                              in_=kf.v([[128, ntile], [1, wd_]]), func=AF.Exp),
                                 reads=[kf.b[0]], writes=[kf.b[0]])
                            S.op("dve", lambda e, i=i, wd_=wd_, kf=kf: e.tensor_tensor(
                                out=KT[i].v([[128, ntile], [1, wd_]]), in0=kf.v([[128, ntile], [1, wd_]]),
                                in1=KK.v([[128, ntile], [1, wd_]]), op=ALU.mult),
                                reads=[kf.b[0], KK.b[0]], writes=[KT[i].b[0]])
                        for tt_ in range(ntile):
                            for i in range(4):
                                c0 = tt_ * 128 + 32 * i
                                S.op("pe", lambda e, tt_=tt_, i=i, c0=c0: e.matmul(pa[:, c0:c0 + 32], lhsT=KT[i][:, tt_ * 128:(tt_ + 1) * 128],
                                                                                 rhs=QT[:, c0:c0 + 32], start=True, stop=True),
                                     reads=[KT[i].b[0], QT.b[0]], writes=[pa.b[0]])
                        S.op("dve", lambda e: e.tensor_tensor(out=ATm.v([[128, ntile], [1, 128]]), in0=pa.v([[128, ntile], [1, 128]]),
                                                              in1=CN.v([[0, ntile], [1, 128]], off=C_CAUS), op=ALU.mult),
                             reads=[pa.b[0]] + cb, writes=[ATm.b[0]])
                    else:
                        kf = KTf
                        S.op("act", lambda e: e.activation(out=kf[:, 0:n], in_=Bc[:, 0:n], func=AF.Exp, scale=-1.0),
                             reads=[Bc.b[0]], writes=[kf.b[0]])
                        S.op("dve", lambda e: e.tensor_tensor(out=KT[0][:, 0:n], in0=kf[:, 0:n], in1=KK[:, 0:n], op=ALU.mult),
                             reads=[kf.b[0], KK.b[0]], writes=[KT[0].b[0]])
                        S.op("pe", lambda e: e.matmul(pa[:, 0:128], lhsT=KT[0][:, 0:128], rhs=QD[:, 0:128], start=True, stop=True),
                             reads=[KT[0].b[0], QD.b[0]], writes=[pa.b[0]])
                        S.op("dve", lambda e: e.tensor_tensor(out=ATm[:, 0:128], in0=pa[:, 0:128], in1=CN[:, C_BD:C_BD + 128], op=ALU.mult),
                             reads=[pa.b[0]] + cb, writes=[ATm.b[0]])
                        S.op("dve", lambda e: e.tensor_tensor(out=VM.v([[128, 16], [1, 128]]), in0=VT.v([[0, 16], [1, 128]]),
                                                              in1=CN.v([[1, 16], [0, 128]], off=C_SEG), op=ALU.mult),
                             reads=[VT.b[0]] + cb, writes=[VM.b[0]])

                def stage_b(st, h, par):
                    t0, n = STS[st]
                    sample = (st == NST - 1)
                    ntile = n // 128
                    QD, ATm, VT, SGG, EBE = QD2[par], AT2[par], VT3[h % 3], SGG3[h % 3], EBE2[par]
                    psu = S.banks[6 + par]
                    KDT, VM = KDT2[par], VM2[par]
                    po = S.banks[4]
                    if not sample:
                        S.op("act", lambda e: e.activation(out=STB4[:, 0, :], in_=ST[:, h, :], func=AF.Identity),
                             reads=[ST.b[h]], writes=[STB4.b[0]])
                        for tt_ in range(ntile):
                            src = ST[:, h, :] if tt_ == 0 else ST4[:, tt_ - 1, :]
                            dst = ST[:, h, :] if tt_ == ntile - 1 else ST4[:, tt_, :]
                            S.op("dve", lambda e, tt_=tt_, src=src, dst=dst: e.scalar_tensor_tensor(
                                out=dst, in0=src, scalar=EBE[:, tt_:tt_ + 1], in1=psu[:, tt_ * 128:(tt_ + 1) * 128],
                                op0=ALU.mult, op1=ALU.add),
                                reads=[ST.b[h], ST4.b[0], EBE.b[0], psu.b[0], STB4.b[0]], writes=[ST.b[h], ST4.b[0]])
                        S.op("act", lambda e: e.activation(out=STB4[:, 1:4, :], in_=ST4[:, 0:3, :], func=AF.Identity),
                             reads=[ST4.b[0]], writes=[STB4.b[0]])
                        for tt_ in range(ntile):
                            cs = slice(tt_ * 128, (tt_ + 1) * 128)
                            S.op("pe", lambda e, cs=cs: e.matmul(po[:, cs], lhsT=VT[:, cs], rhs=ATm[:, cs], start=True, stop=False),
                                 reads=[VT.b[0], ATm.b[0]], writes=[po.b[0]])
                            S.op("pe", lambda e, cs=cs, tt_=tt_: e.matmul(po[:, cs], lhsT=STB4[:, tt_, :], rhs=QD[:, cs], start=False, stop=True),
                                 reads=[STB4.b[0], QD.b[0]], writes=[po.b[0]])
                        if st == NST - 2:
                            S.dma("sp", hgp[j, :, h, :], ST[:, h, :], stchan(), reads=[ST.b[h]])
                    else:
                        S.op("pe", lambda e: e.matmul(po[:, 0:128], lhsT=VT[:, 0:128], rhs=ATm[:, 0:128], start=True, stop=False),
                             reads=[VT.b[0], ATm.b[0]], writes=[po.b[0]])
                        for hf in range(2):
                            S.dma("sp", S0[:], hg0[j, hf * 8:(hf + 1) * 8, h].rearrange("b k v -> k b v"), ldchan(), writes=[S0.b[0]])
                            S.op("act", lambda e: e.activation(out=S0B[:], in_=S0[:], func=AF.Identity), reads=[S0.b[0]], writes=[S0B.b[0]])
                            for bb in range(8):
                                b = hf * 8 + bb
                                S.op("pe", lambda e, bb=bb, b=b: e.matmul(po[:, b * 8:(b + 1) * 8], lhsT=S0B[:, bb, :],
                                                                       rhs=QD[:, b * 8:(b + 1) * 8], start=False, stop=(b == 15)),
                                     reads=[S0B.b[0], QD.b[0]], writes=[po.b[0]])
                            pss2 = [S.banks[6], S.banks[7]]
                            for q in range(2):
                                S.op("pe", lambda e, q=q, hf=hf: e.matmul(pss2[q][:, 0:512], lhsT=KDT[:, 0:128],
                                                                       rhs=VM[:, hf * 1024 + q * 512: hf * 1024 + (q + 1) * 512],
                                                                       start=True, stop=True),
                                     reads=[KDT.b[0], VM.b[0]], writes=[pss2[q].b[0]])
                            S.op("pool", lambda e, hf=hf: e.tensor_tensor(out=SN[:], in0=S0[:],
                                                                       in1=EBE.v([[1, 8], [0, 128]], off=hf * 8), op=ALU.mult),
                                 reads=[S0.b[0], EBE.b[0]], writes=[SN.b[0]])
                            for q in range(2):
                                S.op("dve", lambda e, q=q: e.tensor_tensor(out=SN[:, q * 4:(q + 1) * 4, :], in0=SN[:, q * 4:(q + 1) * 4, :],
                                                                        in1=pss2[q].v([[128, 4], [1, 128]]), op=ALU.add),
                                     reads=[SN.b[0], pss2[q].b[0]], writes=[SN.b[0]])
                            S.dma("sp", hgs[j, hf * 8:(hf + 1) * 8, h].rearrange("b k v -> k b v"), SN[:], stchan(), reads=[SN.b[0]])
                    S.op("act", lambda e: e.activation(out=OSQ[:, 0:n], in_=po[:, 0:n], func=AF.Square), reads=[po.b[0]], writes=[OSQ.b[0]])
                    pss = psu if not sample else S.banks[6]
                    S.op("pe", lambda e: e.matmul(pss[:, 0:n], lhsT=ONESB[:], rhs=OSQ[:, 0:n], start=True, stop=True),
                         reads=[ONESB.b[0], OSQ.b[0]], writes=[pss.b[0]])
                    S.op("act", lambda e: e.activation(out=RS[:, 0:n], in_=pss[:, 0:n], func=AF.Ln, scale=1.0 / 128, bias=EPS[:, 1:2]),
                         reads=[pss.b[0], EPS.b[0]], writes=[RS.b[0]])
                    S.op("act", lambda e: e.activation(out=RS[:, 0:n], in_=RS[:, 0:n], func=AF.Exp, scale=-0.5),
                         reads=[RS.b[0]], writes=[RS.b[0]])
                    S.op("dve", lambda e: e.tensor_tensor(out=T1[:, 0:n], in0=po[:, 0:n], in1=RS[:, 0:n], op=ALU.mult),
                         reads=[po.b[0], RS.b[0]], writes=[T1.b[0]])
                    S.op("dve", lambda e: e.scalar_tensor_tensor(out=OG[:, h, 0:n], in0=T1[:, 0:n], scalar=gw, in1=SGG[:, 0:n],
                                                                op0=ALU.mult, op1=ALU.mult),
                         reads=[T1.b[0], SGG.b[0], VEC.b[0]], writes=[OG.b[h]])

                def out_proj(st):
                    t0, n = STS[st]
                    for nn in range(8):
                        w = W.get(("who", l, st, nn))
                        po = bank6()
                        for h in range(8):
                            S.op("pe", lambda e, h=h: e.matmul(po[:, 0:n], lhsT=w[:, h * 128:(h + 1) * 128], rhs=OG[:, h, 0:n],
                                                              start=(h == 0), stop=(h == 7)),
                                 reads=[w.b[0], OG.b[h]], writes=[po.b[0]])
                        xs = XF[:, nn, t0:t0 + n]
                        S.op("dve", lambda e, xs=xs: e.scalar_tensor_tensor(out=xs, in0=xs, scalar=ALPHA, in1=po[:, 0:n],
                                                                           op0=ALU.mult, op1=ALU.add),
                             reads=xfb(st, [nn]) + [po.b[0]], writes=xfb(st, [nn]))

                for st in range(NST):
                    if True:
                        stage_a1(st, 0, 0)
                        S.replay(S.record(lambda: stage_a1(st, 1, 1)), S.record(lambda: stage_a2(st, 0, 0)))
                        for h in range(8):
                            l1 = S.record(lambda: stage_a1(st, h + 2, (h + 2) % 2)) if h + 2 < 8 else []
                            l2 = S.record(lambda: stage_a2(st, h + 1, (h + 1) % 2)) if h + 1 < 8 else []
                            l3 = S.record(lambda: stage_b(st, h, h % 2))
                            S.replay(l1, l2, l3, weights=[0.6, 1.0, 1.0])
                    out_proj(st)
                    layer_norm((OG, None, SGf, Fm), st, l * V_LAYER + 0, l * V_LAYER + 8)
                S.barrier()

        def s5_phase(l):
            j = l // 2
            PI_S = 3.1415925

            def tt(eng, out, a, b, op, r, w):
                S.op(eng, lambda e: e.tensor_tensor(out=out, in0=a, in1=b, op=op), reads=r, writes=w)

            def ts(eng, out, a, s1, s2, op0, op1, r, w):
                S.op(eng, lambda e: e.tensor_scalar(out=out, in0=a, scalar1=s1, scalar2=s2, op0=op0, op1=op1), reads=r, writes=w)

            def stt(out, a, sc, b, op0, op1, r, w):
                S.op("dve", lambda e: e.scalar_tensor_tensor(out=out, in0=a, scalar=sc, in1=b, op0=op0, op1=op1), reads=r, writes=w)

            def act(out, in_, func, r, w, **kw):
                S.op("act", lambda e: e.activation(out=out, in_=in_, func=func, **kw), reads=r, writes=w)

            with ExitStack() as ph:
                EQ = Tile(S, ph, "s5EQ", [128, 2048], BF16)
                EPQ = Tile(S, ph, "s5EPQ", [128, 2048], BF16)
                EQz = Tile(S, ph, "s5EQz", [128, 2048], BF16)
                EPQz = Tile(S, ph, "s5EPQz", [128, 2048], BF16)
                PAr, PAi, PBr, PBi = [Tile(S, ph, "s5P" + n, [128, 16, 32], F32) for n in "Ar Ai Br Bi".split()]
                BBR, BBI, CR, CI = [Tile(S, ph, "s5" + n, [128, 32, 16], F32) for n in "BBR BBI CR CI".split()]
                MUr, MUi, MUn = [Tile(S, ph, "s5MU" + n, [128, 8, 32], F32) for n in "r i n".split()]
                DB = Tile(S, ph, "s5DB", [128, 64], F32)
                OP = Tile(S, ph, "s5OP", [128, 2, 32], F32)
                cb = [CN.b[0]]
                S.dma("sp", DB[:], s5d[j], ldchan(), writes=[DB.b[0]])
                S.dma("sp", CR[:], s5c[j, 0].rearrange("p (a c) -> p a c", c=16), ldchan(), writes=[CR.b[0]])
                S.dma("sp", CI[:], s5c[j, 1].rearrange("p (a c) -> p a c", c=16), ldchan(), writes=[CI.b[0]])
                with ExitStack() as su:
                    A3 = Tile(S, su, "s5A3", [128, 96], F32)
                    BRE = Tile(S, su, "s5BRE", [128, 32, 16], F32)
                    BIM = Tile(S, su, "s5BIM", [128, 32, 16], F32)
                    EST = Tile(S, su, "s5EST", [128, 2048], F32)
                    ANG, T1, T2, SNt, CSt, MG = [Tile(S, su, "s5" + n, [128, 512], F32) for n in "ANG T1 T2 SN CS MG".split()]
                    SM = Tile(S, su, "s5SM", [128, 16, 32], F32)
                    TAU = Tile(S, su, "s5TAU", [128, 512], F32)
                    S.dma("sp", TAU[:], consts[:, C_TAU:C_TAU + 512], ldchan(), writes=[TAU.b[0]])
                    S.dma("sp", A3[:], s5a[j], ldchan(), writes=[A3.b[0]])
                    S.dma("sp", BRE[:], s5b[j, 0].rearrange("p (a c) -> p a c", c=16), ldchan(), writes=[BRE.b[0]])
                    S.dma("sp", BIM[:], s5b[j, 1].rearrange("p (a c) -> p a c", c=16), ldchan(), writes=[BIM.b[0]])
                    for k, dst in enumerate((EQ, EPQ, EQz, EPQz)):
                        S.dma("sp", EST[:], emat[k], ldchan(), writes=[EST.b[0]])
                        S.op("pool", lambda e, dst=dst: e.tensor_copy(out=dst[:], in_=EST[:]), reads=[EST.b[0]], writes=[dst.b[0]])
                    sm = [SM.b[0]]
                    DL, LR, LI = SM[:, 0, :], SM[:, 1, :], SM[:, 2, :]
                    act(DL, A3[:, 64:96], AF.Exp, [A3.b[0]], sm)
                    tt("dve", LR, A3[:, 0:32], DL, ALU.mult, [A3.b[0]] + sm, sm)
                    tt("dve", LI, A3[:, 32:64], DL, ALU.mult, [A3.b[0]] + sm, sm)
                    tau = TAU[:]
                    for sg, (Pr, Pi) in ((1.0, (PAr, PAi)), (-1.0, (PBr, PBi))):
                        stt(ANG[:], tau, sg, SM.v([[0, 16], [1, 32]], off=2 * 32), ALU.mult, ALU.mult, [TAU.b[0]] + sm, [ANG.b[0]])
                        stt(MG[:], tau, sg, SM.v([[0, 16], [1, 32]], off=1 * 32), ALU.mult, ALU.mult, [TAU.b[0]] + sm, [MG.b[0]])
                        act(MG[:], MG[:], AF.Exp, [MG.b[0]], [MG.b[0]])
                        for which, dstt in ((0, SNt), (1, CSt)):
                            if which == 1:
                                ts("dve", ANG[:], ANG[:], math.pi / 2, None, ALU.add, ALU.bypass, [ANG.b[0]], [ANG.b[0]])
                            ts("dve", T1[:], ANG[:], 1.0 / TWO_PI, MAGIC, ALU.mult, ALU.add, [ANG.b[0]], [T1.b[0]])
                            ts("dve", T1[:], T1[:], -MAGIC, None, ALU.add, ALU.bypass, [T1.b[0]], [T1.b[0]])
                            stt(T2[:], T1[:], -TWO_PI, ANG[:], ALU.mult, ALU.add, [T1.b[0], ANG.b[0]], [T2.b[0]])
                            ts("dve", T2[:], T2[:], -PI_S, PI_S, ALU.max, ALU.min, [T2.b[0]], [T2.b[0]])
                            act(dstt[:], T2[:], AF.Sin, [T2.b[0]], [dstt.b[0]])
                        tt("dve", Pr[:].rearrange("p a b -> p (a b)"), MG[:], CSt[:], ALU.mult, [MG.b[0], CSt.b[0]], [Pr.b[0]])
                        tt("dve", Pi[:].rearrange("p a b -> p (a b)"), MG[:], SNt[:], ALU.mult, [MG.b[0], SNt.b[0]], [Pi.b[0]])
                    ar, ai = A3[:, 0:32], A3[:, 32:64]
                    NR, DEN, t1, t2, CRE, CIM = [SM[:, 3 + k, :] for k in range(6)]
                    a3 = [A3.b[0]]
                    ts("dve", NR, PAr[:, 8, :], -1.0, None, ALU.add, ALU.bypass, [PAr.b[0]], sm)
                    NI = PAi[:, 8, :]
                    tt("dve", t1, ar, ar, ALU.mult, a3, sm)
                    tt("dve", t2, ai, ai, ALU.mult, a3, sm)
                    tt("dve", DEN, t1, t2, ALU.add, sm, sm)
                    S.op("dve", lambda e: e.reciprocal(out=DEN, in_=DEN), reads=sm, writes=sm)
                    tt("dve", t1, NR, ar, ALU.mult, sm + a3, sm)
                    tt("dve", t2, NI, ai, ALU.mult, [PAi.b[0]] + a3, sm)
                    tt("dve", t1, t1, t2, ALU.add, sm, sm)
                    tt("dve", CRE, t1, DEN, ALU.mult, sm, sm)
                    tt("dve", t1, NI, ar, ALU.mult, [PAi.b[0]] + a3, sm)
                    tt("dve", t2, NR, ai, ALU.mult, sm + a3, sm)
                    tt("dve", t1, t1, t2, ALU.subtract, sm, sm)
                    tt("dve", CIM, t1, DEN, ALU.mult, sm, sm)
                    creb = SM.v([[1, 32], [0, 16]], off=7 * 32)
                    cimb = SM.v([[1, 32], [0, 16]], off=8 * 32)
                    TA3 = T1[:].rearrange("p (a c) -> p a c", c=16)
                    tt("dve", BBR[:], creb, BRE[:], ALU.mult, sm + [BRE.b[0]], [BBR.b[0]])
                    tt("dve", TA3, cimb, BIM[:], ALU.mult, sm + [BIM.b[0]], [T1.b[0]])
                    tt("dve", BBR[:], BBR[:], TA3, ALU.subtract, [BBR.b[0], T1.b[0]], [BBR.b[0]])
                    tt("dve", BBI[:], creb, BIM[:], ALU.mult, sm + [BIM.b[0]], [BBI.b[0]])
                    tt("dve", TA3, cimb, BRE[:], ALU.mult, sm + [BRE.b[0]], [T1.b[0]])
                    tt("dve", BBI[:], BBI[:], TA3, ALU.add, [BBI.b[0], T1.b[0]], [BBI.b[0]])
                    mub = [MUr.b[0], MUi.b[0]]
                    S.op("pool", lambda e: e.tensor_copy(out=MUr[:, 0, :], in_=PAr[:, 15, :]), reads=[PAr.b[0]], writes=[MUr.b[0]])
                    S.op("pool", lambda e: e.tensor_copy(out=MUi[:, 0, :], in_=PAi[:, 15, :]), reads=[PAi.b[0]], writes=[MUi.b[0]])
                    for k in range(1, 8):
                        re_, im_ = MUr[:, k - 1, :], MUi[:, k - 1, :]
                        tt("dve", t1, re_, re_, ALU.mult, mub, sm)
                        tt("dve", t2, im_, im_, ALU.mult, mub, sm)
                        tt("dve", MUr[:, k, :], t1, t2, ALU.subtract, sm, [MUr.b[0]])
                        tt("dve", t1, re_, im_, ALU.mult, mub, sm)
                        ts("dve", MUi[:, k, :], t1, 2.0, None, ALU.mult, ALU.bypass, sm, [MUi.b[0]])
                    ts("dve", MUn[:], MUi[:], -1.0, None, ALU.mult, ALU.bypass, [MUi.b[0]], [MUn.b[0]])
                    S.barrier()
                with ExitStack() as cs:
                    TA = Tile(S, cs, "s5TA", [128, 512], F32)
                    UR, UI, VR, VIN = [Tile(S, cs, "s5" + n, [128, 4, 128], F32) for n in "UR UI VR VIN".split()]
                    TW2 = [Tile(S, cs, "s5TW", [128, 8, 128], BF16) for _ in range(2)]
                    PRm2 = [[Tile(S, cs, "s5PRm%d" % h, [128, 4, 128], BF16) for h in range(2)] for _ in range(2)]
                    PNm2 = [[Tile(S, cs, "s5PNm%d" % h, [128, 4, 128], BF16) for h in range(2)] for _ in range(2)]
                    QRT2 = [Tile(S, cs, "s5QRT", [128, 4, 128], BF16) for _ in range(2)]
                    QIT2 = [Tile(S, cs, "s5QIT", [128, 4, 128], BF16) for _ in range(2)]
                    QMR2 = [Tile(S, cs, "s5QMR", [128, 4, 128], BF16) for _ in range(2)]
                    QMI2 = [Tile(S, cs, "s5QMI", [128, 4, 128], BF16) for _ in range(2)]
                    H0c2 = [Tile(S, cs, "s5H0c", [128, 2, 64], F32) for _ in range(2)]
                    OSc2 = [Tile(S, cs, "s5OSc", [128, 2, 64], F32) for _ in range(2)]
                    UB3 = [[Tile(S, cs, "s5UB%d" % h, [128, NB], BF16) for h in range(2)] for _ in range(3)]
                    HPr2 = [Tile(S, cs, "s5HPr", [128, NB], BF16) for _ in range(2)]
                    HPi2 = [Tile(S, cs, "s5HPi", [128, NB], BF16) for _ in range(2)]
                    Ar, Ai = [Tile(S, cs, "s5sc" + n, [128, NB], F32) for n in "Ar Ai".split()]
                    Br, Bi = [Tile(S, cs, "s5sc" + n, [128, 128], F32) for n in "Br Bi".split()]
                    Yb2 = [Tile(S, cs, "s5Yb", [128, 8, NB], BF16, nb=8) for _ in range(2)]
                    TG1 = Tile(S, cs, "s5TG", [128, NB], F32)
                    TG2 = [TG1, TG1]
                    for t_ in HPr2 + HPi2:
                        S.op("pool", lambda e, t_=t_: e.memset(t_[:], 0.0), writes=[t_.b[0]])
                    allxb = lambda kc: [XB.b[kc * NST + st] for st in range(NST)]
                    rrb = {"i": 0}

                    def bankT():
                        return S.banks[6]

                    def ctable(Pr, Pi, s_, Are, Aim, OUTr, OUTi, kc, neg_im):
                        pr = Pr.v([[32, 8], [1, 4], [0, 16]], off=s_ * 32 + 4 * kc)
                        pi = Pi.v([[32, 8], [1, 4], [0, 16]], off=s_ * 32 + 4 * kc)
                        are = Are.v([[0, 8], [16, 4], [1, 16]], off=4 * kc * 16)
                        aim = Aim.v([[0, 8], [16, 4], [1, 16]], off=4 * kc * 16)
                        o_r = OUTr.v([[16, 8], [128, 4], [1, 16]])
                        o_i = OUTi.v([[16, 8], [128, 4], [1, 16]])
                        ta = TA.v([[64, 8], [16, 4], [1, 16]])
                        rd = [Pr.b[0], Pi.b[0], Are.b[0], Aim.b[0]]
                        tt("pool", o_r, pr, are, ALU.mult, rd, [OUTr.b[0]])
                        tt("pool", ta, pi, aim, ALU.mult, rd, [TA.b[0]])
                        tt("pool", o_r, o_r, ta, ALU.subtract, [OUTr.b[0], TA.b[0]], [OUTr.b[0]])
                        tt("pool", o_i, pr, aim, ALU.mult, rd, [OUTi.b[0]])
                        tt("pool", ta, pi, are, ALU.mult, rd, [TA.b[0]])
                        tt("pool", o_i, o_i, ta, ALU.add, [OUTi.b[0], TA.b[0]], [OUTi.b[0]])
                        if neg_im:
                            fl = OUTi[:].rearrange("p a b -> p (a b)")
                            act(fl, fl, AF.Identity, [OUTi.b[0]], [OUTi.b[0]], scale=-1.0)

                    def tables(kc):
                        cp = kc % 2
                        TW, PRm, PNm, QRT, QIT, H0c = TW2[cp], PRm2[cp], PNm2[cp], QRT2[cp], QIT2[cp], H0c2[cp]
                        ctable(PBr, PBi, 7, BBR, BBI, UR, UI, kc, False)
                        ctable(PAr, PAi, 7, CR, CI, VR, VIN, kc, True)
                        for h in range(2):
                            pt = bankT()
                            hs = slice(64 * h, 64 * h + 64)
                            for pl in range(4):
                                S.op("pe", lambda e, pl=pl, pt=pt, hs=hs: e.matmul(pt[:, pl * 128:(pl + 1) * 128], lhsT=UR[hs, pl, :], rhs=VR[hs, pl, :],
                                                                                 start=True, stop=False),
                                     reads=[UR.b[0], VR.b[0]], writes=[pt.b[0]])
                                S.op("pe", lambda e, pl=pl, pt=pt, hs=hs: e.matmul(pt[:, pl * 128:(pl + 1) * 128], lhsT=UI[hs, pl, :], rhs=VIN[hs, pl, :],
                                                                                 start=False, stop=True),
                                     reads=[UI.b[0], VIN.b[0]], writes=[pt.b[0]])
                            tt("dve", TW.v([[256, 4], [1, 128]], off=h * 128), pt.v([[128, 4], [1, 128]]),
                               CN.v([[0, 4], [1, 128]], off=C_TM), ALU.mult, [pt.b[0]] + cb, [TW.b[0]])
                        ctable(PAr, PAi, 8, CR, CI, UR, UI, kc, True)
                        for h in range(2):
                            rm = CN[:, C_RM + h:C_RM + h + 1]
                            act(PRm[h][:], UR[:], AF.Identity, [UR.b[0]] + cb, [PRm[h].b[0]], scale=rm)
                            act(PNm[h][:], UI[:], AF.Identity, [UI.b[0]] + cb, [PNm[h].b[0]], scale=rm)
                        ctable(PBr, PBi, 0, BBR, BBI, VR, VIN, kc, False)
                        QMR, QMI = QMR2[cp], QMI2[cp]
                        mur_b = MUr.v([[1, 4], [0, 128]], off=4 * kc)
                        mui_b = MUi.v([[1, 4], [0, 128]], off=4 * kc)
                        ta3 = TA.v([[128, 4], [1, 128]])
                        mbb = [MUr.b[0], MUi.b[0]]
                        tt("pool", UR[:], VR[:], mur_b, ALU.mult, [VR.b[0]] + mbb, [UR.b[0]])
                        tt("pool", ta3, VIN[:], mui_b, ALU.mult, [VIN.b[0]] + mbb, [TA.b[0]])
                        tt("pool", UR[:], UR[:], ta3, ALU.subtract, [UR.b[0], TA.b[0]], [UR.b[0]])
                        tt("pool", UI[:], VR[:], mui_b, ALU.mult, [VR.b[0]] + mbb, [UI.b[0]])
                        tt("pool", ta3, VIN[:], mur_b, ALU.mult, [VIN.b[0]] + mbb, [TA.b[0]])
                        tt("pool", UI[:], UI[:], ta3, ALU.add, [UI.b[0], TA.b[0]], [UI.b[0]])
                        for src, dst in ((VR, QRT), (VIN, QIT), (UR, QMR), (UI, QMI)):
                            pq = bankT()
                            for pl in range(4):
                                S.op("pe", lambda e, pl=pl, src=src, pq=pq: e.transpose(out=pq[:, pl * 128:(pl + 1) * 128], in_=src[:, pl, :],
                                                                                      identity=CN[:, C_ID:C_ID + 128]),
                                     reads=[src.b[0]] + cb, writes=[pq.b[0]])
                            act(dst[:].rearrange("p a b -> p (a b)"), pq[:, 0:512], AF.Identity, [pq.b[0]], [dst.b[0]])
                        for ri in range(2):
                            S.dma("sp", H0c[:, ri, :], s5h0[j, ri, :, kc * 64:(kc + 1) * 64], ldchan(), writes=[H0c.b[0]])

                    def pair_a1(kc, pl):
                        cp = kc % 2
                        p = 4 * kc + pl
                        pp = p % 2
                        QRT, QIT = QRT2[cp], QIT2[cp]
                        UB = UB3[p % 3]
                        for h in range(2):
                            pu = S.banks[0]
                            for i in range(8):
                                eqt, pb0, npp = (EQ, 32 * pl, 32) if pl < 3 else (EQz, 64, 64)
                                S.op("pe", lambda e, i=i, h=h, pu=pu, eqt=eqt, pb0=pb0, npp=npp: e.matmul(
                                    pu[:, 0:NB], lhsT=eqt.v([[1, 128]], off=(h * 8 + i) * 128, p0=pb0, np_=npp),
                                    rhs=XB.v([[8, NB]], off=kc * NT + i, p0=pb0, np_=npp), start=(i == 0), stop=(i == 7)),
                                    reads=[eqt.b[0]] + allxb(kc), writes=[pu.b[0]])
                            act(UB[h][:], pu[:, 0:NB], AF.Identity, [pu.b[0]], [UB[h].b[0]])
                        pdr, pdi = S.banks[1 + 2 * pp], S.banks[2 + 2 * pp]
                        for qt, qm, pd in ((QRT, QMR2[cp], pdr), (QIT, QMI2[cp], pdi)):
                            for h in range(2):
                                hs = slice(64 * h, 64 * h + 64)
                                ev_ = UB[h].v([[2, 128]], off=0)
                                od_ = UB[h].v([[2, 128]], off=1)
                                rd = [qt.b[0], qm.b[0], UB[h].b[0]]
                                S.op("pe", lambda e, qm=qm, pd=pd, hs=hs, ev_=ev_: e.matmul(pd[hs, 0:128], lhsT=qm[:, pl, hs], rhs=ev_,
                                                                                       start=True, stop=False), reads=rd, writes=[pd.b[0]])
                                S.op("pe", lambda e, qt=qt, pd=pd, hs=hs, od_=od_: e.matmul(pd[hs, 0:128], lhsT=qt[:, pl, hs], rhs=od_,
                                                                                       start=False, stop=True), reads=rd, writes=[pd.b[0]])
                                S.op("pe", lambda e, qt=qt, pd=pd, hs=hs, ev_=ev_: e.matmul(pd[hs, 128:256], lhsT=qt[:, pl, hs], rhs=ev_,
                                                                                       start=True, stop=True), reads=rd, writes=[pd.b[0]])
                                S.op("pe", lambda e, qt=qt, pd=pd, hs=hs, h=h: e.matmul(pd[hs, 256:NB], lhsT=qt[:, pl, hs], rhs=UB[h][:, 256:NB],
                                                                                   start=True, stop=True), reads=rd, writes=[pd.b[0]])

                    def pair_a2(kc, pl):
                        cp = kc % 2
                        p = 4 * kc + pl
                        pp = p % 2
                        H0c, OSc = H0c2[cp], OSc2[cp]
                        HPr, HPi = HPr2[pp], HPi2[pp]
                        pdr, pdi = S.banks[1 + 2 * pp], S.banks[2 + 2 * pp]
                        act(Ar[:], pdr[:, 0:NB], AF.Identity, [pdr.b[0]], [Ar.b[0]])
                        act(Ai[:], pdi[:, 0:NB], AF.Identity, [pdi.b[0]], [Ai.b[0]])
                        h0r, h0i = H0c[:, 0, pl * 16:(pl + 1) * 16], H0c[:, 1, pl * 16:(pl + 1) * 16]
                        mur = lambda k: MUr[:, k, p:p + 1]
                        mui = lambda k: MUi[:, k, p:p + 1]
                        mun = lambda k: MUn[:, k, p:p + 1]
                        mb = [MUr.b[0], MUi.b[0], MUn.b[0]]
                        sr, si = Ar[:, 256:NB], Ai[:, 256:NB]
                        stt(sr, h0r, mur(0), sr, ALU.mult, ALU.add, [H0c.b[0], Ar.b[0]] + mb, [Ar.b[0]])
                        stt(sr, h0i, mun(0), sr, ALU.mult, ALU.add, [H0c.b[0], Ar.b[0]] + mb, [Ar.b[0]])
                        stt(si, h0i, mur(0), si, ALU.mult, ALU.add, [H0c.b[0], Ai.b[0]] + mb, [Ai.b[0]])
                        stt(si, h0r, mui(0), si, ALU.mult, ALU.add, [H0c.b[0], Ai.b[0]] + mb, [Ai.b[0]])
                        src_r, src_i, dst_r, dst_i = Ar, Ai, Br, Bi
                        for k in range(7):
                            s_ = 1 << k
                            lo = 0 if k == 0 else s_ // 2
                            rs = [src_r.b[0], src_i.b[0]] + mb
                            stt(dst_r[:, s_:128], src_r[:, 0:128 - s_], mur(k + 1), src_r[:, s_:128], ALU.mult, ALU.add, rs, [dst_r.b[0]])
                            stt(dst_r[:, s_:128], src_i[:, 0:128 - s_], mun(k + 1), dst_r[:, s_:128], ALU.mult, ALU.add, rs + [dst_r.b[0]], [dst_r.b[0]])
                            stt(dst_i[:, s_:128], src_i[:, 0:128 - s_], mur(k + 1), src_i[:, s_:128], ALU.mult, ALU.add, rs, [dst_i.b[0]])
                            stt(dst_i[:, s_:128], src_r[:, 0:128 - s_], mui(k + 1), dst_i[:, s_:128], ALU.mult, ALU.add, rs + [dst_i.b[0]], [dst_i.b[0]])
                            S.op("dve", lambda e, lo=lo, s_=s_, a=src_r, b=dst_r: e.tensor_copy(out=b[:, lo:s_], in_=a[:, lo:s_]),
                                 reads=[src_r.b[0]], writes=[dst_r.b[0]])
                            S.op("dve", lambda e, lo=lo, s_=s_, a=src_i, b=dst_i: e.tensor_copy(out=b[:, lo:s_], in_=a[:, lo:s_]),
                                 reads=[src_i.b[0]], writes=[dst_i.b[0]])
                            src_r, src_i, dst_r, dst_i = dst_r, dst_i, src_r, src_i
                        er, ei = Ar[:, 129:256], Ai[:, 129:256]
                        bb_ = [Br.b[0], Bi.b[0]] + mb
                        stt(er, Br[:, 0:127], mur(0), er, ALU.mult, ALU.add, bb_ + [Ar.b[0]], [Ar.b[0]])
                        stt(er, Bi[:, 0:127], mun(0), er, ALU.mult, ALU.add, bb_ + [Ar.b[0]], [Ar.b[0]])
                        stt(ei, Bi[:, 0:127], mur(0), ei, ALU.mult, ALU.add, bb_ + [Ai.b[0]], [Ai.b[0]])
                        stt(ei, Br[:, 0:127], mui(0), ei, ALU.mult, ALU.add, bb_ + [Ai.b[0]], [Ai.b[0]])
                        act(HPr.v([[2, 128]], off=1), Ar[:, 128:256], AF.Identity, [Ar.b[0]], [HPr.b[0]])
                        act(HPi.v([[2, 128]], off=1), Ai[:, 128:256], AF.Identity, [Ai.b[0]], [HPi.b[0]])
                        act(HPr.v([[2, 127]], off=2), Br[:, 0:127], AF.Identity, [Br.b[0]], [HPr.b[0]])
                        act(HPi.v([[2, 127]], off=2), Bi[:, 0:127], AF.Identity, [Bi.b[0]], [HPi.b[0]])
                        act(HPr[:, 256:NB], h0r, AF.Identity, [H0c.b[0]], [HPr.b[0]])
                        act(HPi[:, 256:NB], h0i, AF.Identity, [H0c.b[0]], [HPi.b[0]])
                        S.op("pool", lambda e: e.tensor_copy(out=OP[:, 0, p:p + 1], in_=Br[:, 127:128]), reads=[Br.b[0]], writes=[OP.b[0]])
                        S.op("pool", lambda e: e.tensor_copy(out=OP[:, 1, p:p + 1], in_=Bi[:, 127:128]), reads=[Bi.b[0]], writes=[OP.b[0]])
                        S.op("pool", lambda e: e.tensor_copy(out=OSc[:, 0, pl * 16:(pl + 1) * 16], in_=Ar[:, 256:NB]), reads=[Ar.b[0]], writes=[OSc.b[0]])
                        S.op("pool", lambda e: e.tensor_copy(out=OSc[:, 1, pl * 16:(pl + 1) * 16], in_=Ai[:, 256:NB]), reads=[Ai.b[0]], writes=[OSc.b[0]])

                    def pair_b(kc, pl):
                        cp = kc % 2
                        p = 4 * kc + pl
                        pp = p % 2
                        TW, PRm, PNm = TW2[cp], PRm2[cp], PNm2[cp]
                        UB, HPr, HPi = UB3[p % 3], HPr2[pp], HPi2[pp]
                        Yb = Yb2[cp]
                        for h in range(2):
                            gm = 2 * pl + h
                            g = 8 * kc + gm
                            py = S.banks[5]
                            TG = TG2[h]
                            S.op("pe", lambda e, gm=gm, h=h, py=py: e.matmul(py[:, 0:NB], lhsT=TW[:, gm, :], rhs=UB[h][:], start=True, stop=False),
                                 reads=[TW.b[0], UB[h].b[0]], writes=[py.b[0]])
                            S.op("pe", lambda e, h=h, py=py: e.matmul(py[:, 0:NB], lhsT=PRm[h][:, pl, :], rhs=HPr[:], start=False, stop=False),
                                 reads=[PRm[h].b[0], HPr.b[0]], writes=[py.b[0]])
                            S.op("pe", lambda e, h=h, py=py: e.matmul(py[:, 0:NB], lhsT=PNm[h][:, pl, :], rhs=HPi[:], start=False, stop=True),
                                 reads=[PNm[h].b[0], HPi.b[0]], writes=[py.b[0]])
                            stt(TG[:], UB[h][:], DB[:, g:g + 1], py[:, 0:NB], ALU.mult, ALU.add, [UB[h].b[0], DB.b[0], py.b[0]], [TG.b[0]])
                            act(Yb[:, gm, :], TG[:], AF.Gelu_apprx_tanh, [TG.b[0]], [Yb.b[gm]])

                    def unblock(kc):
                        OSc = OSc2[kc % 2]
                        Yb = Yb2[kc % 2]
                        for ri in range(2):
                            S.dma("sp", s5s[j, ri, :, kc * 64:(kc + 1) * 64], OSc[:, ri, :], stchan(), reads=[OSc.b[0]])
                        for i in range(8):
                            pb = S.banks[7]
                            q2, ipar = i // 2, i % 2
                            for gm in range(8):
                                ept, pb0, npp = (EPQ, 32 * q2, 32) if q2 < 3 else (EPQz, 64, 64)
                                S.op("pe", lambda e, gm=gm, pb=pb, ept=ept, pb0=pb0, npp=npp, ipar=ipar, Yb=Yb: e.matmul(
                                    pb[:, 0:NB], lhsT=ept.v([[1, 128]], off=(ipar * 8 + gm) * 128, p0=pb0, np_=npp),
                                    rhs=Yb.v([[1, NB]], off=gm * NB, p0=pb0, np_=npp), start=(gm == 0), stop=(gm == 7)),
                                    reads=[ept.b[0], Yb.b[gm]], writes=[pb.b[0]])
                            act(XB.v([[8, NB]], off=kc * NT + i), pb[:, 0:NB], AF.Identity, [pb.b[0]], allxb(kc))

                    def split(lst, k):
                        n_ = len(lst)
                        return [lst[(i * n_) // k:((i + 1) * n_) // k] for i in range(k)]

                    tables(0)
                    pa1 = lambda p_: S.record(lambda: pair_a1(p_ // 4, p_ % 4)) if p_ < 32 else []
                    pa2 = lambda p_: S.record(lambda: pair_a2(p_ // 4, p_ % 4)) if p_ < 32 else []
                    S.replay(pa1(0))
                    tl0 = split(S.record(lambda: tables(1)), 2)
                    S.replay(pa1(1), pa2(0), tl0[0])
                    extra = {0: [tl0[1]]}
                    for r in range(32):
                        kc, pl = r // 4, r % 4
                        if pl == 0 and 1 <= kc and kc + 1 < 8:
                            tl = split(S.record(lambda: tables(kc + 1)), 2)
                            extra.setdefault(r, []).append(tl[0])
                            extra.setdefault(r + 1, []).append(tl[1])
                        if pl == 0 and kc >= 1:
                            ul = split(S.record(lambda: unblock(kc - 1)), 3)
                            for q_ in range(3):
                                extra.setdefault(r + q_, []).append(ul[q_])
                        lb_ = S.record(lambda: pair_b(kc, pl))
                        ex_ = extra.get(r, [])
                        S.replay(pa1(r + 2), pa2(r + 1), lb_, *ex_, weights=[0.6, 1.0, 1.0] + [1.0] * len(ex_))
                    unblock(7)
                    for ri in range(2):
                        S.dma("sp", s5p[j, ri], OP[:, ri, :], stchan(), reads=[OP.b[0]])
                    S.barrier()
                MX = [Tile(S, ph, "s5MX%d" % i, [128, 512], F32) for i in range(2)]
                SGt = [Tile(S, ph, "s5SG%d" % i, [128, 512], F32) for i in range(2)]
                LSQ = Tile(S, ph, "s5LSQ", [128, KC, 512], BF16)
                LM2 = Tile(S, ph, "s5LM2", [128, 512], F32)
                LRS = Tile(S, ph, "s5LRS", [128, 512], F32)
                kk_ = {"i": 0}

                def glu_st(st):
                    t0, n = STS[st]
                    for nn in range(8):
                        w = W.get(("wglu", l, st, nn))
                        pv, pg = S.banks[2 * (nn % 2)], S.banks[2 * (nn % 2) + 1]
                        for vg, bk in ((0, pv), (1, pg)):
                            for kc in range(KC):
                                S.op("pe", lambda e, vg=vg, kc=kc, bk=bk, w=w: e.matmul(
                                    bk[:, 0:n], lhsT=w[:, (vg * 8 + kc) * 128:(vg * 8 + kc + 1) * 128], rhs=XB[:, kc, t0:t0 + n],
                                    start=(kc == 0), stop=(kc == KC - 1)),
                                    reads=[w.b[0]] + xbb(st, [kc]), writes=[bk.b[0]])
                        sg, mx = SGt[kk_["i"] % 2], MX[kk_["i"] % 2]
                        kk_["i"] += 1
                        act(sg[:, 0:n], pg[:, 0:n], AF.Sigmoid, [pg.b[0]], [sg.b[0]])
                        tt("dve", mx[:, 0:n], pv[:, 0:n], sg[:, 0:n], ALU.mult, [pv.b[0], sg.b[0]], [mx.b[0]])
                        xs = XF[:, nn, t0:t0 + n]
                        stt(xs, xs, ALPHA, mx[:, 0:n], ALU.mult, ALU.add, xfb(st, [nn]) + [mx.b[0]], xfb(st, [nn]))

                for r in range(NST + 1):
                    l1 = S.record(lambda: glu_st(r)) if r < NST else []
                    l2 = S.record(lambda: layer_norm((LSQ, None, LM2, LRS), r - 1, l * V_LAYER + 0, l * V_LAYER + 8,
                                                     banks=(S.banks[4], S.banks[5]))) if r >= 1 else []
                    S.replay(l1, l2, weights=[0.6, 1.0])
                S.barrier()

        for l in range(n_layers):
            vb = l * V_LAYER
            if l % 2 == 0:
                hgrn_phase(l)
            else:
                s5_phase(l)
            ffn_phase(l, last=(l == n_layers - 1))
        for ev in S.pending.values():
            S._wait("sp", ev)
        S.barrier()
        build_program.stats = (S.nop, dict(S.cnt))
    return nc


def _chunkT(W, kcs=8):
    K, N = W.shape
    return W.reshape(K // 128, 128, N).transpose(1, 0, 2)


def _make_consts():
    c = np.zeros((128, NCONST), np.float32)
    p = np.arange(128)
    c[:, C_ID:C_ID + 128] = np.eye(128, dtype=np.float32)
    c[:, C_CAUS:C_CAUS + 128] = (p[:, None] <= p[None, :])
    c[:, C_BD:C_BD + 128] = (p[:, None] <= p[None, :]) & (p[:, None] // 8 == p[None, :] // 8)
    c[:, C_SEG:C_SEG + 16] = (p[:, None] // 8 == np.arange(16)[None, :])
    c[:, C_R512:C_R512 + 512] = (np.arange(512) % 128 != 0)[None, :]
    c[:, C_R8:C_R8 + 128] = (np.arange(128) % 8 != 0)[None, :]
    c[:, C_TM:C_TM + 128] = (p[None, :] // 16 >= p[:, None] // 16)
    c[:, C_RM] = p < 64
    c[:, C_RM + 1] = p >= 64
    tau = np.arange(16, dtype=np.float32) - 7.0
    c[:, C_TAU:C_TAU + 512] = np.repeat(tau, 32)[None, :]
    return c


def _make_emat():
    EQ = np.zeros((128, 2, 8, 8, 16), np.float32)
    EP = np.zeros((128, 2, 8, 128), np.float32)
    for p in range(128):
        r = p % 32
        par, c = r // 16, r % 16
        for i in range(8):
            EQ[p, par, i, i, c] = 1.0
        for gm in range(8):
            EP[p, par, gm, 16 * gm + c] = 1.0
    EQ = EQ.reshape(128, 2048)
    EP = EP.reshape(128, 2048)
    EQz, EPz = EQ.copy(), EP.copy()
    EQz[64:96] = 0
    EPz[64:96] = 0
    return np.stack([EQ, EP, EQz, EPz])


def _vec(v):
    return np.ascontiguousarray(v.reshape(8, 128).T)


def prep_shared(inp):
    f = np.float32
    sh = {}
    sh["consts"] = _make_consts()
    sh["emat"] = _make_emat()
    vecs = np.zeros((128, NVEC), f)
    for l in range(4):
        for k, name in enumerate(["ln_mix_w", "ln_mix_b", "ln_ffn_w", "ln_ffn_b", "ple_norm_w"]):
            vecs[:, l * V_LAYER + 8 * k: l * V_LAYER + 8 * k + 8] = _vec(inp[name][l])
    for j in range(2):
        vecs[:, V_LB + 8 * j: V_LB + 8 * j + 8] = _vec(inp["hg_lower_bounds"][j])
        vecs[:, V_GN + j] = inp["hg_gnorm_w"][j]
    sh["vecs"] = vecs
    whg = np.zeros((2, 8, 2, 128, 2048), f)
    who = np.zeros((2, 8, 128, 1024), f)
    for j in range(2):
        Wt = _chunkT(inp["hg_w_in"][j])
        for h in range(8):
            for ab in range(2):
                for sub in range(2):
                    part = ab * 2 + sub
                    blk = Wt[:, :, part * 1024 + h * 128: part * 1024 + (h + 1) * 128]
                    whg[j, h, ab, :, sub * 1024:(sub + 1) * 1024] = blk.reshape(128, 1024)
        Wo = _chunkT(inp["hg_w_out"][j])
        for n in range(8):
            who[j, n] = Wo[:, :, n * 128:(n + 1) * 128].reshape(128, 1024)
    sh["whg"], sh["who"] = whg, who
    wglu = np.zeros((2, 8, 128, 2048), f)
    for j in range(2):
        Wt = _chunkT(inp["s5_w_glu"][j])
        for n in range(8):
            for vg in range(2):
                wglu[j, n, :, vg * 1024:(vg + 1) * 1024] = Wt[:, :, vg * 1024 + n * 128: vg * 1024 + (n + 1) * 128].reshape(128, 1024)
    sh["wglu"] = wglu
    wgu = np.zeros((4, 22, 128, 2048), f)
    wdn = np.zeros((4, 3, 8, 128, 1024), f)
    wple = np.zeros((4, 8, 128, 1280), f)
    for l in range(4):
        Wt = _chunkT(inp["ffn_w_gate_up"][l])
        for c in range(22):
            for gu in range(2):
                wgu[l, c, :, gu * 1024:(gu + 1) * 1024] = Wt[:, :, gu * 2816 + c * 128: gu * 2816 + (c + 1) * 128].reshape(128, 1024)
        Wd = _chunkT(inp["ffn_w_down"][l])
        for ps, (c0, c1) in enumerate(FPASS):
            for n in range(8):
                wdn[l, ps, n, :, 0:(c1 - c0) * 128] = Wd[:, c0:c1, n * 128:(n + 1) * 128].reshape(128, -1)
        Wg = _chunkT(inp["ple_w_gate"][l])
        Wp = _chunkT(inp["ple_w_proj"][l])
        for n in range(8):
            wple[l, n, :, 0:1024] = Wg[:, :, n * 128:(n + 1) * 128].reshape(128, 1024)
            wple[l, n, :, 1024:1280] = Wp[:, :, n * 128:(n + 1) * 128].reshape(128, 256)
    sh["wgu"], sh["wdn"], sh["wple"] = wgu, wdn, wple
    s5a = np.zeros((2, 128, 96), f)
    s5b = np.zeros((2, 2, 128, 512), f)
    s5c = np.zeros((2, 2, 128, 512), f)
    s5d = np.zeros((2, 128, 64), f)

    def gp(a):
        sh_ = a.shape
        a = a.reshape((32, 2) + sh_[1:])
        a = np.moveaxis(a, 0, 2)
        return a.reshape((128, 32) + sh_[2:])
    for j in range(2):
        s5a[j, :, 0:32] = gp(inp["s5_a_re"][j])
        s5a[j, :, 32:64] = gp(inp["s5_a_im"][j])
        s5a[j, :, 64:96] = gp(np.repeat(inp["s5_log_step"][j][:, None], 64, axis=1))
        s5b[j, 0] = gp(inp["s5_b_re"][j]).reshape(128, 512)
        s5b[j, 1] = gp(inp["s5_b_im"][j]).reshape(128, 512)
        s5c[j, 0] = gp(inp["s5_c_re"][j].transpose(0, 2, 1)).reshape(128, 512)
        s5c[j, 1] = gp(inp["s5_c_im"][j].transpose(0, 2, 1)).reshape(128, 512)
        dd = inp["s5_d"][j].reshape(64, 16)
        s5d[j] = np.tile(dd.T, (8, 1))
    sh["s5a"], sh["s5b"], sh["s5c"], sh["s5d"] = s5a, s5b, s5c, s5d
    return sh


def prep_core(inp, c):
    m = {}
    xs = inp["x_sample"][16 * c:16 * c + 16].reshape(128, D)
    m["xT"] = np.ascontiguousarray(np.concatenate([inp["x_prompt"][c].T, xs.T], axis=1))
    pp = inp["p_prompt"][:, c].transpose(0, 2, 1)
    ps = inp["p_sample"][:, 16 * c:16 * c + 16].reshape(4, 128, 256).transpose(0, 2, 1)
    m["pT"] = np.ascontiguousarray(np.concatenate([pp, ps], axis=2))
    m["hg0"] = np.ascontiguousarray(inp["state_hgrn"][:, 16 * c:16 * c + 16])
    h0 = np.stack([inp["state_s5_re"][:, 16 * c:16 * c + 16], inp["state_s5_im"][:, 16 * c:16 * c + 16]], axis=1)
    h0 = h0.reshape(2, 2, 16, 32, 2, 64).transpose(0, 1, 4, 5, 3, 2)
    m["s5h0"] = np.ascontiguousarray(h0.reshape(2, 2, 128, 512))
    return m


def assemble(results):
    n = len(results)
    f = np.float32
    y_p = np.zeros((n, NPR, D), f)
    y_s = np.zeros((n * 16, 8, D), f)
    hg_p = np.zeros((2, n, 8, 128, 128), f)
    re_p = np.zeros((2, n, 64, 64), f)
    im_p = np.zeros((2, n, 64, 64), f)
    hg_s = np.zeros((2, n * 16, 8, 128, 128), f)
    re_s = np.zeros((2, n * 16, 64, 64), f)
    im_s = np.zeros((2, n * 16, 64, 64), f)
    for c, r in enumerate(results):
        yT = r["yT"]
        y_p[c] = yT[:, :NPR].T
        y_s[16 * c:16 * c + 16] = yT[:, NPR:].T.reshape(16, 8, D)
        hg_p[:, c] = r["hgp"].transpose(0, 2, 1, 3)
        hg_s[:, 16 * c:16 * c + 16] = r["hgs"]
        sp = r["s5p"].reshape(2, 2, 2, 64, 32)
        sp = sp.transpose(0, 1, 4, 2, 3).reshape(2, 2, 64, 64)
        re_p[:, c], im_p[:, c] = sp[:, 0], sp[:, 1]
        ss = r["s5s"].reshape(2, 2, 2, 64, 32, 16)
        ss = ss.transpose(0, 1, 5, 4, 2, 3).reshape(2, 2, 16, 64, 64)
        re_s[:, 16 * c:16 * c + 16], im_s[:, 16 * c:16 * c + 16] = ss[:, 0], ss[:, 1]
    return (y_p, y_s, hg_p, re_p, im_p, hg_s, re_s, im_s)


_NC_CACHE = {}


def kernel(**inputs):
    inp = {k: np.asarray(v) for k, v in inputs.items()}
    sh = prep_shared(inp)
    in_maps = []
    for c in range(8):
        m = dict(sh)
        m.update(prep_core(inp, c))
        in_maps.append(m)
    if "nc" not in _NC_CACHE:
        _NC_CACHE["nc"] = build_program(4)
    res = run_bass_kernel_spmd(_NC_CACHE["nc"], in_maps, core_ids=list(range(8)))
    return assemble(res.results)
```
